# Optimizing a Trainium2 kernel written in Bass

```python
import math
import jax, jax.numpy as jnp
from jax import lax
import numpy as np

D_MODEL = 2048
BATCH = 4
SEQ = 4096
DEPTH = 2

CHUNK = 64
N_LEFT_CHUNKS = 8
BAND = (N_LEFT_CHUNKS + 1) * CHUNK

A_HEADS = 16
A_HEAD_DIM = 64
A_WIDTH = A_HEADS * A_HEAD_DIM
MAX_REL = 128
N_REL = 2 * MAX_REL + 1

S5_WIDTH = D_MODEL // 2
S5_GROUP = 16
S5_GROUPS = S5_WIDTH // S5_GROUP
S5_STATE = 64
DT_MIN = 0.001
DT_MAX = 0.1

C_HEADS = 16
C_HEAD_DIM = 128
C_WIDTH = C_HEADS * C_HEAD_DIM
Q_BLOCK = 128

EVEN_MIX = A_WIDTH + S5_WIDTH
W_IN_EVEN = 3 * A_WIDTH + S5_WIDTH + A_WIDTH + S5_WIDTH
W_IN_ODD = 3 * C_WIDTH + C_WIDTH + C_HEADS
N_EVEN = (DEPTH + 1) // 2
N_ODD = DEPTH // 2
EPS = 1e-6

kernel_name = "hybrid_chunked_attn_s5_fox_encoder"


def rmsnorm(x, g):
    xf = x.astype(jnp.float32)
    y = xf * lax.rsqrt(jnp.mean(xf * xf, axis=-1, keepdims=True) + EPS)
    return (y * g.astype(jnp.float32)).astype(x.dtype)


def chunked_relpos_attention(q, k, v, rel_bias):
    b, l, h, dh = q.shape
    nc = l // CHUNK
    qc = q.reshape(b, nc, CHUNK, h, dh)
    pad = ((0, 0), (N_LEFT_CHUNKS, 0), (0, 0), (0, 0), (0, 0))
    kc = jnp.pad(k.reshape(b, nc, CHUNK, h, dh), pad)
    vc = jnp.pad(v.reshape(b, nc, CHUNK, h, dh), pad)
    band_idx = jnp.arange(nc)[:, None] + jnp.arange(N_LEFT_CHUNKS + 1)[None, :]
    kb = jnp.take(kc, band_idx, axis=1).reshape(b, nc, BAND, h, dh)
    vb = jnp.take(vc, band_idx, axis=1).reshape(b, nc, BAND, h, dh)
    scale = 1.0 / math.sqrt(dh)
    s = jnp.einsum('bcqhd,bckhd->bhcqk', qc, kb).astype(jnp.float32) * scale
    rel = jnp.arange(CHUNK)[:, None] + N_LEFT_CHUNKS * CHUNK - jnp.arange(BAND)[None, :]
    rel_idx = jnp.clip(rel, -MAX_REL, MAX_REL) + MAX_REL
    bias = rel_bias.astype(jnp.float32)[:, rel_idx]
    key_chunk = jnp.arange(nc)[:, None] - N_LEFT_CHUNKS + jnp.arange(BAND)[None, :] // CHUNK
    valid = key_chunk >= 0
    s = s + bias[None, :, None]
    s = jnp.where(valid[None, None, :, None, :], s, -jnp.inf)
    p = jax.nn.softmax(s, axis=-1).astype(v.dtype)
    o = jnp.einsum('bhcqk,bckhd->bcqhd', p, vb)
    return o.reshape(b, l, h * dh)


def s5_ssm(u, lam_re, lam_im, log_dt, b_re, b_im, c_re, c_im, d_skip):
    bsz, l, _ = u.shape
    uf = u.astype(jnp.float32).reshape(bsz, l, S5_GROUPS, S5_GROUP)
    lr = lam_re.astype(jnp.float32)
    li = lam_im.astype(jnp.float32)
    dt = jnp.exp(log_dt.astype(jnp.float32))[:, None]
    mag = jnp.exp(lr * dt)
    ang = li * dt
    a_re = mag * jnp.cos(ang)
    a_im = mag * jnp.sin(ang)
    den = lr * lr + li * li
    nr = a_re - 1.0
    ni = a_im
    coef_re = (nr * lr + ni * li) / den
    coef_im = (ni * lr - nr * li) / den
    br = b_re.astype(jnp.float32)
    bi = b_im.astype(jnp.float32)
    bb_re = coef_re[..., None] * br - coef_im[..., None] * bi
    bb_im = coef_re[..., None] * bi + coef_im[..., None] * br
    bu_re = jnp.einsum('blgc,gpc->blgp', uf, bb_re)
    bu_im = jnp.einsum('blgc,gpc->blgp', uf, bb_im)
    a_re_t = jnp.broadcast_to(a_re, (1, l, S5_GROUPS, S5_STATE))
    a_im_t = jnp.broadcast_to(a_im, (1, l, S5_GROUPS, S5_STATE))

    def combine(e1, e2):
        ar1, ai1, xr1, xi1 = e1
        ar2, ai2, xr2, xi2 = e2
        return (ar2 * ar1 - ai2 * ai1,
                ar2 * ai1 + ai2 * ar1,
                ar2 * xr1 - ai2 * xi1 + xr2,
                ar2 * xi1 + ai2 * xr1 + xi2)

    _, _, xr, xi = lax.associative_scan(combine, (a_re_t, a_im_t, bu_re, bu_im), axis=1)
    y = (jnp.einsum('blgp,gcp->blgc', xr, c_re.astype(jnp.float32))
         - jnp.einsum('blgp,gcp->blgc', xi, c_im.astype(jnp.float32))
         + d_skip.astype(jnp.float32) * uf)
    return y.reshape(bsz, l, S5_WIDTH)


def forgetting_attention(q, k, v, log_f):
    b, l, h, dh = q.shape
    nqb = l // Q_BLOCK
    cum = jnp.cumsum(log_f, axis=1).transpose(0, 2, 1)
    qb = q.reshape(b, nqb, Q_BLOCK, h, dh).transpose(1, 0, 3, 2, 4)
    cq = cum.reshape(b, h, nqb, Q_BLOCK).transpose(2, 0, 1, 3)
    kpos = jnp.arange(l)
    scale = 1.0 / math.sqrt(dh)

    def block(args):
        qi, cqi, bi = args
        s = jnp.einsum('bhqd,bkhd->bhqk', qi, k).astype(jnp.float32) * scale
        s = s + cqi[..., None] - cum[:, :, None, :]
        qpos = bi * Q_BLOCK + jnp.arange(Q_BLOCK)
        s = jnp.where(kpos[None, :] <= qpos[:, None], s, -jnp.inf)
        p = jax.nn.softmax(s, axis=-1).astype(v.dtype)
        return jnp.einsum('bhqk,bkhd->bqhd', p, v)

    o = lax.map(block, (qb, cq, jnp.arange(nqb)))
    return o.transpose(1, 0, 2, 3, 4).reshape(b, l, h * dh)


def even_layer(x, norm_g, w_in, rel_bias, lam_re, lam_im, log_dt, b_re, b_im,
               c_re, c_im, d_skip, w_glu, b_glu, w_out):
    bsz, l, _ = x.shape
    hn = rmsnorm(x, norm_g)
    proj = hn @ w_in
    cuts = [A_WIDTH, 2 * A_WIDTH, 3 * A_WIDTH, 3 * A_WIDTH + S5_WIDTH,
            4 * A_WIDTH + S5_WIDTH]
    q, k, v, u, z_a, z_b = jnp.split(proj, cuts, axis=-1)
    shp = (bsz, l, A_HEADS, A_HEAD_DIM)
    o_a = chunked_relpos_attention(q.reshape(shp), k.reshape(shp), v.reshape(shp), rel_bias)
    o_a = o_a * jax.nn.silu(z_a)
    y = jax.nn.gelu(s5_ssm(u, lam_re, lam_im, log_dt, b_re, b_im, c_re, c_im, d_skip).astype(x.dtype))
    o_b = y * jax.nn.sigmoid(y @ w_glu + b_glu)
    o_b = o_b * jax.nn.silu(z_b)
    return x + jnp.concatenate([o_a, o_b], axis=-1) @ w_out


def odd_layer(x, norm_g, w_in, b_forget, w_out):
    bsz, l, _ = x.shape
    hn = rmsnorm(x, norm_g)
    proj = hn @ w_in
    cuts = [C_WIDTH, 2 * C_WIDTH, 3 * C_WIDTH, 4 * C_WIDTH]
    q, k, v, z, f_logit = jnp.split(proj, cuts, axis=-1)
    log_f = jax.nn.log_sigmoid((f_logit + b_forget).astype(jnp.float32))
    shp = (bsz, l, C_HEADS, C_HEAD_DIM)
    o = forgetting_attention(q.reshape(shp), k.reshape(shp), v.reshape(shp), log_f)
    o = o * jax.nn.silu(z)
    return x + o @ w_out


def setup_inputs(seed: int = 0) -> dict:
    key = jax.random.key(seed)
    ks = jax.random.split(key, 20)
    f32 = jnp.float32
    x = jax.random.normal(ks[0], (BATCH, SEQ, D_MODEL), f32)
    norm_even_g = 1.0 + 0.05 * jax.random.normal(ks[1], (N_EVEN, D_MODEL), f32)
    w_in_even = jax.random.normal(ks[2], (N_EVEN, D_MODEL, W_IN_EVEN), f32) * D_MODEL ** -0.5
    rel_bias = 0.2 * jax.random.normal(ks[3], (N_EVEN, A_HEADS, N_REL), f32)
    n = jnp.arange(S5_STATE, dtype=f32)
    s5_lambda_re = -0.5 + 0.01 * jax.random.normal(ks[4], (N_EVEN, S5_GROUPS, S5_STATE), f32)
    s5_lambda_im = math.pi * n + 0.01 * jax.random.normal(ks[5], (N_EVEN, S5_GROUPS, S5_STATE), f32)
    s5_log_dt = jax.random.uniform(ks[6], (N_EVEN, S5_GROUPS), f32,
                                   math.log(DT_MIN), math.log(DT_MAX))
    s5_b_re = jax.random.normal(ks[7], (N_EVEN, S5_GROUPS, S5_STATE, S5_GROUP), f32) * (2 * S5_GROUP) ** -0.5
    s5_b_im = jax.random.normal(ks[8], (N_EVEN, S5_GROUPS, S5_STATE, S5_GROUP), f32) * (2 * S5_GROUP) ** -0.5
    s5_c_re = jax.random.normal(ks[9], (N_EVEN, S5_GROUPS, S5_GROUP, S5_STATE), f32) * (2 * S5_STATE) ** -0.5
    s5_c_im = jax.random.normal(ks[10], (N_EVEN, S5_GROUPS, S5_GROUP, S5_STATE), f32) * (2 * S5_STATE) ** -0.5
    s5_d = 0.5 * jax.random.normal(ks[11], (N_EVEN, S5_GROUPS, S5_GROUP), f32)
    w_glu = jax.random.normal(ks[12], (N_EVEN, S5_WIDTH, S5_WIDTH), f32) * S5_WIDTH ** -0.5
    b_glu = 0.02 * jax.random.normal(ks[13], (N_EVEN, S5_WIDTH), f32)
    w_out_even = jax.random.normal(ks[14], (N_EVEN, EVEN_MIX, D_MODEL), f32) * EVEN_MIX ** -0.5
    norm_odd_g = 1.0 + 0.05 * jax.random.normal(ks[15], (N_ODD, D_MODEL), f32)
    w_in_odd = jax.random.normal(ks[16], (N_ODD, D_MODEL, W_IN_ODD), f32) * D_MODEL ** -0.5
    b_forget = jax.random.uniform(ks[17], (N_ODD, C_HEADS), f32, 1.0, 4.0)
    w_out_odd = jax.random.normal(ks[18], (N_ODD, C_WIDTH, D_MODEL), f32) * C_WIDTH ** -0.5
    final_norm_g = 1.0 + 0.05 * jax.random.normal(ks[19], (D_MODEL,), f32)
    return {"x": x, "norm_even_g": norm_even_g, "w_in_even": w_in_even, "rel_bias": rel_bias,
            "s5_lambda_re": s5_lambda_re, "s5_lambda_im": s5_lambda_im, "s5_log_dt": s5_log_dt,
            "s5_b_re": s5_b_re, "s5_b_im": s5_b_im, "s5_c_re": s5_c_re, "s5_c_im": s5_c_im,
            "s5_d": s5_d, "w_glu": w_glu, "b_glu": b_glu, "w_out_even": w_out_even,
            "norm_odd_g": norm_odd_g, "w_in_odd": w_in_odd, "b_forget": b_forget,
            "w_out_odd": w_out_odd, "final_norm_g": final_norm_g}


def reference(x, norm_even_g, w_in_even, rel_bias, s5_lambda_re, s5_lambda_im, s5_log_dt,
              s5_b_re, s5_b_im, s5_c_re, s5_c_im, s5_d, w_glu, b_glu, w_out_even,
              norm_odd_g, w_in_odd, b_forget, w_out_odd, final_norm_g):
    for layer in range(DEPTH):
        i = layer // 2
        if layer % 2 == 0:
            x = even_layer(x, norm_even_g[i], w_in_even[i], rel_bias[i], s5_lambda_re[i],
                           s5_lambda_im[i], s5_log_dt[i], s5_b_re[i], s5_b_im[i], s5_c_re[i],
                           s5_c_im[i], s5_d[i], w_glu[i], b_glu[i], w_out_even[i])
        else:
            x = odd_layer(x, norm_odd_g[i], w_in_odd[i], b_forget[i], w_out_odd[i])
    return rmsnorm(x, final_norm_g)
```

```python
import math
from contextlib import ExitStack
import numpy as np
import concourse.bass as bass
import concourse.mybir as mybir
from concourse.bass_utils import run_bass_kernel_spmd

F32 = mybir.dt.float32
BF16 = mybir.dt.bfloat16
AF = mybir.ActivationFunctionType
ALU = mybir.AluOpType

D = 2048
L = 4096
NCH = D // 128
EPS = 1e-6
SAME_ENG_SYNC = True
DMA_RING = 12


class Buf:
    def __init__(self, name, h):
        self.name = name
        self.h = h
        self.last_write = None
        self.reads = {}

    def __getitem__(self, idx):
        return self.h[idx]


class Eng:
    def __init__(self, nc, handle, name, is_pe=False):
        self.nc = nc
        self.h = handle
        self.name = name
        self.sem = nc.alloc_semaphore("sem_" + name)
        self.cnt = 0
        self.waited = {}
        self.is_pe = is_pe
        self.ring = None
        self.dma_i = 0

    def wait(self, tok):
        if tok is None:
            return
        sem, val = tok
        if self.waited.get(sem.num, 0) >= val:
            return
        self.h.wait_ge(sem, val)
        self.waited[sem.num] = val


class Prog:
    def __init__(self):
        nc = bass.Bass("TRN2", target_bir_lowering=False)
        self.nc = nc
        self.pe = Eng(nc, nc.tensor, "pe", is_pe=True)
        self.act = Eng(nc, nc.scalar, "act")
        self.dve = Eng(nc, nc.vector, "dve")
        self.pool = Eng(nc, nc.gpsimd, "pool")
        self.sp = Eng(nc, nc.sync, "sp")
        self.nbuf = 0
        self.stack = []
        self.out_toks = []

    def sb(self, name, shape, dt):
        self.nbuf += 1
        nm = "%s_%d" % (name, self.nbuf)
        if self.stack:
            h = self.stack[-1].enter_context(self.nc.sbuf_tensor(nm, list(shape), dt))
            return Buf(name, h)
        return Buf(name, self.nc.alloc_sbuf_tensor(nm, list(shape), dt))

    def phase_begin(self):
        self.stack.append(ExitStack())

    def phase_end(self):
        self.barrier()
        self.stack.pop().close()

    def barrier(self):
        engs = [self.pe, self.act, self.dve, self.pool, self.sp]
        toks = [(e.sem, e.cnt) for e in engs if e.cnt > 0]
        for q in engs:
            if q.ring is not None:
                for i in range(max(0, q.dma_i - DMA_RING), q.dma_i):
                    toks.append((q.ring[i % DMA_RING], 16 * (i // DMA_RING + 1)))
        for e in engs:
            for t in toks:
                if t[0].num != e.sem.num:
                    e.wait(t)

    def ps(self, name, shape, dt=F32):
        return Buf(name, self.nc.alloc_psum_tensor(name, list(shape), dt))

    def dram(self, name, shape, dt, kind="Internal"):
        t = self.nc.dram_tensor(name, list(shape), dt, kind=kind)
        return Buf(name, t.ap())

    def _deps(self, eng, reads, writes):
        toks = []
        for b in reads:
            if b.last_write is not None:
                toks.append(b.last_write)
        for b in writes:
            if b.last_write is not None:
                toks.append(b.last_write)
            toks.extend(b.reads.values())
        for t in toks:
            if t[0].num == eng.sem.num and (eng.is_pe or not SAME_ENG_SYNC):
                continue
            eng.wait(t)

    def _mark(self, tok, reads, writes):
        for b in reads:
            b.reads[tok[0].num] = tok
        for b in writes:
            b.last_write = tok
            b.reads = {}

    def op(self, eng, fn, reads=(), writes=()):
        self._deps(eng, reads, writes)
        ins = fn(eng.h)
        eng.cnt += 1
        ins.then_inc(eng.sem, 1)
        tok = (eng.sem, eng.cnt)
        self._mark(tok, reads, writes)
        return tok

    def dma(self, out, in_, reads=(), writes=(), q=None, **kw):
        q = q or self.sp
        if q.ring is None:
            q.ring = [self.nc.alloc_semaphore("dq_%s_%d" % (q.name, i)) for i in range(DMA_RING)]
        i = q.dma_i
        q.dma_i += 1
        sem = q.ring[i % DMA_RING]
        val = 16 * (i // DMA_RING + 1)
        if i >= DMA_RING:
            q.wait((sem, val - 16))
        self._deps(q, reads, writes)
        q.h.dma_start(out=out, in_=in_, **kw).then_inc(sem, 16)
        tok = (sem, val)
        self._mark(tok, reads, writes)
        return tok


class PSView:
    pass


def build(phases=("all",), dbg=(), x1_is_input=False):
    P = Prog()
    nc = P.nc
    pe, act, dve, pool, sp = P.pe, P.act, P.dve, P.pool, P.sp
    allp = "all" in phases

    def ein(name, shape):
        return P.dram(name, shape, F32, kind="ExternalInput")

    x_in = ein("x", [L, D])
    ident_in = ein("ident", [128, 128])
    sel_in = ein("sel127", [128, 128])
    tri_in = ein("tri", [128, 128])
    ones96_in = ein("ones96", [96, 128])
    g0_in = ein("g0", [128, NCH])
    w0_in = ein("w0", [D, 6144])
    g1_in = ein("g1", [128, NCH])
    w1_in = ein("w1", [D, 8208])
    bf_in = ein("bf", [16, 1])
    wo0_in = ein("wo0", [D, D])
    wo1_in = ein("wo1", [D, D])
    gf_in = ein("gf", [1, D])
    abias_in = ein("abias", [16, 3, 128, 128])
    amask_in = ein("amask", [3, 128, 128])
    abc_in = ein("abc", [128, 16])
    lv_in = ein("lv", [2, 32, 128, 128])
    ly_in = ein("ly", [2, 32, 128, 128])
    lvt_in = ein("lvt", [2, 32, 128, 128])
    slr_in = ein("slr", [128, 32])
    sli_in = ein("sli", [128, 32])
    sldt_in = ein("sldt", [128, 32])
    dsk_in = ein("dsk", [128, 8])
    iota1_in = ein("iota1", [128, 512])
    wglu_in = ein("wglu", [1024, 1024])
    bglu_in = ein("bglu", [128, 8])
    out_d = P.dram("out", [L, D], F32, kind="ExternalOutput")

    x1 = ein("x1", [L, D]) if x1_is_input else P.dram("x1", [L, D], F32)
    mixT = P.dram("mixT", [D, L], BF16)
    qa = P.dram("qa", [1024, L], BF16)
    ka = P.dram("ka", [1024, L], BF16)
    va = P.dram("va", [L, 1024], BF16)
    ut = P.dram("ut", [1024, L], BF16)
    za = P.dram("za", [1024, L], BF16)
    zb = P.dram("zb", [1024, L], BF16)
    ygd = P.dram("ygd", [1024, L], BF16)
    qc = P.dram("qc", [D, L], BF16)
    kc = P.dram("kc", [D, L], BF16)
    vc = P.dram("vc", [L, D], BF16)
    zc = P.dram("zc", [D, L], BF16)

    ident_f = P.sb("ident_f", [128, 128], F32)
    ident_b = P.sb("ident_b", [128, 128], BF16)
    sel_f = P.sb("sel_f", [128, 128], F32)
    tri_f = P.sb("tri_f", [128, 128], F32)
    tri_b = P.sb("tri_b", [128, 128], BF16)
    flog = P.sb("flog", [16, L], F32)
    P.dma(ident_f[:], ident_in[:], reads=[ident_in], writes=[ident_f])
    P.dma(sel_f[:], sel_in[:], reads=[sel_in], writes=[sel_f])
    P.dma(tri_f[:], tri_in[:], reads=[tri_in], writes=[tri_f])
    P.op(dve, lambda e: e.tensor_copy(out=ident_b[:], in_=ident_f[:]), [ident_f], [ident_b])
    P.op(dve, lambda e: e.tensor_copy(out=tri_b[:], in_=tri_f[:]), [tri_f], [tri_b])
    PS = [P.ps("bank%d" % i, [128, 512], F32) for i in range(8)]

    def front(src, w_in, g_in, specs, fcol=None):
        P.phase_begin()
        ST = 2048
        xt = [P.sb("xt", [128, D], F32) for i in range(2)]
        xn = [P.sb("xn", [128, D], BF16) for i in range(2)]
        junk = P.sb("junk", [128, D], BF16)
        ss = [P.sb("ss", [128, 1], F32) for i in range(2)]
        rstd = [P.sb("rstd", [128, 1], F32) for i in range(2)]
        hnT = P.sb("hnT", [128, NCH, ST], BF16)
        gfull = P.sb("gfull", [128, NCH, 128], F32)
        gcol = P.sb("gcol", [128, NCH], F32)
        wf = [P.sb("wf", [128, NCH, 128], F32) for i in range(2)]
        wb = [P.sb("wb", [128, NCH, 128], BF16) for i in range(2)]
        ob = [P.sb("ob", [128, 512], BF16) for i in range(4)]
        acc = PS[4:8]

        def norm_transpose(t0, tl, i):
            s = i % 2
            tpb = [PS[2 * s], PS[2 * s + 1]]
            P.dma(xt[s][:], src[t0:t0 + 128, :], reads=[src], writes=[xt[s]])
            P.op(act, lambda e: e.activation(out=junk[:], in_=xt[s][:], func=AF.Square,
                                             accum_out=ss[s][:]), [xt[s]], [junk, ss[s]])
            P.op(dve, lambda e: e.tensor_scalar(out=rstd[s][:], in0=ss[s][:], scalar1=1.0 / D,
                                                scalar2=EPS, op0=ALU.mult, op1=ALU.add),
                 [ss[s]], [rstd[s]])
            P.op(act, lambda e: e.activation(out=rstd[s][:], in_=rstd[s][:], func=AF.Sqrt),
                 [rstd[s]], [rstd[s]])
            P.op(dve, lambda e: e.reciprocal(out=rstd[s][:], in_=rstd[s][:]), [rstd[s]], [rstd[s]])
            P.op(act, lambda e: e.activation(out=xn[s][:], in_=xt[s][:], func=AF.Copy,
                                             scale=rstd[s][:, 0:1]), [xt[s], rstd[s]], [xn[s]])
            for hf in range(2):
                tv = tpb[hf][:].bitcast(BF16)
                for c8 in range(8):
                    c = hf * 8 + c8
                    P.op(pe, lambda e: e.transpose(out=tv[:, c8 * 128:(c8 + 1) * 128],
                                                   in_=xn[s][:, c * 128:(c + 1) * 128],
                                                   identity=ident_b[:]), [xn[s], ident_b], [tpb[hf]])
                P.op(dve, lambda e: e.tensor_copy(
                    out=hnT[:, hf * 8:(hf + 1) * 8, tl:tl + 128],
                    in_=tv.rearrange("p (c t) -> p c t", c=8)), [tpb[hf]], [hnT])

        P.dma(gcol[:], g_in[:], reads=[g_in], writes=[gcol])
        for c in range(NCH):
            P.op(dve, lambda e: e.tensor_copy(out=gfull[:, c, :],
                                              in_=gcol[:, c:c + 1].to_broadcast([128, 128])),
                 [gcol], [gfull])
        wv = w_in.h.rearrange("(c p) n -> p c n", p=128)
        blocks = []
        for (c0, nco, dst, kind, evac) in specs:
            for cb in range(nco // 128):
                blocks.append((c0 + cb * 128, cb * 128, dst, kind, evac))
        nblk = len(blocks)
        gblk = [0]

        def wload(bi):
            s_ = gblk[0] % 2
            gblk[0] += 1
            cc = blocks[bi][0]
            P.dma(wf[s_][:], wv[:, :, cc:cc + 128], reads=[w_in], writes=[wf[s_]])
            P.op(pool, lambda e: e.tensor_tensor(out=wb[s_][:], in0=wf[s_][:], in1=gfull[:],
                                                 op=ALU.mult), [wf[s_], gfull], [wb[s_]])
            return s_

        for sup in range(L // ST):
            for i in range(ST // 128):
                norm_transpose(sup * ST + i * 128, i * 128, i)
            nxt = wload(0)
            for bi in range(nblk):
                (cc, r0, dst, kind, evac) = blocks[bi]
                s = nxt
                if bi + 1 < nblk:
                    nxt = wload(bi + 1)
                NT4 = ST // 512
                accs = [PS[(bi % 2) * 4 + tt] for tt in range(NT4)]
                if kind == "fm":
                    for c in range(NCH):
                        for tt in range(NT4):
                            P.op(pe, lambda e: e.matmul(accs[tt][:], lhsT=wb[s][:, c, :],
                                                        rhs=hnT[:, c, tt * 512:(tt + 1) * 512],
                                                        start=(c == 0), stop=(c == NCH - 1)),
                                 [wb[s], hnT], [accs[tt]])
                else:
                    for tt in range(NT4):
                        for j in range(4):
                            for c in range(NCH):
                                P.op(pe, lambda e: e.matmul(
                                    accs[tt][:, j * 128:(j + 1) * 128],
                                    lhsT=hnT[:, c, tt * 512 + j * 128: tt * 512 + (j + 1) * 128],
                                    rhs=wb[s][:, c, :], start=(c == 0), stop=(c == NCH - 1)),
                                    [wb[s], hnT], [accs[tt]])
                for tt in range(NT4):
                    a = accs[tt]
                    o = ob[tt % 4]
                    tg = sup * ST + tt * 512
                    if evac[0] == "copy":
                        ev = act
                        P.op(act, lambda e: e.activation(out=o[:], in_=a[:], func=AF.Copy), [a], [o])
                    elif evac[0] == "scale":
                        ev = act
                        P.op(act, lambda e: e.activation(out=o[:], in_=a[:], func=AF.Copy,
                                                         scale=evac[1]), [a], [o])
                    elif evac[0] == "silu":
                        ev = act
                        P.op(act, lambda e: e.activation(out=o[:], in_=a[:], func=AF.Silu),
                             [a], [o])
                    if kind == "fm":
                        P.dma(dst[r0:r0 + 128, tg:tg + 512], o[:], reads=[o], writes=[dst], q=ev)
                    else:
                        P.dma(dst.h[tg:tg + 512, r0:r0 + 128].rearrange("(j p) n -> p j n", p=128),
                              o[:].rearrange("p (j n) -> p j n", j=4), reads=[o], writes=[dst], q=ev)
            if fcol is not None:
                s = gblk[0] % 2
                gblk[0] += 1
                P.dma(wf[s][:, :, 0:16], wv[:, :, fcol:fcol + 16], reads=[w_in], writes=[wf[s]])
                P.op(pool, lambda e: e.tensor_tensor(out=wb[s][:, :, 0:16], in0=wf[s][:, :, 0:16],
                                                     in1=gfull[:, :, 0:16], op=ALU.mult),
                     [wf[s], gfull], [wb[s]])
                for tt in range(ST // 512):
                    a = acc[tt % 4]
                    tg = sup * ST + tt * 512
                    for c in range(NCH):
                        P.op(pe, lambda e: e.matmul(a[0:16, :], lhsT=wb[s][:, c, 0:16],
                                                    rhs=hnT[:, c, tt * 512:(tt + 1) * 512],
                                                    start=(c == 0), stop=(c == NCH - 1)),
                             [wb[s], hnT], [a])
                    P.op(dve, lambda e: e.tensor_copy(out=flog[:, tg:tg + 512], in_=a[0:16, :]),
                         [a], [flog])
        P.phase_end()

    def fox():
        P.phase_begin()
        NB = L // 128
        nbf = P.sb("nbf", [16, 1], F32)
        onec = P.sb("onec", [16, 1], F32)
        cpos = P.sb("cpos", [16, L], F32)
        cT = P.sb("cT", [128, NB, 16], F32)
        crefB = P.sb("crefB", [128, NB, 16], F32)
        P.dma(nbf[:], bf_in[:], reads=[bf_in], writes=[nbf])
        P.op(dve, lambda e: e.tensor_scalar(out=nbf[:], in0=nbf[:], scalar1=-1.0, scalar2=None,
                                            op0=ALU.mult), [nbf], [nbf])
        P.op(dve, lambda e: e.memset(onec[:], 1.0), [], [onec])
        P.op(act, lambda e: e.activation(out=flog[:], in_=flog[:], func=AF.Exp, scale=-1.0,
                                         bias=nbf[:, 0:1]), [flog, nbf], [flog])
        P.op(act, lambda e: e.activation(out=flog[:], in_=flog[:], func=AF.Ln, bias=onec[:, 0:1]),
             [flog, onec], [flog])
        P.op(dve, lambda e: e.tensor_tensor_scan(out=cpos[:], data0=onec[:, 0:1].to_broadcast([16, L]),
                                                 data1=flog[:], initial=0.0, op0=ALU.mult,
                                                 op1=ALU.add), [flog, onec], [cpos])
        for kb in range(NB):
            P.op(pe, lambda e: e.transpose(out=PS[0][:, kb * 16:(kb + 1) * 16],
                                           in_=cpos[0:16, kb * 128:(kb + 1) * 128],
                                           identity=ident_f[0:16, 0:16]), [cpos, ident_f], [PS[0]])
        P.op(dve, lambda e: e.tensor_copy(out=cT[:].rearrange("p a b -> p (a b)"), in_=PS[0][:]),
             [PS[0]], [cT])
        P.op(pe, lambda e: e.matmul(PS[1][:], lhsT=sel_f[:], rhs=cT[:].rearrange("p a b -> p (a b)"),
                                    start=True, stop=True), [sel_f, cT], [PS[1]])
        P.op(dve, lambda e: e.tensor_copy(out=crefB[:].rearrange("p a b -> p (a b)"), in_=PS[1][:]),
             [PS[1]], [crefB])

        kh = [P.sb("kh", [128, L], BF16) for i in range(2)]
        qh = [P.sb("qh", [128, L], BF16) for i in range(2)]
        zh = [P.sb("zh", [128, L], BF16) for i in range(2)]
        vh = [P.sb("vh", [128, NB, 129], BF16) for i in range(2)]
        mixh = [P.sb("mixh", [128, L], BF16) for i in range(2)]
        cr3 = [P.sb("cr3", [96, NB, 128], BF16) for i in range(2)]
        o96f = P.sb("o96f", [96, 128], F32)
        o96 = P.sb("o96", [96, 128], BF16)
        hiT = P.sb("hiT", [128, NB], BF16)
        miT = P.sb("miT", [128, NB], BF16)
        loT = P.sb("loT", [128, NB], BF16)
        r1 = P.sb("r1", [128, NB], F32)
        r2 = P.sb("r2", [128, NB], F32)
        pT = [P.sb("pT", [128, 512], BF16) for i in range(4)]
        rden = [P.sb("rden", [128, 1], F32) for i in range(8)]
        on = [P.sb("on", [128, 128], BF16) for i in range(8)]
        P.dma(o96f[:], ones96_in[:], reads=[ones96_in], writes=[o96f])
        P.op(dve, lambda e: e.tensor_copy(out=o96[:], in_=o96f[:]), [o96f], [o96])
        for i in range(2):
            P.op(pool, lambda e: e.memset(vh[i][:, :, 128:129], 1.0), [], [vh[i]])
            P.op(pool, lambda e: e.memset(cr3[i][:], 0.0), [], [cr3[i]])
        sT = [PS[0], PS[1]]
        oacc = [[PS[2], PS[3]], [PS[4], PS[5]]]
        oTb = [PS[6], PS[7]]
        npt = 0
        nst = 0

        def fox_load(h_):
            s_ = h_ % 2
            r_ = h_ * 128
            P.dma(kh[s_][:], kc[r_:r_ + 128, :], reads=[kc], writes=[kh[s_]])
            P.dma(qh[s_][:], qc[r_:r_ + 128, :], reads=[qc], writes=[qh[s_]])
            P.dma(vh[s_][:, :, 0:128], vc.h[:, r_:r_ + 128].rearrange("(kb p) n -> p kb n", p=128),
                  reads=[vc], writes=[vh[s_]])
            P.dma(zh[s_][:], zc[r_:r_ + 128, :], reads=[zc], writes=[zh[s_]])

        fox_load(0)
        for h in range(16):
            s = h % 2
            r0 = h * 128
            if h + 1 < 16:
                fox_load(h + 1)
            cb_ = crefB[:, :, h]
            P.op(dve, lambda e: e.tensor_scalar(out=hiT[:], in0=cb_, scalar1=-1.0, scalar2=None, op0=ALU.mult),
                 [crefB], [hiT])
            P.op(dve, lambda e: e.scalar_tensor_tensor(out=r1[:], in0=cb_, scalar=-1.0, in1=hiT[:],
                                                       op0=ALU.mult, op1=ALU.subtract), [crefB, hiT], [r1])
            P.op(dve, lambda e: e.tensor_copy(out=miT[:], in_=r1[:]), [r1], [miT])
            P.op(dve, lambda e: e.tensor_tensor(out=r2[:], in0=r1[:], in1=miT[:], op=ALU.subtract), [r1, miT], [r2])
            P.op(dve, lambda e: e.tensor_copy(out=loT[:], in_=r2[:]), [r2], [loT])
            for (row, src_) in ((0, hiT), (32, miT), (64, loT)):
                P.op(dve, lambda e: e.tensor_copy(
                    out=cr3[s][row:row + 1, :, :],
                    in_=src_[row:row + 1, :].unsqueeze(2).to_broadcast([1, NB, 128])), [src_], [cr3[s]])
            cr3f = cr3[s][:].rearrange("p a b -> p (a b)")
            items = [(qg, kb) for qg in range(NB // 4) for kb in range(4 * qg + 4)]

            def emit_qk(i_):
                qg_, kb_ = items[i_]
                q0_ = max(4 * qg_, kb_)
                nq_ = 4 * qg_ + 4 - q0_
                st_ = sT[i_ % 2]
                P.op(pe, lambda e: e.matmul(st_[:, 0:nq_ * 128],
                                            lhsT=kh[s][:, kb_ * 128:(kb_ + 1) * 128],
                                            rhs=qh[s][:, q0_ * 128:(q0_ + nq_) * 128],
                                            start=True, stop=False), [kh[s], qh[s]], [st_])
                P.op(pe, lambda e: e.matmul(st_[:, 0:nq_ * 128], lhsT=o96[:],
                                            rhs=cr3f[:, q0_ * 128:(q0_ + nq_) * 128],
                                            start=False, stop=True), [o96, cr3[s]], [st_])

            def f_norm(qg_):
                par_ = qg_ % 2
                for jj in range(4):
                    ob_ = oacc[par_][jj // 2]
                    c0 = (jj % 2) * 256
                    rd_ = rden[par_ * 4 + jj]
                    on_ = on[par_ * 4 + jj]
                    P.op(dve, lambda e: e.reciprocal(out=rd_[:], in_=ob_[:, c0 + 128:c0 + 129]), [ob_], [rd_])
                    P.op(dve, lambda e: e.tensor_scalar(out=on_[:], in0=ob_[:, c0:c0 + 128],
                                                        scalar1=rd_[:, 0:1], scalar2=None,
                                                        op0=ALU.mult), [ob_, rd_], [on_])

            def f_fin(qg_):
                par_ = qg_ % 2
                otb = oTb[par_]
                otv = otb[:].bitcast(BF16)
                for jj in range(4):
                    on_ = on[par_ * 4 + jj]
                    P.op(pe, lambda e: e.transpose(out=otv[:, jj * 128:(jj + 1) * 128], in_=on_[:],
                                                   identity=ident_b[:]), [on_, ident_b], [otb])
                t0 = qg_ * 512
                P.op(dve, lambda e: e.tensor_tensor(out=mixh[s][:, t0:t0 + 512], in0=otv[:, 0:512],
                                                    in1=zh[s][:, t0:t0 + 512], op=ALU.mult),
                     [otb, zh[s]], [mixh[s]])

            emit_qk(0)
            for i_, (qg, kb) in enumerate(items):
                par = qg % 2
                q0 = max(4 * qg, kb)
                nq = 4 * qg + 4 - q0
                st = sT[i_ % 2]
                if i_ + 1 < len(items):
                    emit_qk(i_ + 1)
                p = pT[npt % 4]
                npt += 1
                P.op(act, lambda e: e.activation(out=p[:, 0:nq * 128], in_=st[:, 0:nq * 128],
                                                 func=AF.Exp, bias=cT[:, kb, h:h + 1]), [st, cT], [p])
                if kb >= 4 * qg:
                    P.op(pool, lambda e: e.tensor_tensor(out=p[:, 0:128], in0=p[:, 0:128], in1=tri_b[:],
                                                         op=ALU.mult), [p, tri_b], [p])
                for j in range(nq):
                    qb = q0 + j
                    jj = qb - 4 * qg
                    ob_ = oacc[par][jj // 2]
                    P.op(pe, lambda e: e.matmul(ob_[:, (jj % 2) * 256:(jj % 2) * 256 + 129],
                                                lhsT=p[:, j * 128:(j + 1) * 128], rhs=vh[s][:, kb, :],
                                                start=(kb == 0), stop=(kb == qb)),
                         [p, vh[s]], [ob_])
                if kb == 4 * qg + 3:
                    f_norm(qg)
                    if qg > 0:
                        f_fin(qg - 1)
            f_fin(NB // 4 - 1)
            P.dma(mixT[r0:r0 + 128, :], mixh[s][:], reads=[mixh[s]], writes=[mixT], q=pool)
        P.phase_end()

    def attn_a():
        P.phase_begin()
        NB = L // 128
        abc = P.sb("abc", [128, 16], F32)
        am = P.sb("am", [128, 3, 128], F32)
        P.dma(abc[:], abc_in[:], reads=[abc_in], writes=[abc])
        P.dma(am[:], amask_in.h.rearrange("r k q -> k r q"), reads=[amask_in], writes=[am])
        abt = [P.sb("abt", [128, 3, 128], F32) for i in range(2)]
        zero_c = P.sb("zero_c", [128, 128], F32)
        P.op(dve, lambda e: e.memset(zero_c[:], 0.0), [], [zero_c])
        E = [P.sb("E", [128, 5, 2, 128], BF16) for i in range(2)]
        khp = [P.sb("khp", [128, L], BF16) for i in range(2)]
        qhp = [P.sb("qz", [128, 2, L], BF16) for i in range(2)]
        for i in range(2):
            P.op(pool, lambda e: e.memset(qhp[i][64:128, 0, :], 0.0), [], [qhp[i]])
            P.op(pool, lambda e: e.memset(qhp[i][0:64, 1, :], 0.0), [], [qhp[i]])
        zhp = [P.sb("zhp", [128, L], BF16) for i in range(2)]
        vhp = [P.sb("vhp", [128, NB, 2, 65], BF16) for i in range(2)]
        mixh = [P.sb("mixh", [128, L], BF16) for i in range(2)]
        pA = [[P.sb("pA", [128, 512], BF16) for h2 in range(2)] for i in range(2)]
        pB = [P.sb("pB", [128, 256], BF16) for i in range(2)]
        rden = [P.sb("rden", [128, 1], F32) for i in range(4)]
        on = [P.sb("on", [128, 128], BF16) for i in range(2)]
        for i in range(2):
            P.op(pool, lambda e: e.memset(vhp[i][:, :, :, 64:65], 1.0), [], [vhp[i]])
        sA = [[PS[0], PS[1]], [PS[2], PS[3]]]
        sBb = PS[4]
        oTb = PS[5]
        oacc = [PS[6], PS[7]]

        def a_load(hp_):
            s_ = hp_ % 2
            r_ = hp_ * 128
            P.dma(khp[s_][:], ka[r_:r_ + 128, :], reads=[ka], writes=[khp[s_]])
            P.dma(qhp[s_][0:64, 0, :], qa[r_:r_ + 64, :], reads=[qa], writes=[qhp[s_]])
            P.dma(qhp[s_][64:128, 1, :], qa[r_ + 64:r_ + 128, :], reads=[qa], writes=[qhp[s_]])
            for h2_ in range(2):
                P.dma(vhp[s_][:, :, h2_, 0:64],
                      va.h[:, r_ + h2_ * 64:r_ + (h2_ + 1) * 64].rearrange("(kb p) n -> p kb n", p=128),
                      reads=[va], writes=[vhp[s_]])
            P.dma(zhp[s_][:], za[r_:r_ + 128, :], reads=[za], writes=[zhp[s_]])

        NHP = 8
        NM = NB
        a_load(0)
        for hp in range(NHP):
            s = hp % 2
            r0 = hp * 128
            if hp + 1 < NHP:
                a_load(hp + 1)
            for h2 in range(2):
                h = hp * 2 + h2
                P.dma(abt[h2][:], abias_in.h[h].rearrange("r k q -> k r q"), reads=[abias_in],
                      writes=[abt[h2]])
                P.op(dve, lambda e: e.tensor_tensor(out=abt[h2][:], in0=abt[h2][:], in1=am[:],
                                                    op=ALU.add), [abt[h2], am], [abt[h2]])
                for (ei, r) in ((0, 0), (1, 3), (2, 4)):
                    P.op(act, lambda e: e.activation(out=E[s][:, r, h2, :], in_=abt[h2][:, ei, :], func=AF.Exp),
                         [abt[h2]], [E[s]])
                for r in (1, 2):
                    P.op(act, lambda e: e.activation(out=E[s][:, r, h2, :], in_=zero_c[:], func=AF.Exp,
                                                     bias=abc[:, h:h + 1]), [zero_c, abc], [E[s]])
            def emit_qk(m_):
                par_ = m_ % 2
                rlo_ = max(0, 4 - m_)
                rq_ = qhp[s][:, :, m_ * 128:(m_ + 1) * 128]
                for r_ in range(rlo_, 4):
                    kb_ = m_ - 4 + r_
                    sa_ = sA[par_][r_ // 2]
                    c_ = (r_ % 2) * 256
                    P.op(pe, lambda e: e.matmul(sa_[:, c_:c_ + 256], lhsT=khp[s][:, kb_ * 128:(kb_ + 1) * 128],
                                                rhs=rq_, start=True, stop=True), [khp[s], qhp[s]], [sa_])
                P.op(pe, lambda e: e.matmul(sBb[:, par_ * 256:(par_ + 1) * 256],
                                            lhsT=khp[s][:, m_ * 128:(m_ + 1) * 128],
                                            rhs=rq_, start=True, stop=True), [khp[s], qhp[s]], [sBb])

            def a_fin(m_):
                par_ = m_ % 2
                otv_ = oTb[:].bitcast(BF16)[:, par_ * 128:(par_ + 1) * 128]
                P.op(pe, lambda e: e.transpose(out=otv_, in_=on[par_][:], identity=ident_b[:]),
                     [on[par_], ident_b], [oTb])
                t0_ = m_ * 128
                P.op(dve, lambda e: e.tensor_tensor(out=mixh[s][:, t0_:t0_ + 128], in0=otv_,
                                                    in1=zhp[s][:, t0_:t0_ + 128], op=ALU.mult),
                     [oTb, zhp[s]], [mixh[s]])

            emit_qk(0)
            for m in range(NM):
                par = m % 2
                ob_ = oacc[par]
                rlo = max(0, 4 - m)
                if m + 1 < NM:
                    emit_qk(m + 1)
                for bk in range(2):
                    r0_ = max(rlo, 2 * bk)
                    if r0_ > 2 * bk + 1:
                        continue
                    cs_ = slice((r0_ - 2 * bk) * 256, 512)
                    sa = sA[par][bk]
                    pa = pA[par][bk]
                    P.op(act, lambda e: e.activation(out=pa[:, cs_], in_=sa[:, cs_], func=AF.Exp), [sa], [pa])
                    P.op(dve, lambda e: e.tensor_tensor(
                        out=pa[:, cs_], in0=pa[:, cs_],
                        in1=E[s][:, r0_:2 * bk + 2, :, :].rearrange("p a b c -> p (a b c)"), op=ALU.mult),
                        [pa, E[s]], [pa])
                pb = pB[par]
                P.op(act, lambda e: e.activation(out=pb[:], in_=sBb[:, par * 256:(par + 1) * 256], func=AF.Exp), [sBb], [pb])
                P.op(pool, lambda e: e.tensor_tensor(out=pb[:], in0=pb[:],
                                                     in1=E[s][:, 4, :, :].rearrange("p b c -> p (b c)"), op=ALU.mult),
                     [pb, E[s]], [pb])
                for h2 in range(2):
                    for r in range(rlo, 5):
                        kb = m - 4 + r
                        if r < 4:
                            pa = pA[par][r // 2]
                            c_ = ((r % 2) * 2 + h2) * 128
                            lhs = pa[:, c_:c_ + 128]
                        else:
                            pa = pb
                            lhs = pb[:, h2 * 128:(h2 + 1) * 128]
                        P.op(pe, lambda e: e.matmul(ob_[:, h2 * 128:h2 * 128 + 65], lhsT=lhs,
                                                    rhs=vhp[s][:, kb, h2, :], start=(r == rlo), stop=(r == 4)),
                             [pa, vhp[s]], [ob_])
                for h2 in range(2):
                    rd = rden[par * 2 + h2]
                    P.op(dve, lambda e: e.reciprocal(out=rd[:], in_=ob_[:, h2 * 128 + 64:h2 * 128 + 65]),
                         [ob_], [rd])
                    P.op(dve, lambda e: e.tensor_scalar(out=on[par][:, h2 * 64:(h2 + 1) * 64],
                                                        in0=ob_[:, h2 * 128:h2 * 128 + 64],
                                                        scalar1=rd[:, 0:1], scalar2=None, op0=ALU.mult),
                         [ob_, rd], [on[par]])
                if m > 0:
                    a_fin(m - 1)
            a_fin(NM - 1)
            P.dma(mixT[r0:r0 + 128, :], mixh[s][:], reads=[mixh[s]], writes=[mixT], q=pool)
        P.phase_end()

    def s5_glu():
        P.phase_begin()
        TWO_PI = 2.0 * math.pi
        T = 512
        NT = L // T
        P.phase_begin()
        iota1 = P.sb("iota1", [128, T], F32)
        pic = P.sb("pic", [128, 1], F32)
        dsk = P.sb("dsk", [128, 8], F32)
        P.dma(iota1[:], iota1_in[:], reads=[iota1_in], writes=[iota1])
        P.dma(dsk[:], dsk_in[:], reads=[dsk_in], writes=[dsk])
        P.op(dve, lambda e: e.memset(pic[:], math.pi), [], [pic])
        names = ["lr", "li", "dt", "rho", "th", "m", "sn", "cs", "are", "aim", "den", "nr", "t1", "t2",
                 "cr", "ci", "ncr"]
        sc = {n: P.sb("s5_" + n, [128, 32], F32) for n in names}
        P.dma(sc["lr"][:], slr_in[:], reads=[slr_in], writes=[sc["lr"]])
        P.dma(sc["li"][:], sli_in[:], reads=[sli_in], writes=[sc["li"]])
        P.dma(sc["dt"][:], sldt_in[:], reads=[sldt_in], writes=[sc["dt"]])

        def tt(o, a, b, op, eng=dve):
            P.op(eng, lambda e: e.tensor_tensor(out=sc[o][:], in0=sc[a][:], in1=sc[b][:], op=op),
                 [sc[a], sc[b]], [sc[o]])

        def ts(o, a, s1, s2, op0, op1=None):
            if op1 is None:
                P.op(dve, lambda e: e.tensor_scalar(out=sc[o][:], in0=sc[a][:], scalar1=s1, scalar2=None,
                                                    op0=op0), [sc[a]], [sc[o]])
            else:
                P.op(dve, lambda e: e.tensor_scalar(out=sc[o][:], in0=sc[a][:], scalar1=s1, scalar2=s2,
                                                    op0=op0, op1=op1), [sc[a]], [sc[o]])

        def sin_of(o, m_):
            P.op(act, lambda e: e.activation(out=sc[o][:], in_=sc[m_][:], func=AF.Sin, scale=-1.0,
                                             bias=pic[:, 0:1]), [sc[m_], pic], [sc[o]])

        P.op(act, lambda e: e.activation(out=sc["dt"][:], in_=sc["dt"][:], func=AF.Exp), [sc["dt"]], [sc["dt"]])
        tt("rho", "lr", "dt", ALU.mult)
        P.op(act, lambda e: e.activation(out=sc["rho"][:], in_=sc["rho"][:], func=AF.Exp), [sc["rho"]], [sc["rho"]])
        tt("th", "li", "dt", ALU.mult)
        ts("m", "th", math.pi, TWO_PI, ALU.is_gt, ALU.mult)
        ts("t1", "th", 3 * math.pi, TWO_PI, ALU.is_gt, ALU.mult)
        tt("m", "m", "t1", ALU.add)
        ts("t1", "th", 5 * math.pi, TWO_PI, ALU.is_gt, ALU.mult)
        tt("m", "m", "t1", ALU.add)
        tt("m", "th", "m", ALU.subtract)
        P.op(act, lambda e: e.activation(out=sc["sn"][:], in_=sc["m"][:], func=AF.Sin), [sc["m"]], [sc["sn"]])
        ts("t1", "m", 0.5 * math.pi, TWO_PI, ALU.is_gt, ALU.mult)
        ts("t2", "m", 0.5 * math.pi, None, ALU.add)
        tt("t2", "t2", "t1", ALU.subtract)
        P.op(act, lambda e: e.activation(out=sc["cs"][:], in_=sc["t2"][:], func=AF.Sin), [sc["t2"]], [sc["cs"]])
        tt("are", "rho", "cs", ALU.mult)
        tt("aim", "rho", "sn", ALU.mult)
        tt("den", "lr", "lr", ALU.mult)
        tt("t1", "li", "li", ALU.mult)
        tt("den", "den", "t1", ALU.add)
        P.op(dve, lambda e: e.reciprocal(out=sc["den"][:], in_=sc["den"][:]), [sc["den"]], [sc["den"]])
        ts("nr", "are", -1.0, None, ALU.add)
        tt("t1", "nr", "lr", ALU.mult)
        tt("t2", "aim", "li", ALU.mult)
        tt("cr", "t1", "t2", ALU.add)
        tt("cr", "cr", "den", ALU.mult)
        tt("t1", "aim", "lr", ALU.mult)
        tt("t2", "nr", "li", ALU.mult)
        tt("ci", "t1", "t2", ALU.subtract)
        tt("ci", "ci", "den", ALU.mult)
        ts("ncr", "cr", -1.0, None, ALU.mult)

        NK = 12
        pc = [sc["cs"]] + [P.sb("s5_pc", [128, 32], F32) for k_ in range(1, NK)]
        pn = [sc["sn"]] + [P.sb("s5_pn", [128, 32], F32) for k_ in range(1, NK)]
        nn = [P.sb("s5_nn", [128, 32], F32) for k_ in range(NK)]
        for k_ in range(NK):
            if k_ > 0:
                a_, b_ = pc[k_ - 1], pn[k_ - 1]
                P.op(dve, lambda e: e.tensor_tensor(out=sc["t1"][:], in0=a_[:], in1=a_[:], op=ALU.mult), [a_], [sc["t1"]])
                P.op(dve, lambda e: e.tensor_tensor(out=sc["t2"][:], in0=b_[:], in1=b_[:], op=ALU.mult), [b_], [sc["t2"]])
                P.op(dve, lambda e: e.tensor_tensor(out=pc[k_][:], in0=sc["t1"][:], in1=sc["t2"][:], op=ALU.subtract),
                     [sc["t1"], sc["t2"]], [pc[k_]])
                P.op(dve, lambda e: e.tensor_tensor(out=sc["t1"][:], in0=a_[:], in1=b_[:], op=ALU.mult), [a_, b_], [sc["t1"]])
                P.op(dve, lambda e: e.tensor_scalar(out=pn[k_][:], in0=sc["t1"][:], scalar1=2.0, scalar2=None,
                                                    op0=ALU.mult), [sc["t1"]], [pn[k_]])
            P.op(dve, lambda e: e.tensor_scalar(out=nn[k_][:], in0=pn[k_][:], scalar1=-1.0, scalar2=None,
                                                op0=ALU.mult), [pn[k_]], [nn[k_]])
        par_ = [None, sc["are"]] + [P.sb("s5_par", [128, 32], F32) for k_ in range(2, 9)]
        pai_ = [None, sc["aim"]] + [P.sb("s5_pai", [128, 32], F32) for k_ in range(2, 9)]
        nai_ = [None] + [P.sb("s5_nai", [128, 32], F32) for k_ in range(1, 9)]
        for k_ in range(2, 9):
            a_, b_ = par_[k_ - 1], pai_[k_ - 1]
            P.op(dve, lambda e: e.tensor_tensor(out=sc["t1"][:], in0=a_[:], in1=sc["are"][:], op=ALU.mult), [a_, sc["are"]], [sc["t1"]])
            P.op(dve, lambda e: e.tensor_tensor(out=sc["t2"][:], in0=b_[:], in1=sc["aim"][:], op=ALU.mult), [b_, sc["aim"]], [sc["t2"]])
            P.op(dve, lambda e: e.tensor_tensor(out=par_[k_][:], in0=sc["t1"][:], in1=sc["t2"][:], op=ALU.subtract),
                 [sc["t1"], sc["t2"]], [par_[k_]])
            P.op(dve, lambda e: e.tensor_tensor(out=sc["t1"][:], in0=a_[:], in1=sc["aim"][:], op=ALU.mult), [a_, sc["aim"]], [sc["t1"]])
            P.op(dve, lambda e: e.tensor_tensor(out=sc["t2"][:], in0=b_[:], in1=sc["are"][:], op=ALU.mult), [b_, sc["are"]], [sc["t2"]])
            P.op(dve, lambda e: e.tensor_tensor(out=pai_[k_][:], in0=sc["t1"][:], in1=sc["t2"][:], op=ALU.add),
                 [sc["t1"], sc["t2"]], [pai_[k_]])
        for k_ in range(1, 9):
            P.op(dve, lambda e: e.tensor_scalar(out=nai_[k_][:], in0=pai_[k_][:], scalar1=-1.0, scalar2=None,
                                                op0=ALU.mult), [pai_[k_]], [nai_[k_]])
        rho8 = P.sb("s5_rho8", [128, 32], F32)
        tt("t1", "rho", "rho", ALU.mult)
        tt("t2", "t1", "t1", ALU.mult)
        P.op(dve, lambda e: e.tensor_tensor(out=rho8[:], in0=sc["t2"][:], in1=sc["t2"][:], op=ALU.mult), [sc["t2"]], [rho8])
        nci = P.sb("s5_nci", [128, 32], F32)
        P.op(dve, lambda e: e.tensor_scalar(out=nci[:], in0=sc["ci"][:], scalar1=-1.0, scalar2=None, op0=ALU.mult),
             [sc["ci"]], [nci])

        LY = P.sb("LY", [128, 2, 32, 128], BF16)
        lst = [P.sb("lst", [128, 8, 128], F32) for i in range(2)]
        k = 0
        for ri in range(2):
            for q4 in range(4):
                st_ = lst[k % 2]
                k += 1
                P.dma(st_[:], ly_in.h[ri, q4 * 8:(q4 + 1) * 8].rearrange("a k m -> k a m"),
                      reads=[ly_in], writes=[st_])
                P.op(act, lambda e: e.activation(out=LY[:, ri, q4 * 8:(q4 + 1) * 8, :], in_=st_[:],
                                                 func=AF.Copy, scale=(1.0, -1.0)[ri]), [st_], [LY])

        J = L // 8
        bst = [P.sb("bst", [128, 2, 4, 128], F32)] * 2
        cst = [P.sb("cst", [128, 2, 4, 128], F32)] * 2
        Gr = P.sb("Gr", [128, 128], F32)
        Gi = P.sb("Gi", [128, 128], F32)
        Gt = P.sb("Gt", [128, 128], F32)
        Gb = P.sb("Gb", [128, 4, 8, 2, 128], BF16)
        Bs = P.sb("Bs", [128, 4, 8, 2, 128], BF16)
        CA = P.sb("CA", [128, 4, 8, 2, 128], BF16)
        Kt = P.sb("Kt", [128, 8, 128], BF16)
        dI = P.sb("dI", [128, 128], F32)
        uP = [P.sb("uP", [128, L], BF16) for i in range(2)]
        uPm = P.sb("uPm", [128, 8, L // 8], BF16)
        tabc = [P.sb("tabc", [128, J], F32) for i in range(2)]
        tabs = [P.sb("tabs", [128, J], F32) for i in range(2)]
        mm = P.sb("mm", [128, J], F32)
        tmps = []
        for i_ in range(1):
            tmps.append({n: P.sb("tmp_" + n, [128, J], F32) for n in
                         ("vr", "vi", "a1", "a2", "a3", "a4", "er", "ei", "wr", "wi", "xr", "xi")})
        tmps.append(tmps[0])
        Xb = P.sb("Xb", [128, 4, 2, J], BF16)
        ypk = P.sb("ypk", [128, L], F32)
        g1 = [P.sb("g1", [128, 512], F32) for i in range(2)]
        g2 = [P.sb("g2", [128, 512], F32) for i in range(2)]
        ygo = [P.sb("ygo", [128, 512], BF16) for i in range(2)]
        P.op(pool, lambda e: e.memset(Xb[:, :, :, 0:1], 0.0), [], [Xb])

        def T2(eng, o, a, b, op):
            P.op(eng, lambda e: e.tensor_tensor(out=o[:], in0=a[:], in1=b[:], op=op), [a, b], [o])

        def cmul(o_r, o_i, i_r, i_i, sr, si, nsi, tmp_):
            P.op(dve, lambda e: e.tensor_scalar(out=tmp_[:], in0=i_r, scalar1=sr, scalar2=None, op0=ALU.mult),
                 [], [tmp_])
            P.op(dve, lambda e: e.scalar_tensor_tensor(out=o_r, in0=i_i, scalar=nsi, in1=tmp_[:],
                                                       op0=ALU.mult, op1=ALU.add), [tmp_], [])
            P.op(dve, lambda e: e.tensor_scalar(out=tmp_[:], in0=i_i, scalar1=sr, scalar2=None, op0=ALU.mult),
                 [], [tmp_])
            P.op(dve, lambda e: e.scalar_tensor_tensor(out=o_i, in0=i_r, scalar=si, in1=tmp_[:],
                                                       op0=ALU.mult, op1=ALU.add), [tmp_], [])

        def s_load1(pk):
            up_ = uP[pk % 2]
            bs_, cs2_ = bst[pk % 2], cst[pk % 2]
            upv = uPm
            P.dma(up_[:], ut[pk * 128:(pk + 1) * 128, :], reads=[ut], writes=[up_])
            for ri in range(2):
                P.dma(bs_[:, ri, :, :], lvt_in.h[ri, pk * 4:(pk + 1) * 4].rearrange("a k m -> k a m"),
                      reads=[lvt_in], writes=[bs_])

        def s_g(pk):
            up_ = uP[pk % 2]
            bs_, cs2_ = bst[pk % 2], cst[pk % 2]
            upv = uPm
            for p4 in range(4):
                gp = pk * 4 + p4
                g_ = slice(gp, gp + 1)
                P.op(dve, lambda e: e.tensor_scalar(out=Gt[:], in0=bs_[:, 0, p4, :], scalar1=sc["cr"][:, g_],
                                                    scalar2=None, op0=ALU.mult), [bs_, sc["cr"]], [Gt])
                P.op(dve, lambda e: e.scalar_tensor_tensor(out=Gr[:], in0=bs_[:, 1, p4, :], scalar=nci[:, g_],
                                                           in1=Gt[:], op0=ALU.mult, op1=ALU.add), [bs_, nci, Gt], [Gr])
                P.op(dve, lambda e: e.tensor_scalar(out=Gt[:], in0=bs_[:, 1, p4, :], scalar1=sc["cr"][:, g_],
                                                    scalar2=None, op0=ALU.mult), [bs_, sc["cr"]], [Gt])
                P.op(dve, lambda e: e.scalar_tensor_tensor(out=Gi[:], in0=bs_[:, 0, p4, :], scalar=sc["ci"][:, g_],
                                                           in1=Gt[:], op0=ALU.mult, op1=ALU.add), [bs_, sc["ci"], Gt], [Gi])
                for tau in range(8):
                    if tau > 0:
                        P.op(dve, lambda e: e.tensor_scalar(out=Gt[:], in0=Gr[:], scalar1=sc["aim"][:, g_],
                                                            scalar2=None, op0=ALU.mult), [Gr, sc["aim"]], [Gt])
                        P.op(dve, lambda e: e.tensor_scalar(out=Gr[:], in0=Gr[:], scalar1=sc["are"][:, g_],
                                                            scalar2=None, op0=ALU.mult), [Gr, sc["are"]], [Gr])
                        P.op(dve, lambda e: e.scalar_tensor_tensor(out=Gr[:], in0=Gi[:], scalar=nai_[1][:, g_],
                                                                   in1=Gr[:], op0=ALU.mult, op1=ALU.add),
                             [Gi, nai_[1], Gr], [Gr])
                        P.op(dve, lambda e: e.scalar_tensor_tensor(out=Gi[:], in0=Gi[:], scalar=sc["are"][:, g_],
                                                                   in1=Gt[:], op0=ALU.mult, op1=ALU.add),
                             [Gi, sc["are"], Gt], [Gi])
                    for ri, G_ in ((0, Gr), (1, Gi)):
                        P.op(act, lambda e: e.activation(out=Gb[:, p4, tau, ri, :], in_=G_[:], func=AF.Copy),
                             [G_], [Gb])
                        tb_ = PS[6 + (ri % 2)]
                        P.op(pe, lambda e: e.transpose(out=tb_[:, 0:128], in_=G_[:], identity=ident_f[:]),
                             [G_, ident_f], [tb_])
                        P.op(act, lambda e: e.activation(out=Bs[:, p4, tau, ri, :], in_=tb_[:, 0:128], func=AF.Copy),
                             [tb_], [Bs])

        def s_load2(pk):
            up_ = uP[pk % 2]
            bs_, cs2_ = bst[pk % 2], cst[pk % 2]
            upv = uPm
            for ri in range(2):
                P.dma(cs2_[:, ri, :, :], ly_in.h[ri, pk * 4:(pk + 1) * 4].rearrange("a k m -> k a m"),
                      reads=[ly_in], writes=[cs2_])
            P.op(dve, lambda e: e.tensor_scalar(out=cs2_[:, 1, :, :], in0=cs2_[:, 1, :, :], scalar1=-1.0, scalar2=None,
                                                op0=ALU.mult), [cs2_], [cs2_])
            upv0 = up_[:].rearrange("p (j s) -> p s j", s=8)
            P.op(act, lambda e: e.activation(out=uPm[:], in_=upv0, func=AF.Copy), [up_], [uPm])

        def s_ca(pk):
            up_ = uP[pk % 2]
            bs_, cs2_ = bst[pk % 2], cst[pk % 2]
            upv = uPm
            for p4 in range(4):
                gp = pk * 4 + p4
                g_ = slice(gp, gp + 1)
                for tl in range(8):
                    ar_, ai_, nai2 = par_[tl + 1][:, g_], pai_[tl + 1][:, g_], nai_[tl + 1][:, g_]
                    P.op(dve, lambda e: e.tensor_scalar(out=Gt[:], in0=cs2_[:, 0, p4, :], scalar1=ar_, scalar2=None,
                                                        op0=ALU.mult), [cs2_, par_[tl + 1]], [Gt])
                    P.op(dve, lambda e: e.scalar_tensor_tensor(out=CA[:, p4, tl, 0, :], in0=cs2_[:, 1, p4, :], scalar=ai_,
                                                               in1=Gt[:], op0=ALU.mult, op1=ALU.add),
                         [cs2_, pai_[tl + 1], Gt], [CA])
                    P.op(dve, lambda e: e.tensor_scalar(out=Gt[:], in0=cs2_[:, 0, p4, :], scalar1=nai2, scalar2=None,
                                                        op0=ALU.mult), [cs2_, nai_[tl + 1]], [Gt])
                    P.op(dve, lambda e: e.scalar_tensor_tensor(out=CA[:, p4, tl, 1, :], in0=cs2_[:, 1, p4, :], scalar=ar_,
                                                               in1=Gt[:], op0=ALU.mult, op1=ALU.add),
                         [cs2_, par_[tl + 1], Gt], [CA])

        def s_k(pk):
            up_ = uP[pk % 2]
            bs_, cs2_ = bst[pk % 2], cst[pk % 2]
            upv = uPm
            P.op(dve, lambda e: e.tensor_scalar(out=dI[:], in0=ident_f[:], scalar1=dsk[:, pk:pk + 1], scalar2=None,
                                                op0=ALU.mult), [ident_f, dsk], [dI])
            for tau in range(8):
                kb_ = PS[6 + (tau % 2)]
                for p4 in range(4):
                    gp = pk * 4 + p4
                    P.op(pe, lambda e: e.matmul(kb_[:, 0:128], lhsT=Gb[:, p4, tau, 0, :], rhs=LY[:, 0, gp, :],
                                                start=(p4 == 0), stop=False), [Gb, LY], [kb_])
                    P.op(pe, lambda e: e.matmul(kb_[:, 0:128], lhsT=Gb[:, p4, tau, 1, :], rhs=LY[:, 1, gp, :],
                                                start=False, stop=(p4 == 3)), [Gb, LY], [kb_])
                if tau == 0:
                    P.op(dve, lambda e: e.tensor_tensor(out=Kt[:, 0, :], in0=kb_[:, 0:128], in1=dI[:], op=ALU.add),
                         [kb_, dI], [Kt])
                else:
                    P.op(act, lambda e: e.activation(out=Kt[:, tau, :], in_=kb_[:, 0:128], func=AF.Copy), [kb_], [Kt])

        def s_main(pk):
            up_ = uP[pk % 2]
            bs_, cs2_ = bst[pk % 2], cst[pk % 2]
            upv = uPm
            def emit_e(p4_):
                for ri_ in range(2):
                    pb_ = PS[2 * (p4_ % 2) + ri_]
                    for s_ in range(8):
                        P.op(pe, lambda e: e.matmul(pb_[:], lhsT=Bs[:, p4_, 7 - s_, ri_, :], rhs=upv[:, s_, :],
                                                    start=(s_ == 0), stop=(s_ == 7)), [Bs, uPm], [pb_])

            def st_tab(p4_):
                gp_ = pk * 4 + p4_
                g2_ = slice(gp_, gp_ + 1)
                tc_, ts_ = tabc[p4_ % 2], tabs[p4_ % 2]
                P.op(dve, lambda e: e.tensor_copy(out=tc_[:, 0:1], in_=pc[3][:, g2_]), [pc[3]], [tc_])
                P.op(dve, lambda e: e.tensor_copy(out=ts_[:, 0:1], in_=pn[3][:, g2_]), [pn[3]], [ts_])
                for k_ in range(9):
                    n_ = 1 << k_
                    lo = slice(0, n_)
                    hi = slice(n_, 2 * n_)
                    kk = k_ + 3
                    P.op(dve, lambda e: e.tensor_scalar(out=mm[:, lo], in0=tc_[:, lo], scalar1=pc[kk][:, g2_],
                                                        scalar2=None, op0=ALU.mult), [tc_, pc[kk]], [mm])
                    P.op(dve, lambda e: e.scalar_tensor_tensor(out=tc_[:, hi], in0=ts_[:, lo], scalar=nn[kk][:, g2_],
                                                               in1=mm[:, lo], op0=ALU.mult, op1=ALU.add),
                         [ts_, nn[kk], mm], [tc_])
                    P.op(dve, lambda e: e.tensor_scalar(out=mm[:, lo], in0=ts_[:, lo], scalar1=pc[kk][:, g2_],
                                                        scalar2=None, op0=ALU.mult), [ts_, pc[kk]], [mm])
                    P.op(dve, lambda e: e.scalar_tensor_tensor(out=ts_[:, hi], in0=tc_[:, lo], scalar=pn[kk][:, g2_],
                                                               in1=mm[:, lo], op0=ALU.mult, op1=ALU.add),
                         [tc_, pn[kk], mm], [ts_])

            emit_e(0)
            for p4 in range(4):
                gp = pk * 4 + p4
                if p4 + 1 < 4:
                    emit_e(p4 + 1)
                st_tab(p4)
                tmp = tmps[p4 % 2]
                psa, psb = PS[2 * (p4 % 2)], PS[2 * (p4 % 2) + 1]
                tc_, ts_ = tabc[p4 % 2], tabs[p4 % 2]
                P.op(act, lambda e: e.activation(out=tmp["vr"][:], in_=psa[:], func=AF.Copy), [psa], [tmp["vr"]])
                P.op(act, lambda e: e.activation(out=tmp["vi"][:], in_=psb[:], func=AF.Copy), [psb], [tmp["vi"]])
                T2(dve, tmp["a1"], tc_, tmp["vr"], ALU.mult)
                T2(dve, tmp["a2"], ts_, tmp["vi"], ALU.mult)
                T2(dve, tmp["a3"], tc_, tmp["vi"], ALU.mult)
                T2(dve, tmp["a4"], ts_, tmp["vr"], ALU.mult)
                T2(dve, tmp["er"], tmp["a1"], tmp["a2"], ALU.add)
                T2(dve, tmp["ei"], tmp["a3"], tmp["a4"], ALU.subtract)
                rhoc = rho8[:, gp:gp + 1].to_broadcast([128, J])
                for (w_, e_) in ((tmp["wr"], tmp["er"]), (tmp["wi"], tmp["ei"])):
                    P.op(dve, lambda e: e.tensor_tensor_scan(out=w_[:], data0=rhoc, data1=e_[:], initial=0.0,
                                                             op0=ALU.mult, op1=ALU.add), [rho8, e_], [w_])
                T2(dve, tmp["a1"], tc_, tmp["wr"], ALU.mult)
                T2(dve, tmp["a2"], ts_, tmp["wi"], ALU.mult)
                T2(dve, tmp["a3"], ts_, tmp["wr"], ALU.mult)
                T2(dve, tmp["a4"], tc_, tmp["wi"], ALU.mult)
                T2(dve, tmp["xr"], tmp["a1"], tmp["a2"], ALU.subtract)
                T2(dve, tmp["xi"], tmp["a3"], tmp["a4"], ALU.add)
                P.op(act, lambda e: e.activation(out=Xb[:, p4, 0, 1:J], in_=tmp["xr"][:, 0:J - 1], func=AF.Copy),
                     [tmp["xr"]], [Xb])
                P.op(act, lambda e: e.activation(out=Xb[:, p4, 1, 1:J], in_=tmp["xi"][:, 0:J - 1], func=AF.Copy),
                     [tmp["xi"]], [Xb])

        def s_y(pk):
            up_ = uP[pk % 2]
            bs_, cs2_ = bst[pk % 2], cst[pk % 2]
            upv = uPm
            ypv = ypk[:].rearrange("p (j s) -> p s j", s=8)
            for tl in range(8):
                yb_ = PS[4 + (tl % 2)]
                first = True
                for p4 in range(4):
                    for ri in range(2):
                        P.op(pe, lambda e: e.matmul(yb_[:], lhsT=CA[:, p4, tl, ri, :], rhs=Xb[:, p4, ri, :],
                                                    start=first, stop=False), [CA, Xb], [yb_])
                        first = False
                for s_ in range(tl + 1):
                    P.op(pe, lambda e: e.matmul(yb_[:], lhsT=Kt[:, tl - s_, :], rhs=upv[:, s_, :],
                                                start=False, stop=(s_ == tl)), [Kt, uPm], [yb_])
                P.op(act, lambda e: e.activation(out=ypv[:, tl, :], in_=yb_[:], func=AF.Copy), [yb_], [ypk])

        def s_gelu(pk):
            for cch in range(8):
                cs_ = slice(cch * 512, (cch + 1) * 512)
                g1_, g2_, yo_ = g1[cch % 2], g2[cch % 2], ygo[cch % 2]
                P.op(act, lambda e: e.activation(out=g1_[:], in_=ypk[:, cs_], func=AF.Square), [ypk], [g1_])
                P.op(dve, lambda e: e.tensor_scalar(out=g1_[:], in0=g1_[:], scalar1=0.044715, scalar2=1.0,
                                                    op0=ALU.mult, op1=ALU.add), [g1_], [g1_])
                P.op(dve, lambda e: e.tensor_tensor(out=g2_[:], in0=g1_[:], in1=ypk[:, cs_], op=ALU.mult),
                     [g1_, ypk], [g2_])
                P.op(act, lambda e: e.activation(out=g2_[:], in_=g2_[:], func=AF.Sigmoid,
                                                 scale=1.5957691216057308), [g2_], [g2_])
                P.op(dve, lambda e: e.tensor_tensor(out=yo_[:], in0=g2_[:], in1=ypk[:, cs_], op=ALU.mult),
                     [g2_, ypk], [yo_])
                P.dma(ygd[pk * 128:(pk + 1) * 128, cs_], yo_[:], reads=[yo_], writes=[ygd], q=act)

        s_load1(0)
        s_g(0)
        s_load2(0)
        s_ca(0)
        s_k(0)
        for pk in range(8):
            s_main(pk)
            s_y(pk)
            if pk + 1 < 8:
                s_load1(pk + 1)
                s_g(pk + 1)
            s_gelu(pk)
            if pk + 1 < 8:
                s_load2(pk + 1)
                s_ca(pk + 1)
                s_k(pk + 1)
        P.phase_end()
        lst = [P.sb("lst2", [128, 8, 128], F32) for i in range(2)]
        wg = P.sb("wg", [128, 8, 1024], BF16)
        bgl = P.sb("bgl", [128, 8], F32)
        P.dma(bgl[:], bglu_in[:], reads=[bglu_in], writes=[bgl])
        wgv = wglu_in.h.rearrange("(c p) n -> p c n", p=128)
        for c in range(8):
            st_ = lst[c % 2]
            P.dma(st_[:].rearrange("p a b -> p (a b)"), wgv[:, c, :], reads=[wglu_in], writes=[st_])
            P.op(pool, lambda e: e.tensor_copy(out=wg[:, c, :], in_=st_[:].rearrange("p a b -> p (a b)")),
                 [st_], [wg])
        zt = [P.sb("zt", [128, T], BF16) for i in range(2)]
        gt = [P.sb("gt", [128, T], BF16) for i in range(2)]
        ygt = [P.sb("ygt", [128, 8, T], BF16) for i in range(2)]
        ygv = ygd.h.rearrange("(c p) t -> p c t", p=128)
        k = 0
        for ti in range(NT):
            t0 = ti * T
            yg_ = ygt[ti % 2]
            P.dma(yg_[:], ygv[:, :, t0:t0 + T], reads=[ygd], writes=[yg_])
            for nb in range(8):
                s_ = k % 2
                k += 1
                a = PS[k % 4]
                P.dma(zt[s_][:], zb[nb * 128:(nb + 1) * 128, t0:t0 + T], reads=[zb], writes=[zt[s_]])
                for jc in range(8):
                    P.op(pe, lambda e: e.matmul(a[:], lhsT=wg[:, jc, nb * 128:(nb + 1) * 128],
                                                rhs=yg_[:, jc, :], start=(jc == 0), stop=(jc == 7)),
                         [wg, yg_], [a])
                P.op(act, lambda e: e.activation(out=gt[s_][:], in_=a[:], func=AF.Sigmoid,
                                                 bias=bgl[:, nb:nb + 1]), [a, bgl], [gt[s_]])
                P.op(dve, lambda e: e.tensor_tensor(out=gt[s_][:], in0=gt[s_][:], in1=yg_[:, nb, :],
                                                    op=ALU.mult), [gt[s_], yg_], [gt[s_]])
                P.op(dve, lambda e: e.tensor_tensor(out=gt[s_][:], in0=gt[s_][:], in1=zt[s_][:],
                                                    op=ALU.mult), [gt[s_], zt[s_]], [gt[s_]])
                P.dma(mixT[1024 + nb * 128:1024 + (nb + 1) * 128, t0:t0 + T], gt[s_][:], reads=[gt[s_]],
                      writes=[mixT], q=act)
        P.phase_end()

    def back(w_out, resid, dst, final):
        P.phase_begin()
        wo = P.sb("wo", [128, NCH, D], BF16)
        wst = [P.sb("wst", [128, D], F32) for i in range(2)]
        mt = [P.sb("mt", [128, NCH, 512], BF16) for i in range(2)]
        xr = [P.sb("xr", [128, D], F32) for i in range(2)]
        xo = [P.sb("xo", [128, D], F32) for i in range(2)]
        wov = w_out.h.rearrange("(c p) n -> p c n", p=128)
        for c in range(NCH):
            P.dma(wst[c % 2][:], wov[:, c, :], reads=[w_out], writes=[wst[c % 2]])
            P.op(pool, lambda e: e.tensor_copy(out=wo[:, c, :], in_=wst[c % 2][:]), [wst[c % 2]], [wo])
        if final:
            gB = P.sb("gB", [128, D], F32)
            junk = P.sb("junk", [128, D], BF16)
            ss = [P.sb("ss", [128, 1], F32) for i in range(2)]
            rs = [P.sb("rs", [128, 1], F32) for i in range(2)]
            P.dma(gB[:], gf_in[0:1, :].partition_broadcast(128), reads=[gf_in], writes=[gB])
        mv = mixT.h.rearrange("(c p) t -> p c t", p=128)
        for tb in range(L // 128):
            s = tb % 2
            t0 = tb * 128
            ms = (tb // 4) % 2
            mo = (tb % 4) * 128
            if tb % 4 == 0:
                P.dma(mt[ms][:], mv[:, :, t0:t0 + 512], reads=[mixT], writes=[mt[ms]])
            P.dma(xr[s][:], resid[t0:t0 + 128, :], reads=[resid], writes=[xr[s]])
            for ng in range(4):
                a = PS[(tb % 2) * 4 + ng]
                for c in range(NCH):
                    P.op(pe, lambda e: e.matmul(a[:], lhsT=mt[ms][:, c, mo:mo + 128],
                                                rhs=wo[:, c, ng * 512:(ng + 1) * 512],
                                                start=(c == 0), stop=(c == NCH - 1)), [mt[ms], wo], [a])
                P.op(dve, lambda e: e.tensor_tensor(out=xo[s][:, ng * 512:(ng + 1) * 512], in0=a[:],
                                                    in1=xr[s][:, ng * 512:(ng + 1) * 512], op=ALU.add),
                     [a, xr[s]], [xo[s]])
            if final:
                P.op(act, lambda e: e.activation(out=junk[:], in_=xo[s][:], func=AF.Square,
                                                 accum_out=ss[s][:]), [xo[s]], [junk, ss[s]])
                P.op(dve, lambda e: e.tensor_scalar(out=rs[s][:], in0=ss[s][:], scalar1=1.0 / D,
                                                    scalar2=EPS, op0=ALU.mult, op1=ALU.add),
                     [ss[s]], [rs[s]])
                P.op(act, lambda e: e.activation(out=rs[s][:], in_=rs[s][:], func=AF.Sqrt),
                     [rs[s]], [rs[s]])
                P.op(dve, lambda e: e.reciprocal(out=rs[s][:], in_=rs[s][:]), [rs[s]], [rs[s]])
                P.op(dve, lambda e: e.scalar_tensor_tensor(out=xo[s][:], in0=xo[s][:],
                                                            scalar=rs[s][:, 0:1], in1=gB[:],
                                                            op0=ALU.mult, op1=ALU.mult),
                     [xo[s], rs[s], gB], [xo[s]])
            t = P.dma(dst[t0:t0 + 128, :], xo[s][:], reads=[xo[s]], writes=[dst], q=act)
            if final:
                P.out_toks.append(t)
        P.phase_end()

    if allp or "front0" in phases:
        front(x_in, w0_in, g0_in, [
            (0, 1024, qa, "fm", ("scale", 0.125)),
            (1024, 1024, ka, "fm", ("copy",)),
            (2048, 1024, va, "tm", ("copy",)),
            (3072, 1024, ut, "fm", ("copy",)),
            (4096, 1024, za, "fm", ("silu",)),
            (5120, 1024, zb, "fm", ("silu",)),
        ])
    if allp or "attn_a" in phases:
        attn_a()
    if allp or "s5" in phases:
        s5_glu()
    if allp or "back0" in phases:
        back(wo0_in, x_in, x1, False)
    if allp or "front1" in phases:
        sc = 1.0 / math.sqrt(128.0)
        front(x1, w1_in, g1_in, [
            (0, 2048, qc, "fm", ("scale", sc)),
            (2048, 2048, kc, "fm", ("copy",)),
            (4096, 2048, vc, "tm", ("copy",)),
            (6144, 2048, zc, "fm", ("silu",)),
        ], fcol=8192)
    if allp or "fox" in phases:
        fox()
    if allp or "back1" in phases:
        back(wo1_in, x1, out_d, True)

    name2buf = dict(qa=qa, ka=ka, va=va, ut=ut, za=za, zb=zb, mixT=mixT, ygd=ygd)
    if dbg:
        P.phase_begin()
        stage = P.sb("dbg_stage", [128, 4096], BF16)
        for nm in dbg:
            src = name2buf[nm]
            shp = src.h.shape
            dd = P.dram("dbg_" + nm, list(shp), BF16, kind="ExternalOutput")
            for r in range(0, shp[0], 128):
                for c in range(0, shp[1], 4096):
                    w = min(4096, shp[1] - c)
                    P.dma(stage[:, 0:w], src[r:r + 128, c:c + w], reads=[src], writes=[stage])
                    t = P.dma(dd[r:r + 128, c:c + w], stage[:, 0:w], reads=[stage], writes=[dd])
                    P.out_toks.append(t)
        P.phase_end()
    for t in P.out_toks:
        sp.wait(t)
    return P


def host_inputs(inp, b):
    f = np.float32
    A = lambda a: np.ascontiguousarray(a, dtype=f)
    m = {}
    m["x"] = A(inp["x"][b])
    m["ident"] = np.eye(128, dtype=f)
    sel = np.zeros((128, 128), f); sel[127, :] = 1
    m["sel127"] = sel
    m["tri"] = np.triu(np.ones((128, 128), f))
    o96 = np.zeros((96, 128), f); o96[[0, 32, 64], :] = 1
    m["ones96"] = o96
    m["g0"] = A(inp["norm_even_g"][0].reshape(16, 128).T)
    m["w0"] = A(inp["w_in_even"][0])
    m["g1"] = A(inp["norm_odd_g"][0].reshape(16, 128).T)
    m["w1"] = A(inp["w_in_odd"][0])
    m["bf"] = A(inp["b_forget"][0].reshape(16, 1))
    m["wo0"] = A(inp["w_out_even"][0])
    m["wo1"] = A(inp["w_out_odd"][0])
    m["gf"] = A(inp["final_norm_g"].reshape(1, 2048))
    rb = np.asarray(inp["rel_bias"][0])
    kl = np.arange(128)[:, None]; ql = np.arange(128)[None, :]
    ab = np.zeros((16, 3, 128, 128), f); am = np.zeros((3, 128, 128), f)
    for i, r in enumerate((0, 3, 4)):
        rel = (ql - kl) + 128 * (4 - r)
        idx = np.clip(rel, -128, 128) + 128
        ab[:, i] = rb[:, idx]
    am[0][(kl < 64) & (ql >= 64)] = -30000.0
    am[2][(kl >= 64) & (ql < 64)] = -30000.0
    m["abias"] = ab; m["amask"] = am
    m["abc"] = A(np.broadcast_to(rb[:, 256][None, :], (128, 16)))
    bre = np.asarray(inp["s5_b_re"][0]); bim = np.asarray(inp["s5_b_im"][0])
    cre = np.asarray(inp["s5_c_re"][0]); cim = np.asarray(inp["s5_c_im"][0])
    lv = np.zeros((2, 32, 128, 128), f); ly = np.zeros((2, 32, 128, 128), f); lvt = np.zeros((2, 32, 128, 128), f)
    slr = np.zeros((128, 32), f); sli = np.zeros((128, 32), f); sldt = np.zeros((128, 32), f)
    dsk = np.zeros((128, 8), f)
    lre = np.asarray(inp["s5_lambda_re"][0]); lim = np.asarray(inp["s5_lambda_im"][0])
    ldt = np.asarray(inp["s5_log_dt"][0]); dsp = np.asarray(inp["s5_d"][0])
    for g in range(64):
        gp = g // 2; slot = g % 2; gl = g % 8
        rs = slice(16 * gl, 16 * gl + 16); ps_ = slice(64 * slot, 64 * slot + 64)
        lv[0, gp, rs, ps_] = bre[g].T
        lv[1, gp, rs, ps_] = bim[g].T
        lvt[0, gp, ps_, rs] = bre[g]
        lvt[1, gp, ps_, rs] = bim[g]
        ly[0, gp, ps_, rs] = cre[g].T
        ly[1, gp, ps_, rs] = cim[g].T
        slr[ps_, gp] = lre[g]; sli[ps_, gp] = lim[g]; sldt[ps_, gp] = ldt[g]
        dsk[rs, g // 8] = dsp[g]
    m.update(lv=lv, ly=ly, lvt=lvt, slr=slr, sli=sli, sldt=sldt, dsk=dsk)
    m["iota1"] = A(np.broadcast_to(np.arange(1, 513, dtype=f)[None, :], (128, 512)))
    m["wglu"] = A(inp["w_glu"][0])
    m["bglu"] = A(inp["b_glu"][0].reshape(8, 128).T)
    return m


_PROG = None


def kernel(**inputs):
    global _PROG
    inp = {k: np.asarray(v) for k, v in inputs.items()}
    if _PROG is None:
        _PROG = build()
    nb = inp["x"].shape[0]
    in_maps = [host_inputs(inp, b) for b in range(nb)]
    res = run_bass_kernel_spmd(_PROG.nc, in_maps, core_ids=list(range(nb)))
    return np.stack([np.asarray(r["out"]) for r in res.results], axis=0).astype(np.float32)
```

```python
import math
from contextlib import ExitStack
import numpy as np
import concourse.bass as bass
import concourse.mybir as mybir
from concourse.bass_utils import run_bass_kernel_spmd

F32 = mybir.dt.float32
BF16 = mybir.dt.bfloat16
AF = mybir.ActivationFunctionType
ALU = mybir.AluOpType

D = 2048
L = 4096
NCH = D // 128
EPS = 1e-6
SAME_ENG_SYNC = True
DMA_RING = 12


class Buf:
    def __init__(self, name, h):
        self.name = name
        self.h = h
        self.last_write = None
        self.reads = {}

    def __getitem__(self, idx):
        return self.h[idx]


class Eng:
    def __init__(self, nc, handle, name, is_pe=False):
        self.nc = nc
        self.h = handle
        self.name = name
        self.sem = nc.alloc_semaphore("sem_" + name)
        self.cnt = 0
        self.waited = {}
        self.is_pe = is_pe
        self.ring = None
        self.dma_i = 0

    def wait(self, tok):
        if tok is None:
            return
        sem, val = tok
        if self.waited.get(sem.num, 0) >= val:
            return
        self.h.wait_ge(sem, val)
        self.waited[sem.num] = val


class Prog:
    def __init__(self):
        nc = bass.Bass("TRN2", target_bir_lowering=False)
        self.nc = nc
        self.pe = Eng(nc, nc.tensor, "pe", is_pe=True)
        self.act = Eng(nc, nc.scalar, "act")
        self.dve = Eng(nc, nc.vector, "dve")
        self.pool = Eng(nc, nc.gpsimd, "pool")
        self.sp = Eng(nc, nc.sync, "sp")
        self.nbuf = 0
        self.stack = []
        self.out_toks = []

    def sb(self, name, shape, dt):
        self.nbuf += 1
        nm = "%s_%d" % (name, self.nbuf)
        if self.stack:
            h = self.stack[-1].enter_context(self.nc.sbuf_tensor(nm, list(shape), dt))
            return Buf(name, h)
        return Buf(name, self.nc.alloc_sbuf_tensor(nm, list(shape), dt))

    def phase_begin(self):
        self.stack.append(ExitStack())

    def phase_end(self):
        self.barrier()
        self.stack.pop().close()

    def barrier(self):
        engs = [self.pe, self.act, self.dve, self.pool, self.sp]
        toks = [(e.sem, e.cnt) for e in engs if e.cnt > 0]
        for q in engs:
            if q.ring is not None:
                for i in range(max(0, q.dma_i - DMA_RING), q.dma_i):
                    toks.append((q.ring[i % DMA_RING], 16 * (i // DMA_RING + 1)))
        for e in engs:
            for t in toks:
                if t[0].num != e.sem.num:
                    e.wait(t)

    def ps(self, name, shape, dt=F32):
        return Buf(name, self.nc.alloc_psum_tensor(name, list(shape), dt))

    def dram(self, name, shape, dt, kind="Internal"):
        t = self.nc.dram_tensor(name, list(shape), dt, kind=kind)
        return Buf(name, t.ap())

    def _deps(self, eng, reads, writes):
        toks = []
        for b in reads:
            if b.last_write is not None:
                toks.append(b.last_write)
        for b in writes:
            if b.last_write is not None:
                toks.append(b.last_write)
            toks.extend(b.reads.values())
        for t in toks:
            if t[0].num == eng.sem.num and (eng.is_pe or not SAME_ENG_SYNC):
                continue
            eng.wait(t)

    def _mark(self, tok, reads, writes):
        for b in reads:
            b.reads[tok[0].num] = tok
        for b in writes:
            b.last_write = tok
            b.reads = {}

    def op(self, eng, fn, reads=(), writes=()):
        self._deps(eng, reads, writes)
        ins = fn(eng.h)
        eng.cnt += 1
        ins.then_inc(eng.sem, 1)
        tok = (eng.sem, eng.cnt)
        self._mark(tok, reads, writes)
        return tok

    def dma(self, out, in_, reads=(), writes=(), q=None, **kw):
        q = q or self.sp
        if q.ring is None:
            q.ring = [self.nc.alloc_semaphore("dq_%s_%d" % (q.name, i)) for i in range(DMA_RING)]
        i = q.dma_i
        q.dma_i += 1
        sem = q.ring[i % DMA_RING]
        val = 16 * (i // DMA_RING + 1)
        if i >= DMA_RING:
            q.wait((sem, val - 16))
        self._deps(q, reads, writes)
        q.h.dma_start(out=out, in_=in_, **kw).then_inc(sem, 16)
        tok = (sem, val)
        self._mark(tok, reads, writes)
        return tok


class PSView:
    pass


def build(phases=("all",), dbg=(), x1_is_input=False):
    P = Prog()
    nc = P.nc
    pe, act, dve, pool, sp = P.pe, P.act, P.dve, P.pool, P.sp
    allp = "all" in phases

    def ein(name, shape):
        return P.dram(name, shape, F32, kind="ExternalInput")

    x_in = ein("x", [L, D])
    ident_in = ein("ident", [128, 128])
    sel_in = ein("sel127", [128, 128])
    tri_in = ein("tri", [128, 128])
    ones96_in = ein("ones96", [96, 128])
    g0_in = ein("g0", [128, NCH])
    w0_in = ein("w0", [D, 6144])
    g1_in = ein("g1", [128, NCH])
    w1_in = ein("w1", [D, 8208])
    bf_in = ein("bf", [16, 1])
    wo0_in = ein("wo0", [D, D])
    wo1_in = ein("wo1", [D, D])
    gf_in = ein("gf", [1, D])
    abias_in = ein("abias", [16, 3, 128, 128])
    amask_in = ein("amask", [3, 128, 128])
    abc_in = ein("abc", [128, 16])
    lv_in = ein("lv", [2, 32, 128, 128])
    ly_in = ein("ly", [2, 32, 128, 128])
    lvt_in = ein("lvt", [2, 32, 128, 128])
    slr_in = ein("slr", [128, 32])
    sli_in = ein("sli", [128, 32])
    sldt_in = ein("sldt", [128, 32])
    dsk_in = ein("dsk", [128, 8])
    iota1_in = ein("iota1", [128, 512])
    wglu_in = ein("wglu", [1024, 1024])
    bglu_in = ein("bglu", [128, 8])
    out_d = P.dram("out", [L, D], F32, kind="ExternalOutput")

    x1 = ein("x1", [L, D]) if x1_is_input else P.dram("x1", [L, D], F32)
    mixT = P.dram("mixT", [D, L], BF16)
    qa = P.dram("qa", [1024, L], BF16)
    ka = P.dram("ka", [1024, L], BF16)
    va = P.dram("va", [L, 1024], BF16)
    ut = P.dram("ut", [1024, L], BF16)
    za = P.dram("za", [1024, L], BF16)
    zb = P.dram("zb", [1024, L], BF16)
    ygd = P.dram("ygd", [1024, L], BF16)
    qc = P.dram("qc", [D, L], BF16)
    kc = P.dram("kc", [D, L], BF16)
    vc = P.dram("vc", [L, D], BF16)
    zc = P.dram("zc", [D, L], BF16)

    ident_f = P.sb("ident_f", [128, 128], F32)
    ident_b = P.sb("ident_b", [128, 128], BF16)
    sel_f = P.sb("sel_f", [128, 128], F32)
    tri_f = P.sb("tri_f", [128, 128], F32)
    tri_b = P.sb("tri_b", [128, 128], BF16)
    flog = P.sb("flog", [16, L], F32)
    P.dma(ident_f[:], ident_in[:], reads=[ident_in], writes=[ident_f])
    P.dma(sel_f[:], sel_in[:], reads=[sel_in], writes=[sel_f])
    P.dma(tri_f[:], tri_in[:], reads=[tri_in], writes=[tri_f])
    P.op(dve, lambda e: e.tensor_copy(out=ident_b[:], in_=ident_f[:]), [ident_f], [ident_b])
    P.op(dve, lambda e: e.tensor_copy(out=tri_b[:], in_=tri_f[:]), [tri_f], [tri_b])
    PS = [P.ps("bank%d" % i, [128, 512], F32) for i in range(8)]

    def front(src, w_in, g_in, specs, fcol=None):
        P.phase_begin()
        ST = 2048
        xt = [P.sb("xt", [128, D], F32) for i in range(2)]
        xn = [P.sb("xn", [128, D], BF16) for i in range(2)]
        ss = [P.sb("ss", [128, 1], F32) for i in range(2)]
        rstd = [P.sb("rstd", [128, 1], F32) for i in range(2)]
        hnTs = [P.sb("hnT", [128, NCH, ST], BF16) for i in range(2)]
        gfull = P.sb("gfull", [128, NCH, 128], F32)
        gcol = P.sb("gcol", [128, NCH], F32)
        wf = [P.sb("wf", [128, NCH, 128], F32) for i in range(2)]
        wb = [P.sb("wb", [128, NCH, 128], BF16) for i in range(2)]
        ob = [P.sb("ob", [128, 512], BF16) for i in range(4)]
        accb = PS[2:8]
        nacc = [0]
        tpb = [PS[0], PS[1]]

        def nt_pre(t0, i):
            s = i % 2
            P.dma(xt[s][:], src[t0:t0 + 128, :], reads=[src], writes=[xt[s]])
            P.op(act, lambda e: e.activation(out=xn[s][:], in_=xt[s][:], func=AF.Square,
                                             accum_out=ss[s][:]), [xt[s]], [xn[s], ss[s]])
            P.op(dve, lambda e: e.tensor_scalar(out=rstd[s][:], in0=ss[s][:], scalar1=1.0 / D,
                                                scalar2=EPS, op0=ALU.mult, op1=ALU.add),
                 [ss[s]], [rstd[s]])
            P.op(act, lambda e: e.activation(out=rstd[s][:], in_=rstd[s][:], func=AF.Sqrt),
                 [rstd[s]], [rstd[s]])
            P.op(dve, lambda e: e.reciprocal(out=rstd[s][:], in_=rstd[s][:]), [rstd[s]], [rstd[s]])
            P.op(act, lambda e: e.activation(out=xn[s][:], in_=xt[s][:], func=AF.Copy,
                                             scale=rstd[s][:, 0:1]), [xt[s], rstd[s]], [xn[s]])

        def nt_post(tl, i, hb):
            s = i % 2
            for hf in range(2):
                tv = tpb[hf][:].bitcast(BF16)
                for c8 in range(8):
                    c = hf * 8 + c8
                    P.op(pe, lambda e: e.transpose(out=tv[:, c8 * 128:(c8 + 1) * 128],
                                                   in_=xn[s][:, c * 128:(c + 1) * 128],
                                                   identity=ident_b[:]), [xn[s], ident_b], [tpb[hf]])
                P.op(dve, lambda e: e.tensor_copy(
                    out=hb[:, hf * 8:(hf + 1) * 8, tl:tl + 128],
                    in_=tv.rearrange("p (c t) -> p c t", c=8)), [tpb[hf]], [hb])

        P.dma(gcol[:], g_in[:], reads=[g_in], writes=[gcol])
        for c in range(NCH):
            P.op(dve, lambda e: e.tensor_copy(out=gfull[:, c, :],
                                              in_=gcol[:, c:c + 1].to_broadcast([128, 128])),
                 [gcol], [gfull])
        wv = w_in.h.rearrange("(c p) n -> p c n", p=128)
        blocks = []
        for (c0, nco, dst, kind, evac) in specs:
            for cb in range(nco // 128):
                blocks.append((c0 + cb * 128, cb * 128, dst, kind, evac))
        nblk = len(blocks)
        gblk = [0]
        NP = ST // 128
        NSUP = L // ST

        def wload(bi):
            s_ = gblk[0] % 2
            gblk[0] += 1
            cc = blocks[bi][0]
            P.dma(wf[s_][:, 0:8, :], wv[:, 0:8, cc:cc + 128], reads=[w_in], writes=[wf[s_]])
            P.dma(wf[s_][:, 8:16, :], wv[:, 8:16, cc:cc + 128], reads=[w_in], writes=[wf[s_]], q=act)
            P.op(pool, lambda e: e.tensor_tensor(out=wb[s_][:], in0=wf[s_][:], in1=gfull[:],
                                                 op=ALU.mult), [wf[s_], gfull], [wb[s_]])
            return s_

        for i in range(NP):
            nt_pre(i * 128, i)
            nt_post(i * 128, i, hnTs[0])
        for sup in range(NSUP):
            hnT = hnTs[sup % 2]
            hnN = hnTs[(sup + 1) % 2]
            more = sup + 1 < NSUP
            nxt = wload(0)
            for bi in range(nblk):
                (cc, r0, dst, kind, evac) = blocks[bi]
                s = nxt
                if bi + 1 < nblk:
                    nxt = wload(bi + 1)
                if more and bi < NP:
                    nt_pre((sup + 1) * ST + bi * 128, bi)
                NT4 = ST // 512
                for tt in range(NT4):
                    a = accb[nacc[0] % 6]
                    nacc[0] += 1
                    o = ob[tt % 4]
                    tg = sup * ST + tt * 512
                    if kind == "fm":
                        for c in range(NCH):
                            P.op(pe, lambda e: e.matmul(a[:], lhsT=wb[s][:, c, :],
                                                        rhs=hnT[:, c, tt * 512:(tt + 1) * 512],
                                                        start=(c == 0), stop=(c == NCH - 1)),
                                 [wb[s], hnT], [a])
                    else:
                        for j in range(4):
                            for c in range(NCH):
                                P.op(pe, lambda e: e.matmul(
                                    a[:, j * 128:(j + 1) * 128],
                                    lhsT=hnT[:, c, tt * 512 + j * 128: tt * 512 + (j + 1) * 128],
                                    rhs=wb[s][:, c, :], start=(c == 0), stop=(c == NCH - 1)),
                                    [wb[s], hnT], [a])
                    if evac[0] == "copy":
                        P.op(act, lambda e: e.activation(out=o[:], in_=a[:], func=AF.Copy), [a], [o])
                    elif evac[0] == "scale":
                        P.op(act, lambda e: e.activation(out=o[:], in_=a[:], func=AF.Copy,
                                                         scale=evac[1]), [a], [o])
                    elif evac[0] == "silu":
                        P.op(act, lambda e: e.activation(out=o[:], in_=a[:], func=AF.Silu),
                             [a], [o])
                    if kind == "fm":
                        P.dma(dst[r0:r0 + 128, tg:tg + 512], o[:], reads=[o], writes=[dst], q=act)
                    else:
                        P.dma(dst.h[tg:tg + 512, r0:r0 + 128].rearrange("(j p) n -> p j n", p=128),
                              o[:].rearrange("p (j n) -> p j n", j=4), reads=[o], writes=[dst], q=act)
                if more and 1 <= bi <= NP:
                    nt_post((bi - 1) * 128, bi - 1, hnN)
            if fcol is not None:
                s = gblk[0] % 2
                gblk[0] += 1
                P.dma(wf[s][:, :, 0:16], wv[:, :, fcol:fcol + 16], reads=[w_in], writes=[wf[s]])
                P.op(pool, lambda e: e.tensor_tensor(out=wb[s][:, :, 0:16], in0=wf[s][:, :, 0:16],
                                                     in1=gfull[:, :, 0:16], op=ALU.mult),
                     [wf[s], gfull], [wb[s]])
                for tt in range(ST // 512):
                    a = accb[nacc[0] % 6]
                    nacc[0] += 1
                    tg = sup * ST + tt * 512
                    for c in range(NCH):
                        P.op(pe, lambda e: e.matmul(a[0:16, :], lhsT=wb[s][:, c, 0:16],
                                                    rhs=hnT[:, c, tt * 512:(tt + 1) * 512],
                                                    start=(c == 0), stop=(c == NCH - 1)),
                             [wb[s], hnT], [a])
                    P.op(dve, lambda e: e.tensor_copy(out=flog[:, tg:tg + 512], in_=a[0:16, :]),
                         [a], [flog])
        P.phase_end()

    def fox():
        P.phase_begin()
        NB = L // 128
        nbf = P.sb("nbf", [16, 1], F32)
        onec = P.sb("onec", [16, 1], F32)
        cpos = P.sb("cpos", [16, L], F32)
        cT = P.sb("cT", [128, NB, 16], F32)
        crefB = P.sb("crefB", [128, NB, 16], F32)
        P.dma(nbf[:], bf_in[:], reads=[bf_in], writes=[nbf])
        P.op(dve, lambda e: e.tensor_scalar(out=nbf[:], in0=nbf[:], scalar1=-1.0, scalar2=None,
                                            op0=ALU.mult), [nbf], [nbf])
        P.op(dve, lambda e: e.memset(onec[:], 1.0), [], [onec])
        P.op(act, lambda e: e.activation(out=flog[:], in_=flog[:], func=AF.Exp, scale=-1.0,
                                         bias=nbf[:, 0:1]), [flog, nbf], [flog])
        P.op(act, lambda e: e.activation(out=flog[:], in_=flog[:], func=AF.Ln, bias=onec[:, 0:1]),
             [flog, onec], [flog])
        P.op(dve, lambda e: e.tensor_tensor_scan(out=cpos[:], data0=onec[:, 0:1].to_broadcast([16, L]),
                                                 data1=flog[:], initial=0.0, op0=ALU.mult,
                                                 op1=ALU.add), [flog, onec], [cpos])
        for kb in range(NB):
            P.op(pe, lambda e: e.transpose(out=PS[0][:, kb * 16:(kb + 1) * 16],
                                           in_=cpos[0:16, kb * 128:(kb + 1) * 128],
                                           identity=ident_f[0:16, 0:16]), [cpos, ident_f], [PS[0]])
        P.op(dve, lambda e: e.tensor_copy(out=cT[:].rearrange("p a b -> p (a b)"), in_=PS[0][:]),
             [PS[0]], [cT])
        P.op(pe, lambda e: e.matmul(PS[1][:], lhsT=sel_f[:], rhs=cT[:].rearrange("p a b -> p (a b)"),
                                    start=True, stop=True), [sel_f, cT], [PS[1]])
        P.op(dve, lambda e: e.tensor_copy(out=crefB[:].rearrange("p a b -> p (a b)"), in_=PS[1][:]),
             [PS[1]], [crefB])

        kh = [P.sb("kh", [128, L], BF16) for i in range(2)]
        qh = [P.sb("qh", [128, L], BF16) for i in range(2)]
        zh = [P.sb("zh", [128, L], BF16) for i in range(2)]
        vh = [P.sb("vh", [128, NB, 129], BF16) for i in range(2)]
        mixh = [P.sb("mixh", [128, L], BF16) for i in range(2)]
        cr3 = [P.sb("cr3", [96, NB, 128], BF16) for i in range(2)]
        o96f = P.sb("o96f", [96, 128], F32)
        o96 = P.sb("o96", [96, 128], BF16)
        hiT = P.sb("hiT", [128, NB], BF16)
        miT = P.sb("miT", [128, NB], BF16)
        loT = P.sb("loT", [128, NB], BF16)
        r1 = P.sb("r1", [128, NB], F32)
        r2 = P.sb("r2", [128, NB], F32)
        pT = [P.sb("pT", [128, 512], BF16) for i in range(4)]
        rden = [P.sb("rden", [128, 1], F32) for i in range(8)]
        on = [P.sb("on", [128, 128], BF16) for i in range(8)]
        P.dma(o96f[:], ones96_in[:], reads=[ones96_in], writes=[o96f])
        P.op(dve, lambda e: e.tensor_copy(out=o96[:], in_=o96f[:]), [o96f], [o96])
        for i in range(2):
            P.op(pool, lambda e: e.memset(vh[i][:, :, 128:129], 1.0), [], [vh[i]])
            P.op(pool, lambda e: e.memset(cr3[i][:], 0.0), [], [cr3[i]])
        sT = [PS[0], PS[1]]
        oacc = [[PS[2], PS[3]], [PS[4], PS[5]]]
        oTb = [PS[6], PS[7]]
        npt = 0
        nst = 0

        def fox_load(h_):
            s_ = h_ % 2
            r_ = h_ * 128
            P.dma(kh[s_][:], kc[r_:r_ + 128, :], reads=[kc], writes=[kh[s_]])
            P.dma(qh[s_][:], qc[r_:r_ + 128, :], reads=[qc], writes=[qh[s_]])
            P.dma(vh[s_][:, :, 0:128], vc.h[:, r_:r_ + 128].rearrange("(kb p) n -> p kb n", p=128),
                  reads=[vc], writes=[vh[s_]])
            P.dma(zh[s_][:], zc[r_:r_ + 128, :], reads=[zc], writes=[zh[s_]])

        fox_load(0)
        for h in range(16):
            s = h % 2
            r0 = h * 128
            if h + 1 < 16:
                fox_load(h + 1)
            cb_ = crefB[:, :, h]
            P.op(dve, lambda e: e.tensor_scalar(out=hiT[:], in0=cb_, scalar1=-1.0, scalar2=None, op0=ALU.mult),
                 [crefB], [hiT])
            P.op(dve, lambda e: e.scalar_tensor_tensor(out=r1[:], in0=cb_, scalar=-1.0, in1=hiT[:],
                                                       op0=ALU.mult, op1=ALU.subtract), [crefB, hiT], [r1])
            P.op(dve, lambda e: e.tensor_copy(out=miT[:], in_=r1[:]), [r1], [miT])
            P.op(dve, lambda e: e.tensor_tensor(out=r2[:], in0=r1[:], in1=miT[:], op=ALU.subtract), [r1, miT], [r2])
            P.op(dve, lambda e: e.tensor_copy(out=loT[:], in_=r2[:]), [r2], [loT])
            for (row, src_) in ((0, hiT), (32, miT), (64, loT)):
                P.op(dve, lambda e: e.tensor_copy(
                    out=cr3[s][row:row + 1, :, :],
                    in_=src_[row:row + 1, :].unsqueeze(2).to_broadcast([1, NB, 128])), [src_], [cr3[s]])
            cr3f = cr3[s][:].rearrange("p a b -> p (a b)")
            items = [(qg, kb) for qg in range(NB // 4) for kb in range(4 * qg + 4)]

            def emit_qk(i_):
                qg_, kb_ = items[i_]
                q0_ = max(4 * qg_, kb_)
                nq_ = 4 * qg_ + 4 - q0_
                st_ = sT[i_ % 2]
                P.op(pe, lambda e: e.matmul(st_[:, 0:nq_ * 128],
                                            lhsT=kh[s][:, kb_ * 128:(kb_ + 1) * 128],
                                            rhs=qh[s][:, q0_ * 128:(q0_ + nq_) * 128],
                                            start=True, stop=False), [kh[s], qh[s]], [st_])
                P.op(pe, lambda e: e.matmul(st_[:, 0:nq_ * 128], lhsT=o96[:],
                                            rhs=cr3f[:, q0_ * 128:(q0_ + nq_) * 128],
                                            start=False, stop=True), [o96, cr3[s]], [st_])

            def f_norm(qg_):
                par_ = qg_ % 2
                for jj in range(4):
                    ob_ = oacc[par_][jj // 2]
                    c0 = (jj % 2) * 256
                    rd_ = rden[par_ * 4 + jj]
                    on_ = on[par_ * 4 + jj]
                    P.op(dve, lambda e: e.reciprocal(out=rd_[:], in_=ob_[:, c0 + 128:c0 + 129]), [ob_], [rd_])
                    P.op(dve, lambda e: e.tensor_scalar(out=on_[:], in0=ob_[:, c0:c0 + 128],
                                                        scalar1=rd_[:, 0:1], scalar2=None,
                                                        op0=ALU.mult), [ob_, rd_], [on_])

            def f_fin(qg_):
                par_ = qg_ % 2
                otb = oTb[par_]
                otv = otb[:].bitcast(BF16)
                for jj in range(4):
                    on_ = on[par_ * 4 + jj]
                    P.op(pe, lambda e: e.transpose(out=otv[:, jj * 128:(jj + 1) * 128], in_=on_[:],
                                                   identity=ident_b[:]), [on_, ident_b], [otb])
                t0 = qg_ * 512
                P.op(dve, lambda e: e.tensor_tensor(out=mixh[s][:, t0:t0 + 512], in0=otv[:, 0:512],
                                                    in1=zh[s][:, t0:t0 + 512], op=ALU.mult),
                     [otb, zh[s]], [mixh[s]])

            emit_qk(0)
            for i_, (qg, kb) in enumerate(items):
                par = qg % 2
                q0 = max(4 * qg, kb)
                nq = 4 * qg + 4 - q0
                st = sT[i_ % 2]
                if i_ + 1 < len(items):
                    emit_qk(i_ + 1)
                p = pT[npt % 4]
                npt += 1
                P.op(act, lambda e: e.activation(out=p[:, 0:nq * 128], in_=st[:, 0:nq * 128],
                                                 func=AF.Exp, bias=cT[:, kb, h:h + 1]), [st, cT], [p])
                if kb >= 4 * qg:
                    P.op(pool, lambda e: e.tensor_tensor(out=p[:, 0:128], in0=p[:, 0:128], in1=tri_b[:],
                                                         op=ALU.mult), [p, tri_b], [p])
                for j in range(nq):
                    qb = q0 + j
                    jj = qb - 4 * qg
                    ob_ = oacc[par][jj // 2]
                    P.op(pe, lambda e: e.matmul(ob_[:, (jj % 2) * 256:(jj % 2) * 256 + 129],
                                                lhsT=p[:, j * 128:(j + 1) * 128], rhs=vh[s][:, kb, :],
                                                start=(kb == 0), stop=(kb == qb)),
                         [p, vh[s]], [ob_])
                if kb == 4 * qg + 3:
                    f_norm(qg)
                    if qg > 0:
                        f_fin(qg - 1)
            f_fin(NB // 4 - 1)
            P.dma(mixT[r0:r0 + 128, :], mixh[s][:], reads=[mixh[s]], writes=[mixT], q=pool)
        P.phase_end()

    def attn_a():
        P.phase_begin()
        NB = L // 128
        abc = P.sb("abc", [128, 16], F32)
        am = P.sb("am", [128, 3, 128], F32)
        P.dma(abc[:], abc_in[:], reads=[abc_in], writes=[abc])
        P.dma(am[:], amask_in.h.rearrange("r k q -> k r q"), reads=[amask_in], writes=[am])
        abt = [P.sb("abt", [128, 3, 128], F32) for i in range(2)]
        zero_c = P.sb("zero_c", [128, 128], F32)
        P.op(dve, lambda e: e.memset(zero_c[:], 0.0), [], [zero_c])
        E = [P.sb("E", [128, 5, 2, 128], BF16) for i in range(2)]
        khp = [P.sb("khp", [128, L], BF16) for i in range(2)]
        qhp = [P.sb("qz", [128, 2, L], BF16) for i in range(2)]
        for i in range(2):
            P.op(pool, lambda e: e.memset(qhp[i][64:128, 0, :], 0.0), [], [qhp[i]])
            P.op(pool, lambda e: e.memset(qhp[i][0:64, 1, :], 0.0), [], [qhp[i]])
        zhp = [P.sb("zhp", [128, L], BF16) for i in range(2)]
        vhp = [P.sb("vhp", [128, NB, 2, 65], BF16) for i in range(2)]
        mixh = [P.sb("mixh", [128, L], BF16) for i in range(2)]
        pA = [[P.sb("pA", [128, 512], BF16) for h2 in range(2)] for i in range(2)]
        pB = [P.sb("pB", [128, 256], BF16) for i in range(2)]
        rden = [P.sb("rden", [128, 1], F32) for i in range(4)]
        on = [P.sb("on", [128, 128], BF16) for i in range(2)]
        for i in range(2):
            P.op(pool, lambda e: e.memset(vhp[i][:, :, :, 64:65], 1.0), [], [vhp[i]])
        sA = [[PS[0], PS[1]], [PS[2], PS[3]]]
        sBb = PS[4]
        oTb = PS[5]
        oacc = [PS[6], PS[7]]

        def a_load(hp_):
            s_ = hp_ % 2
            r_ = hp_ * 128
            P.dma(khp[s_][:], ka[r_:r_ + 128, :], reads=[ka], writes=[khp[s_]])
            P.dma(qhp[s_][0:64, 0, :], qa[r_:r_ + 64, :], reads=[qa], writes=[qhp[s_]])
            P.dma(qhp[s_][64:128, 1, :], qa[r_ + 64:r_ + 128, :], reads=[qa], writes=[qhp[s_]])
            for h2_ in range(2):
                P.dma(vhp[s_][:, :, h2_, 0:64],
                      va.h[:, r_ + h2_ * 64:r_ + (h2_ + 1) * 64].rearrange("(kb p) n -> p kb n", p=128),
                      reads=[va], writes=[vhp[s_]])
            P.dma(zhp[s_][:], za[r_:r_ + 128, :], reads=[za], writes=[zhp[s_]])

        NHP = 8
        NM = NB
        a_load(0)
        for hp in range(NHP):
            s = hp % 2
            r0 = hp * 128
            if hp + 1 < NHP:
                a_load(hp + 1)
            for h2 in range(2):
                h = hp * 2 + h2
                P.dma(abt[h2][:], abias_in.h[h].rearrange("r k q -> k r q"), reads=[abias_in],
                      writes=[abt[h2]])
                P.op(dve, lambda e: e.tensor_tensor(out=abt[h2][:], in0=abt[h2][:], in1=am[:],
                                                    op=ALU.add), [abt[h2], am], [abt[h2]])
                for (ei, r) in ((0, 0), (1, 3), (2, 4)):
                    P.op(act, lambda e: e.activation(out=E[s][:, r, h2, :], in_=abt[h2][:, ei, :], func=AF.Exp),
                         [abt[h2]], [E[s]])
                for r in (1, 2):
                    P.op(act, lambda e: e.activation(out=E[s][:, r, h2, :], in_=zero_c[:], func=AF.Exp,
                                                     bias=abc[:, h:h + 1]), [zero_c, abc], [E[s]])
            def emit_qk(m_):
                par_ = m_ % 2
                rlo_ = max(0, 4 - m_)
                rq_ = qhp[s][:, :, m_ * 128:(m_ + 1) * 128]
                for r_ in range(rlo_, 4):
                    kb_ = m_ - 4 + r_
                    sa_ = sA[par_][r_ // 2]
                    c_ = (r_ % 2) * 256
                    P.op(pe, lambda e: e.matmul(sa_[:, c_:c_ + 256], lhsT=khp[s][:, kb_ * 128:(kb_ + 1) * 128],
                                                rhs=rq_, start=True, stop=True), [khp[s], qhp[s]], [sa_])
                P.op(pe, lambda e: e.matmul(sBb[:, par_ * 256:(par_ + 1) * 256],
                                            lhsT=khp[s][:, m_ * 128:(m_ + 1) * 128],
                                            rhs=rq_, start=True, stop=True), [khp[s], qhp[s]], [sBb])

            def a_fin(m_):
                par_ = m_ % 2
                otv_ = oTb[:].bitcast(BF16)[:, par_ * 128:(par_ + 1) * 128]
                P.op(pe, lambda e: e.transpose(out=otv_, in_=on[par_][:], identity=ident_b[:]),
                     [on[par_], ident_b], [oTb])
                t0_ = m_ * 128
                P.op(dve, lambda e: e.tensor_tensor(out=mixh[s][:, t0_:t0_ + 128], in0=otv_,
                                                    in1=zhp[s][:, t0_:t0_ + 128], op=ALU.mult),
                     [oTb, zhp[s]], [mixh[s]])

            emit_qk(0)
            for m in range(NM):
                par = m % 2
                ob_ = oacc[par]
                rlo = max(0, 4 - m)
                if m + 1 < NM:
                    emit_qk(m + 1)
                for bk in range(2):
                    r0_ = max(rlo, 2 * bk)
                    if r0_ > 2 * bk + 1:
                        continue
                    cs_ = slice((r0_ - 2 * bk) * 256, 512)
                    sa = sA[par][bk]
                    pa = pA[par][bk]
                    P.op(act, lambda e: e.activation(out=pa[:, cs_], in_=sa[:, cs_], func=AF.Exp), [sa], [pa])
                    P.op(dve, lambda e: e.tensor_tensor(
                        out=pa[:, cs_], in0=pa[:, cs_],
                        in1=E[s][:, r0_:2 * bk + 2, :, :].rearrange("p a b c -> p (a b c)"), op=ALU.mult),
                        [pa, E[s]], [pa])
                pb = pB[par]
                P.op(act, lambda e: e.activation(out=pb[:], in_=sBb[:, par * 256:(par + 1) * 256], func=AF.Exp), [sBb], [pb])
                P.op(pool, lambda e: e.tensor_tensor(out=pb[:], in0=pb[:],
                                                     in1=E[s][:, 4, :, :].rearrange("p b c -> p (b c)"), op=ALU.mult),
                     [pb, E[s]], [pb])
                for h2 in range(2):
                    for r in range(rlo, 5):
                        kb = m - 4 + r
                        if r < 4:
                            pa = pA[par][r // 2]
                            c_ = ((r % 2) * 2 + h2) * 128
                            lhs = pa[:, c_:c_ + 128]
                        else:
                            pa = pb
                            lhs = pb[:, h2 * 128:(h2 + 1) * 128]
                        P.op(pe, lambda e: e.matmul(ob_[:, h2 * 128:h2 * 128 + 65], lhsT=lhs,
                                                    rhs=vhp[s][:, kb, h2, :], start=(r == rlo), stop=(r == 4)),
                             [pa, vhp[s]], [ob_])
                for h2 in range(2):
                    rd = rden[par * 2 + h2]
                    P.op(dve, lambda e: e.reciprocal(out=rd[:], in_=ob_[:, h2 * 128 + 64:h2 * 128 + 65]),
                         [ob_], [rd])
                    P.op(dve, lambda e: e.tensor_scalar(out=on[par][:, h2 * 64:(h2 + 1) * 64],
                                                        in0=ob_[:, h2 * 128:h2 * 128 + 64],
                                                        scalar1=rd[:, 0:1], scalar2=None, op0=ALU.mult),
                         [ob_, rd], [on[par]])
                if m > 0:
                    a_fin(m - 1)
            a_fin(NM - 1)
            P.dma(mixT[r0:r0 + 128, :], mixh[s][:], reads=[mixh[s]], writes=[mixT], q=pool)
        P.phase_end()

    def s5_glu():
        P.phase_begin()
        TWO_PI = 2.0 * math.pi
        T = 512
        NT = L // T
        P.phase_begin()
        iota1 = P.sb("iota1", [128, T], F32)
        pic = P.sb("pic", [128, 1], F32)
        dsk = P.sb("dsk", [128, 8], F32)
        P.dma(iota1[:], iota1_in[:], reads=[iota1_in], writes=[iota1])
        P.dma(dsk[:], dsk_in[:], reads=[dsk_in], writes=[dsk])
        P.op(dve, lambda e: e.memset(pic[:], math.pi), [], [pic])
        names = ["lr", "li", "dt", "rho", "th", "m", "sn", "cs", "are", "aim", "den", "nr", "t1", "t2",
                 "cr", "ci", "ncr"]
        sc = {n: P.sb("s5_" + n, [128, 32], F32) for n in names}
        P.dma(sc["lr"][:], slr_in[:], reads=[slr_in], writes=[sc["lr"]])
        P.dma(sc["li"][:], sli_in[:], reads=[sli_in], writes=[sc["li"]])
        P.dma(sc["dt"][:], sldt_in[:], reads=[sldt_in], writes=[sc["dt"]])

        def tt(o, a, b, op, eng=dve):
            P.op(eng, lambda e: e.tensor_tensor(out=sc[o][:], in0=sc[a][:], in1=sc[b][:], op=op),
                 [sc[a], sc[b]], [sc[o]])

        def ts(o, a, s1, s2, op0, op1=None):
            if op1 is None:
                P.op(dve, lambda e: e.tensor_scalar(out=sc[o][:], in0=sc[a][:], scalar1=s1, scalar2=None,
                                                    op0=op0), [sc[a]], [sc[o]])
            else:
                P.op(dve, lambda e: e.tensor_scalar(out=sc[o][:], in0=sc[a][:], scalar1=s1, scalar2=s2,
                                                    op0=op0, op1=op1), [sc[a]], [sc[o]])

        def sin_of(o, m_):
            P.op(act, lambda e: e.activation(out=sc[o][:], in_=sc[m_][:], func=AF.Sin, scale=-1.0,
                                             bias=pic[:, 0:1]), [sc[m_], pic], [sc[o]])

        P.op(act, lambda e: e.activation(out=sc["dt"][:], in_=sc["dt"][:], func=AF.Exp), [sc["dt"]], [sc["dt"]])
        tt("rho", "lr", "dt", ALU.mult)
        P.op(act, lambda e: e.activation(out=sc["rho"][:], in_=sc["rho"][:], func=AF.Exp), [sc["rho"]], [sc["rho"]])
        tt("th", "li", "dt", ALU.mult)
        ts("m", "th", math.pi, TWO_PI, ALU.is_gt, ALU.mult)
        ts("t1", "th", 3 * math.pi, TWO_PI, ALU.is_gt, ALU.mult)
        tt("m", "m", "t1", ALU.add)
        ts("t1", "th", 5 * math.pi, TWO_PI, ALU.is_gt, ALU.mult)
        tt("m", "m", "t1", ALU.add)
        tt("m", "th", "m", ALU.subtract)
        P.op(act, lambda e: e.activation(out=sc["sn"][:], in_=sc["m"][:], func=AF.Sin), [sc["m"]], [sc["sn"]])
        ts("t1", "m", 0.5 * math.pi, TWO_PI, ALU.is_gt, ALU.mult)
        ts("t2", "m", 0.5 * math.pi, None, ALU.add)
        tt("t2", "t2", "t1", ALU.subtract)
        P.op(act, lambda e: e.activation(out=sc["cs"][:], in_=sc["t2"][:], func=AF.Sin), [sc["t2"]], [sc["cs"]])
        tt("are", "rho", "cs", ALU.mult)
        tt("aim", "rho", "sn", ALU.mult)
        tt("den", "lr", "lr", ALU.mult)
        tt("t1", "li", "li", ALU.mult)
        tt("den", "den", "t1", ALU.add)
        P.op(dve, lambda e: e.reciprocal(out=sc["den"][:], in_=sc["den"][:]), [sc["den"]], [sc["den"]])
        ts("nr", "are", -1.0, None, ALU.add)
        tt("t1", "nr", "lr", ALU.mult)
        tt("t2", "aim", "li", ALU.mult)
        tt("cr", "t1", "t2", ALU.add)
        tt("cr", "cr", "den", ALU.mult)
        tt("t1", "aim", "lr", ALU.mult)
        tt("t2", "nr", "li", ALU.mult)
        tt("ci", "t1", "t2", ALU.subtract)
        tt("ci", "ci", "den", ALU.mult)
        ts("ncr", "cr", -1.0, None, ALU.mult)

        NK = 12
        pc = [sc["cs"]] + [P.sb("s5_pc", [128, 32], F32) for k_ in range(1, NK)]
        pn = [sc["sn"]] + [P.sb("s5_pn", [128, 32], F32) for k_ in range(1, NK)]
        nn = [P.sb("s5_nn", [128, 32], F32) for k_ in range(NK)]
        for k_ in range(NK):
            if k_ > 0:
                a_, b_ = pc[k_ - 1], pn[k_ - 1]
                P.op(dve, lambda e: e.tensor_tensor(out=sc["t1"][:], in0=a_[:], in1=a_[:], op=ALU.mult), [a_], [sc["t1"]])
                P.op(dve, lambda e: e.tensor_tensor(out=sc["t2"][:], in0=b_[:], in1=b_[:], op=ALU.mult), [b_], [sc["t2"]])
                P.op(dve, lambda e: e.tensor_tensor(out=pc[k_][:], in0=sc["t1"][:], in1=sc["t2"][:], op=ALU.subtract),
                     [sc["t1"], sc["t2"]], [pc[k_]])
                P.op(dve, lambda e: e.tensor_tensor(out=sc["t1"][:], in0=a_[:], in1=b_[:], op=ALU.mult), [a_, b_], [sc["t1"]])
                P.op(dve, lambda e: e.tensor_scalar(out=pn[k_][:], in0=sc["t1"][:], scalar1=2.0, scalar2=None,
                                                    op0=ALU.mult), [sc["t1"]], [pn[k_]])
            P.op(dve, lambda e: e.tensor_scalar(out=nn[k_][:], in0=pn[k_][:], scalar1=-1.0, scalar2=None,
                                                op0=ALU.mult), [pn[k_]], [nn[k_]])
        par_ = [None, sc["are"]] + [P.sb("s5_par", [128, 32], F32) for k_ in range(2, 9)]
        pai_ = [None, sc["aim"]] + [P.sb("s5_pai", [128, 32], F32) for k_ in range(2, 9)]
        nai_ = [None] + [P.sb("s5_nai", [128, 32], F32) for k_ in range(1, 9)]
        for k_ in range(2, 9):
            a_, b_ = par_[k_ - 1], pai_[k_ - 1]
            P.op(dve, lambda e: e.tensor_tensor(out=sc["t1"][:], in0=a_[:], in1=sc["are"][:], op=ALU.mult), [a_, sc["are"]], [sc["t1"]])
            P.op(dve, lambda e: e.tensor_tensor(out=sc["t2"][:], in0=b_[:], in1=sc["aim"][:], op=ALU.mult), [b_, sc["aim"]], [sc["t2"]])
            P.op(dve, lambda e: e.tensor_tensor(out=par_[k_][:], in0=sc["t1"][:], in1=sc["t2"][:], op=ALU.subtract),
                 [sc["t1"], sc["t2"]], [par_[k_]])
            P.op(dve, lambda e: e.tensor_tensor(out=sc["t1"][:], in0=a_[:], in1=sc["aim"][:], op=ALU.mult), [a_, sc["aim"]], [sc["t1"]])
            P.op(dve, lambda e: e.tensor_tensor(out=sc["t2"][:], in0=b_[:], in1=sc["are"][:], op=ALU.mult), [b_, sc["are"]], [sc["t2"]])
            P.op(dve, lambda e: e.tensor_tensor(out=pai_[k_][:], in0=sc["t1"][:], in1=sc["t2"][:], op=ALU.add),
                 [sc["t1"], sc["t2"]], [pai_[k_]])
        for k_ in range(1, 9):
            P.op(dve, lambda e: e.tensor_scalar(out=nai_[k_][:], in0=pai_[k_][:], scalar1=-1.0, scalar2=None,
                                                op0=ALU.mult), [pai_[k_]], [nai_[k_]])
        rho8 = P.sb("s5_rho8", [128, 32], F32)
        tt("t1", "rho", "rho", ALU.mult)
        tt("t2", "t1", "t1", ALU.mult)
        P.op(dve, lambda e: e.tensor_tensor(out=rho8[:], in0=sc["t2"][:], in1=sc["t2"][:], op=ALU.mult), [sc["t2"]], [rho8])
        nci = P.sb("s5_nci", [128, 32], F32)
        P.op(dve, lambda e: e.tensor_scalar(out=nci[:], in0=sc["ci"][:], scalar1=-1.0, scalar2=None, op0=ALU.mult),
             [sc["ci"]], [nci])

        LY = P.sb("LY", [128, 2, 32, 128], BF16)
        lst = [P.sb("lst", [128, 8, 128], F32) for i in range(2)]
        k = 0
        for ri in range(2):
            for q4 in range(4):
                st_ = lst[k % 2]
                k += 1
                P.dma(st_[:], ly_in.h[ri, q4 * 8:(q4 + 1) * 8].rearrange("a k m -> k a m"),
                      reads=[ly_in], writes=[st_])
                P.op(act, lambda e: e.activation(out=LY[:, ri, q4 * 8:(q4 + 1) * 8, :], in_=st_[:],
                                                 func=AF.Copy, scale=(1.0, -1.0)[ri]), [st_], [LY])

        J = L // 8
        bst = [P.sb("bst", [128, 2, 4, 128], F32)] * 2
        cst = [P.sb("cst", [128, 2, 4, 128], F32)] * 2
        Gr = P.sb("Gr", [128, 128], F32)
        Gi = P.sb("Gi", [128, 128], F32)
        Gt = P.sb("Gt", [128, 128], F32)
        Gb = P.sb("Gb", [128, 4, 8, 2, 128], BF16)
        Bs = P.sb("Bs", [128, 4, 8, 2, 128], BF16)
        CA = P.sb("CA", [128, 4, 8, 2, 128], BF16)
        Kt = P.sb("Kt", [128, 8, 128], BF16)
        dI = P.sb("dI", [128, 128], F32)
        uP = [P.sb("uP", [128, L], BF16) for i in range(2)]
        uPm = P.sb("uPm", [128, 8, L // 8], BF16)
        tabc = [P.sb("tabc", [128, J], F32) for i in range(2)]
        tabs = [P.sb("tabs", [128, J], F32) for i in range(2)]
        mm = P.sb("mm", [128, J], F32)
        tmps = []
        for i_ in range(1):
            tmps.append({n: P.sb("tmp_" + n, [128, J], F32) for n in
                         ("vr", "vi", "a1", "a2", "a3", "a4", "er", "ei", "wr", "wi", "xr", "xi")})
        tmps.append(tmps[0])
        Xb = P.sb("Xb", [128, 4, 2, J], BF16)
        ypk = P.sb("ypk", [128, L], F32)
        g1 = [P.sb("g1", [128, 512], F32) for i in range(2)]
        g2 = [P.sb("g2", [128, 512], F32) for i in range(2)]
        ygo = [P.sb("ygo", [128, 512], BF16) for i in range(2)]
        P.op(pool, lambda e: e.memset(Xb[:, :, :, 0:1], 0.0), [], [Xb])

        def T2(eng, o, a, b, op):
            P.op(eng, lambda e: e.tensor_tensor(out=o[:], in0=a[:], in1=b[:], op=op), [a, b], [o])

        def cmul(o_r, o_i, i_r, i_i, sr, si, nsi, tmp_):
            P.op(dve, lambda e: e.tensor_scalar(out=tmp_[:], in0=i_r, scalar1=sr, scalar2=None, op0=ALU.mult),
                 [], [tmp_])
            P.op(dve, lambda e: e.scalar_tensor_tensor(out=o_r, in0=i_i, scalar=nsi, in1=tmp_[:],
                                                       op0=ALU.mult, op1=ALU.add), [tmp_], [])
            P.op(dve, lambda e: e.tensor_scalar(out=tmp_[:], in0=i_i, scalar1=sr, scalar2=None, op0=ALU.mult),
                 [], [tmp_])
            P.op(dve, lambda e: e.scalar_tensor_tensor(out=o_i, in0=i_r, scalar=si, in1=tmp_[:],
                                                       op0=ALU.mult, op1=ALU.add), [tmp_], [])

        def s_load1(pk):
            up_ = uP[pk % 2]
            bs_, cs2_ = bst[pk % 2], cst[pk % 2]
            upv = uPm
            P.dma(up_[:], ut[pk * 128:(pk + 1) * 128, :], reads=[ut], writes=[up_])
            for ri in range(2):
                P.dma(bs_[:, ri, :, :], lvt_in.h[ri, pk * 4:(pk + 1) * 4].rearrange("a k m -> k a m"),
                      reads=[lvt_in], writes=[bs_])

        def s_g(pk):
            up_ = uP[pk % 2]
            bs_, cs2_ = bst[pk % 2], cst[pk % 2]
            upv = uPm
            for p4 in range(4):
                gp = pk * 4 + p4
                g_ = slice(gp, gp + 1)
                P.op(dve, lambda e: e.tensor_scalar(out=Gt[:], in0=bs_[:, 0, p4, :], scalar1=sc["cr"][:, g_],
                                                    scalar2=None, op0=ALU.mult), [bs_, sc["cr"]], [Gt])
                P.op(dve, lambda e: e.scalar_tensor_tensor(out=Gr[:], in0=bs_[:, 1, p4, :], scalar=nci[:, g_],
                                                           in1=Gt[:], op0=ALU.mult, op1=ALU.add), [bs_, nci, Gt], [Gr])
                P.op(dve, lambda e: e.tensor_scalar(out=Gt[:], in0=bs_[:, 1, p4, :], scalar1=sc["cr"][:, g_],
                                                    scalar2=None, op0=ALU.mult), [bs_, sc["cr"]], [Gt])
                P.op(dve, lambda e: e.scalar_tensor_tensor(out=Gi[:], in0=bs_[:, 0, p4, :], scalar=sc["ci"][:, g_],
                                                           in1=Gt[:], op0=ALU.mult, op1=ALU.add), [bs_, sc["ci"], Gt], [Gi])
                for tau in range(8):
                    if tau > 0:
                        P.op(dve, lambda e: e.tensor_scalar(out=Gt[:], in0=Gr[:], scalar1=sc["aim"][:, g_],
                                                            scalar2=None, op0=ALU.mult), [Gr, sc["aim"]], [Gt])
                        P.op(dve, lambda e: e.tensor_scalar(out=Gr[:], in0=Gr[:], scalar1=sc["are"][:, g_],
                                                            scalar2=None, op0=ALU.mult), [Gr, sc["are"]], [Gr])
                        P.op(dve, lambda e: e.scalar_tensor_tensor(out=Gr[:], in0=Gi[:], scalar=nai_[1][:, g_],
                                                                   in1=Gr[:], op0=ALU.mult, op1=ALU.add),
                             [Gi, nai_[1], Gr], [Gr])
                        P.op(dve, lambda e: e.scalar_tensor_tensor(out=Gi[:], in0=Gi[:], scalar=sc["are"][:, g_],
                                                                   in1=Gt[:], op0=ALU.mult, op1=ALU.add),
                             [Gi, sc["are"], Gt], [Gi])
                    for ri, G_ in ((0, Gr), (1, Gi)):
                        P.op(act, lambda e: e.activation(out=Gb[:, p4, tau, ri, :], in_=G_[:], func=AF.Copy),
                             [G_], [Gb])
                        tb_ = PS[6 + (ri % 2)]
                        P.op(pe, lambda e: e.transpose(out=tb_[:, 0:128], in_=G_[:], identity=ident_f[:]),
                             [G_, ident_f], [tb_])
                        P.op(act, lambda e: e.activation(out=Bs[:, p4, tau, ri, :], in_=tb_[:, 0:128], func=AF.Copy),
                             [tb_], [Bs])

        def s_load2(pk):
            up_ = uP[pk % 2]
            bs_, cs2_ = bst[pk % 2], cst[pk % 2]
            upv = uPm
            for ri in range(2):
                P.dma(cs2_[:, ri, :, :], ly_in.h[ri, pk * 4:(pk + 1) * 4].rearrange("a k m -> k a m"),
                      reads=[ly_in], writes=[cs2_])
            P.op(dve, lambda e: e.tensor_scalar(out=cs2_[:, 1, :, :], in0=cs2_[:, 1, :, :], scalar1=-1.0, scalar2=None,
                                                op0=ALU.mult), [cs2_], [cs2_])
            upv0 = up_[:].rearrange("p (j s) -> p s j", s=8)
            P.op(act, lambda e: e.activation(out=uPm[:], in_=upv0, func=AF.Copy), [up_], [uPm])

        def s_ca(pk):
            up_ = uP[pk % 2]
            bs_, cs2_ = bst[pk % 2], cst[pk % 2]
            upv = uPm
            for p4 in range(4):
                gp = pk * 4 + p4
                g_ = slice(gp, gp + 1)
                for tl in range(8):
                    ar_, ai_, nai2 = par_[tl + 1][:, g_], pai_[tl + 1][:, g_], nai_[tl + 1][:, g_]
                    P.op(dve, lambda e: e.tensor_scalar(out=Gt[:], in0=cs2_[:, 0, p4, :], scalar1=ar_, scalar2=None,
                                                        op0=ALU.mult), [cs2_, par_[tl + 1]], [Gt])
                    P.op(dve, lambda e: e.scalar_tensor_tensor(out=CA[:, p4, tl, 0, :], in0=cs2_[:, 1, p4, :], scalar=ai_,
                                                               in1=Gt[:], op0=ALU.mult, op1=ALU.add),
                         [cs2_, pai_[tl + 1], Gt], [CA])
                    P.op(dve, lambda e: e.tensor_scalar(out=Gt[:], in0=cs2_[:, 0, p4, :], scalar1=nai2, scalar2=None,
                                                        op0=ALU.mult), [cs2_, nai_[tl + 1]], [Gt])
                    P.op(dve, lambda e: e.scalar_tensor_tensor(out=CA[:, p4, tl, 1, :], in0=cs2_[:, 1, p4, :], scalar=ar_,
                                                               in1=Gt[:], op0=ALU.mult, op1=ALU.add),
                         [cs2_, par_[tl + 1], Gt], [CA])

        def s_k(pk):
            up_ = uP[pk % 2]
            bs_, cs2_ = bst[pk % 2], cst[pk % 2]
            upv = uPm
            P.op(dve, lambda e: e.tensor_scalar(out=dI[:], in0=ident_f[:], scalar1=dsk[:, pk:pk + 1], scalar2=None,
                                                op0=ALU.mult), [ident_f, dsk], [dI])
            for tau in range(8):
                kb_ = PS[6 + (tau % 2)]
                for p4 in range(4):
                    gp = pk * 4 + p4
                    P.op(pe, lambda e: e.matmul(kb_[:, 0:128], lhsT=Gb[:, p4, tau, 0, :], rhs=LY[:, 0, gp, :],
                                                start=(p4 == 0), stop=False), [Gb, LY], [kb_])
                    P.op(pe, lambda e: e.matmul(kb_[:, 0:128], lhsT=Gb[:, p4, tau, 1, :], rhs=LY[:, 1, gp, :],
                                                start=False, stop=(p4 == 3)), [Gb, LY], [kb_])
                if tau == 0:
                    P.op(dve, lambda e: e.tensor_tensor(out=Kt[:, 0, :], in0=kb_[:, 0:128], in1=dI[:], op=ALU.add),
                         [kb_, dI], [Kt])
                else:
                    P.op(act, lambda e: e.activation(out=Kt[:, tau, :], in_=kb_[:, 0:128], func=AF.Copy), [kb_], [Kt])

        def s_main(pk):
            up_ = uP[pk % 2]
            bs_, cs2_ = bst[pk % 2], cst[pk % 2]
            upv = uPm
            def emit_e(p4_):
                for ri_ in range(2):
                    pb_ = PS[2 * (p4_ % 2) + ri_]
                    for s_ in range(8):
                        P.op(pe, lambda e: e.matmul(pb_[:], lhsT=Bs[:, p4_, 7 - s_, ri_, :], rhs=upv[:, s_, :],
                                                    start=(s_ == 0), stop=(s_ == 7)), [Bs, uPm], [pb_])

            def st_tab(p4_):
                gp_ = pk * 4 + p4_
                g2_ = slice(gp_, gp_ + 1)
                tc_, ts_ = tabc[p4_ % 2], tabs[p4_ % 2]
                P.op(dve, lambda e: e.tensor_copy(out=tc_[:, 0:1], in_=pc[3][:, g2_]), [pc[3]], [tc_])
                P.op(dve, lambda e: e.tensor_copy(out=ts_[:, 0:1], in_=pn[3][:, g2_]), [pn[3]], [ts_])
                for k_ in range(9):
                    n_ = 1 << k_
                    lo = slice(0, n_)
                    hi = slice(n_, 2 * n_)
                    kk = k_ + 3
                    P.op(dve, lambda e: e.tensor_scalar(out=mm[:, lo], in0=tc_[:, lo], scalar1=pc[kk][:, g2_],
                                                        scalar2=None, op0=ALU.mult), [tc_, pc[kk]], [mm])
                    P.op(dve, lambda e: e.scalar_tensor_tensor(out=tc_[:, hi], in0=ts_[:, lo], scalar=nn[kk][:, g2_],
                                                               in1=mm[:, lo], op0=ALU.mult, op1=ALU.add),
                         [ts_, nn[kk], mm], [tc_])
                    P.op(dve, lambda e: e.tensor_scalar(out=mm[:, lo], in0=ts_[:, lo], scalar1=pc[kk][:, g2_],
                                                        scalar2=None, op0=ALU.mult), [ts_, pc[kk]], [mm])
                    P.op(dve, lambda e: e.scalar_tensor_tensor(out=ts_[:, hi], in0=tc_[:, lo], scalar=pn[kk][:, g2_],
                                                               in1=mm[:, lo], op0=ALU.mult, op1=ALU.add),
                         [tc_, pn[kk], mm], [ts_])

            emit_e(0)
            for p4 in range(4):
                gp = pk * 4 + p4
                if p4 + 1 < 4:
                    emit_e(p4 + 1)
                st_tab(p4)
                tmp = tmps[p4 % 2]
                psa, psb = PS[2 * (p4 % 2)], PS[2 * (p4 % 2) + 1]
                tc_, ts_ = tabc[p4 % 2], tabs[p4 % 2]
                P.op(act, lambda e: e.activation(out=tmp["vr"][:], in_=psa[:], func=AF.Copy), [psa], [tmp["vr"]])
                P.op(act, lambda e: e.activation(out=tmp["vi"][:], in_=psb[:], func=AF.Copy), [psb], [tmp["vi"]])
                T2(dve, tmp["a1"], tc_, tmp["vr"], ALU.mult)
                T2(dve, tmp["a2"], ts_, tmp["vi"], ALU.mult)
                T2(dve, tmp["a3"], tc_, tmp["vi"], ALU.mult)
                T2(dve, tmp["a4"], ts_, tmp["vr"], ALU.mult)
                T2(dve, tmp["er"], tmp["a1"], tmp["a2"], ALU.add)
                T2(dve, tmp["ei"], tmp["a3"], tmp["a4"], ALU.subtract)
                rhoc = rho8[:, gp:gp + 1].to_broadcast([128, J])
                for (w_, e_) in ((tmp["wr"], tmp["er"]), (tmp["wi"], tmp["ei"])):
                    P.op(dve, lambda e: e.tensor_tensor_scan(out=w_[:], data0=rhoc, data1=e_[:], initial=0.0,
                                                             op0=ALU.mult, op1=ALU.add), [rho8, e_], [w_])
                T2(dve, tmp["a1"], tc_, tmp["wr"], ALU.mult)
                T2(dve, tmp["a2"], ts_, tmp["wi"], ALU.mult)
                T2(dve, tmp["a3"], ts_, tmp["wr"], ALU.mult)
                T2(dve, tmp["a4"], tc_, tmp["wi"], ALU.mult)
                T2(dve, tmp["xr"], tmp["a1"], tmp["a2"], ALU.subtract)
                T2(dve, tmp["xi"], tmp["a3"], tmp["a4"], ALU.add)
                P.op(act, lambda e: e.activation(out=Xb[:, p4, 0, 1:J], in_=tmp["xr"][:, 0:J - 1], func=AF.Copy),
                     [tmp["xr"]], [Xb])
                P.op(act, lambda e: e.activation(out=Xb[:, p4, 1, 1:J], in_=tmp["xi"][:, 0:J - 1], func=AF.Copy),
                     [tmp["xi"]], [Xb])

        def s_y(pk):
            up_ = uP[pk % 2]
            bs_, cs2_ = bst[pk % 2], cst[pk % 2]
            upv = uPm
            ypv = ypk[:].rearrange("p (j s) -> p s j", s=8)
            for tl in range(8):
                yb_ = PS[4 + (tl % 2)]
                first = True
                for p4 in range(4):
                    for ri in range(2):
                        P.op(pe, lambda e: e.matmul(yb_[:], lhsT=CA[:, p4, tl, ri, :], rhs=Xb[:, p4, ri, :],
                                                    start=first, stop=False), [CA, Xb], [yb_])
                        first = False
                for s_ in range(tl + 1):
                    P.op(pe, lambda e: e.matmul(yb_[:], lhsT=Kt[:, tl - s_, :], rhs=upv[:, s_, :],
                                                start=False, stop=(s_ == tl)), [Kt, uPm], [yb_])
                P.op(act, lambda e: e.activation(out=ypv[:, tl, :], in_=yb_[:], func=AF.Copy), [yb_], [ypk])

        def s_gelu(pk):
            for cch in range(8):
                cs_ = slice(cch * 512, (cch + 1) * 512)
                g1_, g2_, yo_ = g1[cch % 2], g2[cch % 2], ygo[cch % 2]
                P.op(act, lambda e: e.activation(out=g1_[:], in_=ypk[:, cs_], func=AF.Square), [ypk], [g1_])
                P.op(dve, lambda e: e.tensor_scalar(out=g1_[:], in0=g1_[:], scalar1=0.044715, scalar2=1.0,
                                                    op0=ALU.mult, op1=ALU.add), [g1_], [g1_])
                P.op(dve, lambda e: e.tensor_tensor(out=g2_[:], in0=g1_[:], in1=ypk[:, cs_], op=ALU.mult),
                     [g1_, ypk], [g2_])
                P.op(act, lambda e: e.activation(out=g2_[:], in_=g2_[:], func=AF.Sigmoid,
                                                 scale=1.5957691216057308), [g2_], [g2_])
                P.op(dve, lambda e: e.tensor_tensor(out=yo_[:], in0=g2_[:], in1=ypk[:, cs_], op=ALU.mult),
                     [g2_, ypk], [yo_])
                P.dma(ygd[pk * 128:(pk + 1) * 128, cs_], yo_[:], reads=[yo_], writes=[ygd], q=act)

        s_load1(0)
        s_g(0)
        s_load2(0)
        s_ca(0)
        s_k(0)
        for pk in range(8):
            s_main(pk)
            s_y(pk)
            if pk + 1 < 8:
                s_load1(pk + 1)
                s_g(pk + 1)
            s_gelu(pk)
            if pk + 1 < 8:
                s_load2(pk + 1)
                s_ca(pk + 1)
                s_k(pk + 1)
        P.phase_end()
        lst = [P.sb("lst2", [128, 8, 128], F32) for i in range(2)]
        wg = P.sb("wg", [128, 8, 1024], BF16)
        bgl = P.sb("bgl", [128, 8], F32)
        P.dma(bgl[:], bglu_in[:], reads=[bglu_in], writes=[bgl])
        wgv = wglu_in.h.rearrange("(c p) n -> p c n", p=128)
        for c in range(8):
            st_ = lst[c % 2]
            P.dma(st_[:].rearrange("p a b -> p (a b)"), wgv[:, c, :], reads=[wglu_in], writes=[st_])
            P.op(pool, lambda e: e.tensor_copy(out=wg[:, c, :], in_=st_[:].rearrange("p a b -> p (a b)")),
                 [st_], [wg])
        zt = [P.sb("zt", [128, T], BF16) for i in range(2)]
        gt = [P.sb("gt", [128, T], BF16) for i in range(2)]
        ygt = [P.sb("ygt", [128, 8, T], BF16) for i in range(2)]
        ygv = ygd.h.rearrange("(c p) t -> p c t", p=128)
        k = 0
        for ti in range(NT):
            t0 = ti * T
            yg_ = ygt[ti % 2]
            P.dma(yg_[:], ygv[:, :, t0:t0 + T], reads=[ygd], writes=[yg_])
            for nb in range(8):
                s_ = k % 2
                k += 1
                a = PS[k % 4]
                P.dma(zt[s_][:], zb[nb * 128:(nb + 1) * 128, t0:t0 + T], reads=[zb], writes=[zt[s_]])
                for jc in range(8):
                    P.op(pe, lambda e: e.matmul(a[:], lhsT=wg[:, jc, nb * 128:(nb + 1) * 128],
                                                rhs=yg_[:, jc, :], start=(jc == 0), stop=(jc == 7)),
                         [wg, yg_], [a])
                P.op(act, lambda e: e.activation(out=gt[s_][:], in_=a[:], func=AF.Sigmoid,
                                                 bias=bgl[:, nb:nb + 1]), [a, bgl], [gt[s_]])
                P.op(dve, lambda e: e.tensor_tensor(out=gt[s_][:], in0=gt[s_][:], in1=yg_[:, nb, :],
                                                    op=ALU.mult), [gt[s_], yg_], [gt[s_]])
                P.op(dve, lambda e: e.tensor_tensor(out=gt[s_][:], in0=gt[s_][:], in1=zt[s_][:],
                                                    op=ALU.mult), [gt[s_], zt[s_]], [gt[s_]])
                P.dma(mixT[1024 + nb * 128:1024 + (nb + 1) * 128, t0:t0 + T], gt[s_][:], reads=[gt[s_]],
                      writes=[mixT], q=act)
        P.phase_end()

    def back(w_out, resid, dst, final):
        P.phase_begin()
        wo = P.sb("wo", [128, NCH, D], BF16)
        wst = [P.sb("wst", [128, D], F32) for i in range(2)]
        mt = [P.sb("mt", [128, NCH, 512], BF16) for i in range(2)]
        xr = [P.sb("xr", [128, D], F32) for i in range(2)]
        xo = [P.sb("xo", [128, D], F32) for i in range(2)]
        wov = w_out.h.rearrange("(c p) n -> p c n", p=128)
        for c in range(NCH):
            P.dma(wst[c % 2][:], wov[:, c, :], reads=[w_out], writes=[wst[c % 2]])
            P.op(pool, lambda e: e.tensor_copy(out=wo[:, c, :], in_=wst[c % 2][:]), [wst[c % 2]], [wo])
        if final:
            gB = P.sb("gB", [128, D], F32)
            junk = P.sb("junk", [128, D], BF16)
            ss = [P.sb("ss", [128, 1], F32) for i in range(2)]
            rs = [P.sb("rs", [128, 1], F32) for i in range(2)]
            P.dma(gB[:], gf_in[0:1, :].partition_broadcast(128), reads=[gf_in], writes=[gB])
        mv = mixT.h.rearrange("(c p) t -> p c t", p=128)
        for tb in range(L // 128):
            s = tb % 2
            t0 = tb * 128
            ms = (tb // 4) % 2
            mo = (tb % 4) * 128
            if tb % 4 == 0:
                P.dma(mt[ms][:], mv[:, :, t0:t0 + 512], reads=[mixT], writes=[mt[ms]])
            P.dma(xr[s][:], resid[t0:t0 + 128, :], reads=[resid], writes=[xr[s]])
            for ng in range(4):
                a = PS[(tb % 2) * 4 + ng]
                for c in range(NCH):
                    P.op(pe, lambda e: e.matmul(a[:], lhsT=mt[ms][:, c, mo:mo + 128],
                                                rhs=wo[:, c, ng * 512:(ng + 1) * 512],
                                                start=(c == 0), stop=(c == NCH - 1)), [mt[ms], wo], [a])
                P.op(dve, lambda e: e.tensor_tensor(out=xo[s][:, ng * 512:(ng + 1) * 512], in0=a[:],
                                                    in1=xr[s][:, ng * 512:(ng + 1) * 512], op=ALU.add),
                     [a, xr[s]], [xo[s]])
            if final:
                P.op(act, lambda e: e.activation(out=junk[:], in_=xo[s][:], func=AF.Square,
                                                 accum_out=ss[s][:]), [xo[s]], [junk, ss[s]])
                P.op(dve, lambda e: e.tensor_scalar(out=rs[s][:], in0=ss[s][:], scalar1=1.0 / D,
                                                    scalar2=EPS, op0=ALU.mult, op1=ALU.add),
                     [ss[s]], [rs[s]])
                P.op(act, lambda e: e.activation(out=rs[s][:], in_=rs[s][:], func=AF.Sqrt),
                     [rs[s]], [rs[s]])
                P.op(dve, lambda e: e.reciprocal(out=rs[s][:], in_=rs[s][:]), [rs[s]], [rs[s]])
                P.op(dve, lambda e: e.scalar_tensor_tensor(out=xo[s][:], in0=xo[s][:],
                                                            scalar=rs[s][:, 0:1], in1=gB[:],
                                                            op0=ALU.mult, op1=ALU.mult),
                     [xo[s], rs[s], gB], [xo[s]])
            t = P.dma(dst[t0:t0 + 128, :], xo[s][:], reads=[xo[s]], writes=[dst], q=act)
            if final:
                P.out_toks.append(t)
        P.phase_end()

    if allp or "front0" in phases:
        front(x_in, w0_in, g0_in, [
            (0, 1024, qa, "fm", ("scale", 0.125)),
            (1024, 1024, ka, "fm", ("copy",)),
            (2048, 1024, va, "tm", ("copy",)),
            (3072, 1024, ut, "fm", ("copy",)),
            (4096, 1024, za, "fm", ("silu",)),
            (5120, 1024, zb, "fm", ("silu",)),
        ])
    if allp or "attn_a" in phases:
        attn_a()
    if allp or "s5" in phases:
        s5_glu()
    if allp or "back0" in phases:
        back(wo0_in, x_in, x1, False)
    if allp or "front1" in phases:
        sc = 1.0 / math.sqrt(128.0)
        front(x1, w1_in, g1_in, [
            (0, 2048, qc, "fm", ("scale", sc)),
            (2048, 2048, kc, "fm", ("copy",)),
            (4096, 2048, vc, "tm", ("copy",)),
            (6144, 2048, zc, "fm", ("silu",)),
        ], fcol=8192)
    if allp or "fox" in phases:
        fox()
    if allp or "back1" in phases:
        back(wo1_in, x1, out_d, True)

    name2buf = dict(qa=qa, ka=ka, va=va, ut=ut, za=za, zb=zb, mixT=mixT, ygd=ygd)
    if dbg:
        P.phase_begin()
        stage = P.sb("dbg_stage", [128, 4096], BF16)
        for nm in dbg:
            src = name2buf[nm]
            shp = src.h.shape
            dd = P.dram("dbg_" + nm, list(shp), BF16, kind="ExternalOutput")
            for r in range(0, shp[0], 128):
                for c in range(0, shp[1], 4096):
                    w = min(4096, shp[1] - c)
                    P.dma(stage[:, 0:w], src[r:r + 128, c:c + w], reads=[src], writes=[stage])
                    t = P.dma(dd[r:r + 128, c:c + w], stage[:, 0:w], reads=[stage], writes=[dd])
                    P.out_toks.append(t)
        P.phase_end()
    for t in P.out_toks:
        sp.wait(t)
    return P


def host_inputs(inp, b):
    f = np.float32
    A = lambda a: np.ascontiguousarray(a, dtype=f)
    m = {}
    m["x"] = A(inp["x"][b])
    m["ident"] = np.eye(128, dtype=f)
    sel = np.zeros((128, 128), f); sel[127, :] = 1
    m["sel127"] = sel
    m["tri"] = np.triu(np.ones((128, 128), f))
    o96 = np.zeros((96, 128), f); o96[[0, 32, 64], :] = 1
    m["ones96"] = o96
    m["g0"] = A(inp["norm_even_g"][0].reshape(16, 128).T)
    m["w0"] = A(inp["w_in_even"][0])
    m["g1"] = A(inp["norm_odd_g"][0].reshape(16, 128).T)
    m["w1"] = A(inp["w_in_odd"][0])
    m["bf"] = A(inp["b_forget"][0].reshape(16, 1))
    m["wo0"] = A(inp["w_out_even"][0])
    m["wo1"] = A(inp["w_out_odd"][0])
    m["gf"] = A(inp["final_norm_g"].reshape(1, 2048))
    rb = np.asarray(inp["rel_bias"][0])
    kl = np.arange(128)[:, None]; ql = np.arange(128)[None, :]
    ab = np.zeros((16, 3, 128, 128), f); am = np.zeros((3, 128, 128), f)
    for i, r in enumerate((0, 3, 4)):
        rel = (ql - kl) + 128 * (4 - r)
        idx = np.clip(rel, -128, 128) + 128
        ab[:, i] = rb[:, idx]
    am[0][(kl < 64) & (ql >= 64)] = -30000.0
    am[2][(kl >= 64) & (ql < 64)] = -30000.0
    m["abias"] = ab; m["amask"] = am
    m["abc"] = A(np.broadcast_to(rb[:, 256][None, :], (128, 16)))
    bre = np.asarray(inp["s5_b_re"][0]); bim = np.asarray(inp["s5_b_im"][0])
    cre = np.asarray(inp["s5_c_re"][0]); cim = np.asarray(inp["s5_c_im"][0])
    lv = np.zeros((2, 32, 128, 128), f); ly = np.zeros((2, 32, 128, 128), f); lvt = np.zeros((2, 32, 128, 128), f)
    slr = np.zeros((128, 32), f); sli = np.zeros((128, 32), f); sldt = np.zeros((128, 32), f)
    dsk = np.zeros((128, 8), f)
    lre = np.asarray(inp["s5_lambda_re"][0]); lim = np.asarray(inp["s5_lambda_im"][0])
    ldt = np.asarray(inp["s5_log_dt"][0]); dsp = np.asarray(inp["s5_d"][0])
    for g in range(64):
        gp = g // 2; slot = g % 2; gl = g % 8
        rs = slice(16 * gl, 16 * gl + 16); ps_ = slice(64 * slot, 64 * slot + 64)
        lv[0, gp, rs, ps_] = bre[g].T
        lv[1, gp, rs, ps_] = bim[g].T
        lvt[0, gp, ps_, rs] = bre[g]
        lvt[1, gp, ps_, rs] = bim[g]
        ly[0, gp, ps_, rs] = cre[g].T
        ly[1, gp, ps_, rs] = cim[g].T
        slr[ps_, gp] = lre[g]; sli[ps_, gp] = lim[g]; sldt[ps_, gp] = ldt[g]
        dsk[rs, g // 8] = dsp[g]
    m.update(lv=lv, ly=ly, lvt=lvt, slr=slr, sli=sli, sldt=sldt, dsk=dsk)
    m["iota1"] = A(np.broadcast_to(np.arange(1, 513, dtype=f)[None, :], (128, 512)))
    m["wglu"] = A(inp["w_glu"][0])
    m["bglu"] = A(inp["b_glu"][0].reshape(8, 128).T)
    return m


_PROG = None


def kernel(**inputs):
    global _PROG
    inp = {k: np.asarray(v) for k, v in inputs.items()}
    if _PROG is None:
        _PROG = build()
    nb = inp["x"].shape[0]
    in_maps = [host_inputs(inp, b) for b in range(nb)]
    res = run_bass_kernel_spmd(_PROG.nc, in_maps, core_ids=list(range(nb)))
    return np.stack([np.asarray(r["out"]) for r in res.results], axis=0).astype(np.float32)
```

```python
import math
from contextlib import ExitStack
import numpy as np
import concourse.bass as bass
import concourse.mybir as mybir
from concourse.bass_utils import run_bass_kernel_spmd

F32 = mybir.dt.float32
BF16 = mybir.dt.bfloat16
AF = mybir.ActivationFunctionType
ALU = mybir.AluOpType

D = 2048
L = 4096
NCH = D // 128
EPS = 1e-6
SAME_ENG_SYNC = True
DMA_RING = 12


class Buf:
    def __init__(self, name, h):
        self.name = name
        self.h = h
        self.last_write = None
        self.reads = {}

    def __getitem__(self, idx):
        return self.h[idx]


class Eng:
    def __init__(self, nc, handle, name, is_pe=False):
        self.nc = nc
        self.h = handle
        self.name = name
        self.sem = nc.alloc_semaphore("sem_" + name)
        self.cnt = 0
        self.waited = {}
        self.is_pe = is_pe
        self.ring = None
        self.dma_i = 0

    def wait(self, tok):
        if tok is None:
            return
        sem, val = tok
        if self.waited.get(sem.num, 0) >= val:
            return
        self.h.wait_ge(sem, val)
        self.waited[sem.num] = val


class Prog:
    def __init__(self):
        nc = bass.Bass("TRN2", target_bir_lowering=False)
        self.nc = nc
        self.pe = Eng(nc, nc.tensor, "pe", is_pe=True)
        self.act = Eng(nc, nc.scalar, "act")
        self.dve = Eng(nc, nc.vector, "dve")
        self.pool = Eng(nc, nc.gpsimd, "pool")
        self.sp = Eng(nc, nc.sync, "sp")
        self.nbuf = 0
        self.stack = []
        self.out_toks = []

    def sb(self, name, shape, dt):
        self.nbuf += 1
        nm = "%s_%d" % (name, self.nbuf)
        if self.stack:
            h = self.stack[-1].enter_context(self.nc.sbuf_tensor(nm, list(shape), dt))
            return Buf(name, h)
        return Buf(name, self.nc.alloc_sbuf_tensor(nm, list(shape), dt))

    def phase_begin(self):
        self.stack.append(ExitStack())

    def phase_end(self):
        self.barrier()
        self.stack.pop().close()

    def barrier(self):
        engs = [self.pe, self.act, self.dve, self.pool, self.sp]
        toks = [(e.sem, e.cnt) for e in engs if e.cnt > 0]
        for q in engs:
            if q.ring is not None:
                for i in range(max(0, q.dma_i - DMA_RING), q.dma_i):
                    toks.append((q.ring[i % DMA_RING], 16 * (i // DMA_RING + 1)))
        for e in engs:
            for t in toks:
                if t[0].num != e.sem.num:
                    e.wait(t)

    def ps(self, name, shape, dt=F32):
        return Buf(name, self.nc.alloc_psum_tensor(name, list(shape), dt))

    def dram(self, name, shape, dt, kind="Internal"):
        t = self.nc.dram_tensor(name, list(shape), dt, kind=kind)
        return Buf(name, t.ap())

    def _deps(self, eng, reads, writes):
        toks = []
        for b in reads:
            if b.last_write is not None:
                toks.append(b.last_write)
        for b in writes:
            if b.last_write is not None:
                toks.append(b.last_write)
            toks.extend(b.reads.values())
        for t in toks:
            if t[0].num == eng.sem.num and (eng.is_pe or not SAME_ENG_SYNC):
                continue
            eng.wait(t)

    def _mark(self, tok, reads, writes):
        for b in reads:
            b.reads[tok[0].num] = tok
        for b in writes:
            b.last_write = tok
            b.reads = {}

    def op(self, eng, fn, reads=(), writes=()):
        self._deps(eng, reads, writes)
        ins = fn(eng.h)
        eng.cnt += 1
        ins.then_inc(eng.sem, 1)
        tok = (eng.sem, eng.cnt)
        self._mark(tok, reads, writes)
        return tok

    def dma(self, out, in_, reads=(), writes=(), q=None, **kw):
        q = q or self.sp
        if q.ring is None:
            q.ring = [self.nc.alloc_semaphore("dq_%s_%d" % (q.name, i)) for i in range(DMA_RING)]
        i = q.dma_i
        q.dma_i += 1
        sem = q.ring[i % DMA_RING]
        val = 16 * (i // DMA_RING + 1)
        if i >= DMA_RING:
            q.wait((sem, val - 16))
        self._deps(q, reads, writes)
        q.h.dma_start(out=out, in_=in_, **kw).then_inc(sem, 16)
        tok = (sem, val)
        self._mark(tok, reads, writes)
        return tok


class PSView:
    pass


def build(phases=("all",), dbg=(), x1_is_input=False):
    P = Prog()
    nc = P.nc
    pe, act, dve, pool, sp = P.pe, P.act, P.dve, P.pool, P.sp
    allp = "all" in phases

    def ein(name, shape):
        return P.dram(name, shape, F32, kind="ExternalInput")

    x_in = ein("x", [L, D])
    ident_in = ein("ident", [128, 128])
    sel_in = ein("sel127", [128, 128])
    tri_in = ein("tri", [128, 128])
    ones96_in = ein("ones96", [96, 128])
    g0_in = ein("g0", [128, NCH])
    w0_in = ein("w0", [D, 6144])
    g1_in = ein("g1", [128, NCH])
    w1_in = ein("w1", [D, 8208])
    bf_in = ein("bf", [16, 1])
    wo0_in = ein("wo0", [D, D])
    wo1_in = ein("wo1", [D, D])
    gf_in = ein("gf", [1, D])
    abias_in = ein("abias", [16, 3, 128, 128])
    amask_in = ein("amask", [3, 128, 128])
    abc_in = ein("abc", [128, 16])
    lv_in = ein("lv", [2, 32, 128, 128])
    ly_in = ein("ly", [2, 32, 128, 128])
    lvt_in = ein("lvt", [2, 32, 128, 128])
    slr_in = ein("slr", [128, 32])
    sli_in = ein("sli", [128, 32])
    sldt_in = ein("sldt", [128, 32])
    dsk_in = ein("dsk", [128, 8])
    iota1_in = ein("iota1", [128, 512])
    wglu_in = ein("wglu", [1024, 1024])
    bglu_in = ein("bglu", [128, 8])
    out_d = P.dram("out", [L, D], F32, kind="ExternalOutput")

    x1 = ein("x1", [L, D]) if x1_is_input else P.dram("x1", [L, D], F32)
    mixT = P.dram("mixT", [D, L], BF16)
    qa = P.dram("qa", [1024, L], BF16)
    ka = P.dram("ka", [1024, L], BF16)
    va = P.dram("va", [L, 1024], BF16)
    ut = P.dram("ut", [1024, L], BF16)
    za = P.dram("za", [1024, L], BF16)
    zb = P.dram("zb", [1024, L], BF16)
    ygd = P.dram("ygd", [1024, L], BF16)
    qc = P.dram("qc", [D, L], BF16)
    kc = P.dram("kc", [D, L], BF16)
    vc = P.dram("vc", [L, D], BF16)
    zc = P.dram("zc", [D, L], BF16)

    ident_f = P.sb("ident_f", [128, 128], F32)
    ident_b = P.sb("ident_b", [128, 128], BF16)
    sel_f = P.sb("sel_f", [128, 128], F32)
    tri_f = P.sb("tri_f", [128, 128], F32)
    tri_b = P.sb("tri_b", [128, 128], BF16)
    flog = P.sb("flog", [16, L], F32)
    P.dma(ident_f[:], ident_in[:], reads=[ident_in], writes=[ident_f])
    P.dma(sel_f[:], sel_in[:], reads=[sel_in], writes=[sel_f])
    P.dma(tri_f[:], tri_in[:], reads=[tri_in], writes=[tri_f])
    P.op(dve, lambda e: e.tensor_copy(out=ident_b[:], in_=ident_f[:]), [ident_f], [ident_b])
    P.op(dve, lambda e: e.tensor_copy(out=tri_b[:], in_=tri_f[:]), [tri_f], [tri_b])
    PS = [P.ps("bank%d" % i, [128, 512], F32) for i in range(8)]

    def front(src, w_in, g_in, specs, fcol=None):
        P.phase_begin()
        ST = 2048
        xt = [P.sb("xt", [128, D], F32) for i in range(2)]
        xn = [P.sb("xn", [128, D], BF16) for i in range(2)]
        ss = [P.sb("ss", [128, 1], F32) for i in range(2)]
        rstd = [P.sb("rstd", [128, 1], F32) for i in range(2)]
        hnTs = [P.sb("hnT", [128, NCH, ST], BF16) for i in range(2)]
        gfull = P.sb("gfull", [128, NCH, 128], F32)
        gcol = P.sb("gcol", [128, NCH], F32)
        wf = [P.sb("wf", [128, NCH, 128], F32) for i in range(2)]
        wb = [P.sb("wb", [128, NCH, 128], BF16) for i in range(2)]
        ob = [P.sb("ob", [128, 512], BF16) for i in range(4)]
        accb = PS[2:8]
        nacc = [0]
        tpb = [PS[0], PS[1]]

        def nt_pre(t0, i):
            s = i % 2
            P.dma(xt[s][:], src[t0:t0 + 128, :], reads=[src], writes=[xt[s]])
            P.op(act, lambda e: e.activation(out=xn[s][:], in_=xt[s][:], func=AF.Square,
                                             accum_out=ss[s][:]), [xt[s]], [xn[s], ss[s]])
            P.op(dve, lambda e: e.tensor_scalar(out=rstd[s][:], in0=ss[s][:], scalar1=1.0 / D,
                                                scalar2=EPS, op0=ALU.mult, op1=ALU.add),
                 [ss[s]], [rstd[s]])
            P.op(act, lambda e: e.activation(out=rstd[s][:], in_=rstd[s][:], func=AF.Sqrt),
                 [rstd[s]], [rstd[s]])
            P.op(dve, lambda e: e.reciprocal(out=rstd[s][:], in_=rstd[s][:]), [rstd[s]], [rstd[s]])
            P.op(act, lambda e: e.activation(out=xn[s][:], in_=xt[s][:], func=AF.Copy,
                                             scale=rstd[s][:, 0:1]), [xt[s], rstd[s]], [xn[s]])

        def nt_post(tl, i, hb):
            s = i % 2
            for hf in range(2):
                tv = tpb[hf][:].bitcast(BF16)
                for c8 in range(8):
                    c = hf * 8 + c8
                    P.op(pe, lambda e: e.transpose(out=tv[:, c8 * 128:(c8 + 1) * 128],
                                                   in_=xn[s][:, c * 128:(c + 1) * 128],
                                                   identity=ident_b[:]), [xn[s], ident_b], [tpb[hf]])
                P.op(dve, lambda e: e.tensor_copy(
                    out=hb[:, hf * 8:(hf + 1) * 8, tl:tl + 128],
                    in_=tv.rearrange("p (c t) -> p c t", c=8)), [tpb[hf]], [hb])

        P.dma(gcol[:], g_in[:], reads=[g_in], writes=[gcol])
        for c in range(NCH):
            P.op(dve, lambda e: e.tensor_copy(out=gfull[:, c, :],
                                              in_=gcol[:, c:c + 1].to_broadcast([128, 128])),
                 [gcol], [gfull])
        wv = w_in.h.rearrange("(c p) n -> p c n", p=128)
        blocks = []
        for (c0, nco, dst, kind, evac) in specs:
            for cb in range(nco // 128):
                blocks.append((c0 + cb * 128, cb * 128, dst, kind, evac))
        nblk = len(blocks)
        gblk = [0]
        NP = ST // 128
        NSUP = L // ST

        wsc = P.dram("wsc_%d" % id(w_in), [nblk, 128, NCH * 128], BF16)
        cur_sup = [0]

        def wload(bi):
            s_ = gblk[0] % 2
            gblk[0] += 1
            cc = blocks[bi][0]
            if cur_sup[0] == 0:
                P.dma(wf[s_][:, 0:8, :], wv[:, 0:8, cc:cc + 128], reads=[w_in], writes=[wf[s_]])
                P.dma(wf[s_][:, 8:16, :], wv[:, 8:16, cc:cc + 128], reads=[w_in], writes=[wf[s_]], q=act)
                P.op(pool, lambda e: e.tensor_tensor(out=wb[s_][:], in0=wf[s_][:], in1=gfull[:],
                                                     op=ALU.mult), [wf[s_], gfull], [wb[s_]])
                if NSUP > 1:
                    P.dma(wsc.h[bi], wb[s_][:].rearrange("p c n -> p (c n)"), reads=[wb[s_]], writes=[wsc], q=pool)
            else:
                P.dma(wb[s_][:].rearrange("p c n -> p (c n)"), wsc.h[bi], reads=[wsc], writes=[wb[s_]])
            return s_

        for i in range(NP):
            nt_pre(i * 128, i)
            nt_post(i * 128, i, hnTs[0])
        for sup in range(NSUP):
            cur_sup[0] = sup
            hnT = hnTs[sup % 2]
            hnN = hnTs[(sup + 1) % 2]
            more = sup + 1 < NSUP
            nxt = wload(0)
            for bi in range(nblk):
                (cc, r0, dst, kind, evac) = blocks[bi]
                s = nxt
                if bi + 1 < nblk:
                    nxt = wload(bi + 1)
                if more and bi < NP:
                    nt_pre((sup + 1) * ST + bi * 128, bi)
                NT4 = ST // 512
                for tt in range(NT4):
                    a = accb[nacc[0] % 6]
                    nacc[0] += 1
                    o = ob[tt % 4]
                    tg = sup * ST + tt * 512
                    if kind == "fm":
                        for c in range(NCH):
                            P.op(pe, lambda e: e.matmul(a[:], lhsT=wb[s][:, c, :],
                                                        rhs=hnT[:, c, tt * 512:(tt + 1) * 512],
                                                        start=(c == 0), stop=(c == NCH - 1)),
                                 [wb[s], hnT], [a])
                    else:
                        for j in range(4):
                            for c in range(NCH):
                                P.op(pe, lambda e: e.matmul(
                                    a[:, j * 128:(j + 1) * 128],
                                    lhsT=hnT[:, c, tt * 512 + j * 128: tt * 512 + (j + 1) * 128],
                                    rhs=wb[s][:, c, :], start=(c == 0), stop=(c == NCH - 1)),
                                    [wb[s], hnT], [a])
                    if evac[0] == "copy":
                        P.op(act, lambda e: e.activation(out=o[:], in_=a[:], func=AF.Copy), [a], [o])
                    elif evac[0] == "scale":
                        P.op(act, lambda e: e.activation(out=o[:], in_=a[:], func=AF.Copy,
                                                         scale=evac[1]), [a], [o])
                    elif evac[0] == "silu":
                        P.op(act, lambda e: e.activation(out=o[:], in_=a[:], func=AF.Silu),
                             [a], [o])
                    if kind == "fm":
                        P.dma(dst[r0:r0 + 128, tg:tg + 512], o[:], reads=[o], writes=[dst], q=act)
                    else:
                        P.dma(dst.h[tg:tg + 512, r0:r0 + 128].rearrange("(j p) n -> p j n", p=128),
                              o[:].rearrange("p (j n) -> p j n", j=4), reads=[o], writes=[dst], q=act)
                if more and 1 <= bi <= NP:
                    nt_post((bi - 1) * 128, bi - 1, hnN)
            if fcol is not None:
                s = gblk[0] % 2
                gblk[0] += 1
                P.dma(wf[s][:, :, 0:16], wv[:, :, fcol:fcol + 16], reads=[w_in], writes=[wf[s]])
                P.op(pool, lambda e: e.tensor_tensor(out=wb[s][:, :, 0:16], in0=wf[s][:, :, 0:16],
                                                     in1=gfull[:, :, 0:16], op=ALU.mult),
                     [wf[s], gfull], [wb[s]])
                for tt in range(ST // 512):
                    a = accb[nacc[0] % 6]
                    nacc[0] += 1
                    tg = sup * ST + tt * 512
                    for c in range(NCH):
                        P.op(pe, lambda e: e.matmul(a[0:16, :], lhsT=wb[s][:, c, 0:16],
                                                    rhs=hnT[:, c, tt * 512:(tt + 1) * 512],
                                                    start=(c == 0), stop=(c == NCH - 1)),
                             [wb[s], hnT], [a])
                    P.op(dve, lambda e: e.tensor_copy(out=flog[:, tg:tg + 512], in_=a[0:16, :]),
                         [a], [flog])
        P.phase_end()

    def fox():
        P.phase_begin()
        NB = L // 128
        nbf = P.sb("nbf", [16, 1], F32)
        onec = P.sb("onec", [16, 1], F32)
        cpos = P.sb("cpos", [16, L], F32)
        cT = P.sb("cT", [128, NB, 16], F32)
        crefB = P.sb("crefB", [128, NB, 16], F32)
        P.dma(nbf[:], bf_in[:], reads=[bf_in], writes=[nbf])
        P.op(dve, lambda e: e.tensor_scalar(out=nbf[:], in0=nbf[:], scalar1=-1.0, scalar2=None,
                                            op0=ALU.mult), [nbf], [nbf])
        P.op(dve, lambda e: e.memset(onec[:], 1.0), [], [onec])
        P.op(act, lambda e: e.activation(out=flog[:], in_=flog[:], func=AF.Exp, scale=-1.0,
                                         bias=nbf[:, 0:1]), [flog, nbf], [flog])
        P.op(act, lambda e: e.activation(out=flog[:], in_=flog[:], func=AF.Ln, bias=onec[:, 0:1]),
             [flog, onec], [flog])
        P.op(dve, lambda e: e.tensor_tensor_scan(out=cpos[:], data0=onec[:, 0:1].to_broadcast([16, L]),
                                                 data1=flog[:], initial=0.0, op0=ALU.mult,
                                                 op1=ALU.add), [flog, onec], [cpos])
        for kb in range(NB):
            P.op(pe, lambda e: e.transpose(out=PS[0][:, kb * 16:(kb + 1) * 16],
                                           in_=cpos[0:16, kb * 128:(kb + 1) * 128],
                                           identity=ident_f[0:16, 0:16]), [cpos, ident_f], [PS[0]])
        P.op(dve, lambda e: e.tensor_copy(out=cT[:].rearrange("p a b -> p (a b)"), in_=PS[0][:]),
             [PS[0]], [cT])
        P.op(pe, lambda e: e.matmul(PS[1][:], lhsT=sel_f[:], rhs=cT[:].rearrange("p a b -> p (a b)"),
                                    start=True, stop=True), [sel_f, cT], [PS[1]])
        P.op(dve, lambda e: e.tensor_copy(out=crefB[:].rearrange("p a b -> p (a b)"), in_=PS[1][:]),
             [PS[1]], [crefB])

        kh = [P.sb("kh", [128, L], BF16) for i in range(2)]
        qh = [P.sb("qh", [128, L], BF16) for i in range(2)]
        zh = [P.sb("zh", [128, L], BF16) for i in range(2)]
        vh = [P.sb("vh", [128, NB, 129], BF16) for i in range(2)]
        mixh = [P.sb("mixh", [128, L], BF16) for i in range(2)]
        cr3 = [P.sb("cr3", [96, NB, 128], BF16) for i in range(2)]
        o96f = P.sb("o96f", [96, 128], F32)
        o96 = P.sb("o96", [96, 128], BF16)
        hiT = P.sb("hiT", [128, NB], BF16)
        miT = P.sb("miT", [128, NB], BF16)
        loT = P.sb("loT", [128, NB], BF16)
        r1 = P.sb("r1", [128, NB], F32)
        r2 = P.sb("r2", [128, NB], F32)
        pT = [P.sb("pT", [128, 512], BF16) for i in range(4)]
        rden = [P.sb("rden", [128, 1], F32) for i in range(8)]
        on = [P.sb("on", [128, 128], BF16) for i in range(8)]
        P.dma(o96f[:], ones96_in[:], reads=[ones96_in], writes=[o96f])
        P.op(dve, lambda e: e.tensor_copy(out=o96[:], in_=o96f[:]), [o96f], [o96])
        for i in range(2):
            P.op(pool, lambda e: e.memset(vh[i][:, :, 128:129], 1.0), [], [vh[i]])
            P.op(pool, lambda e: e.memset(cr3[i][:], 0.0), [], [cr3[i]])
        sT = [PS[0], PS[1]]
        oacc = [[PS[2], PS[3]], [PS[4], PS[5]]]
        oTb = [PS[6], PS[7]]
        npt = 0
        nst = 0

        def fox_load(h_):
            s_ = h_ % 2
            r_ = h_ * 128
            P.dma(kh[s_][:], kc[r_:r_ + 128, :], reads=[kc], writes=[kh[s_]])
            P.dma(qh[s_][:], qc[r_:r_ + 128, :], reads=[qc], writes=[qh[s_]])
            P.dma(vh[s_][:, :, 0:128], vc.h[:, r_:r_ + 128].rearrange("(kb p) n -> p kb n", p=128),
                  reads=[vc], writes=[vh[s_]])
            P.dma(zh[s_][:], zc[r_:r_ + 128, :], reads=[zc], writes=[zh[s_]])

        fox_load(0)
        for h in range(16):
            s = h % 2
            r0 = h * 128
            if h + 1 < 16:
                fox_load(h + 1)
            cb_ = crefB[:, :, h]
            P.op(dve, lambda e: e.tensor_scalar(out=hiT[:], in0=cb_, scalar1=-1.0, scalar2=None, op0=ALU.mult),
                 [crefB], [hiT])
            P.op(dve, lambda e: e.scalar_tensor_tensor(out=r1[:], in0=cb_, scalar=-1.0, in1=hiT[:],
                                                       op0=ALU.mult, op1=ALU.subtract), [crefB, hiT], [r1])
            P.op(dve, lambda e: e.tensor_copy(out=miT[:], in_=r1[:]), [r1], [miT])
            P.op(dve, lambda e: e.tensor_tensor(out=r2[:], in0=r1[:], in1=miT[:], op=ALU.subtract), [r1, miT], [r2])
            P.op(dve, lambda e: e.tensor_copy(out=loT[:], in_=r2[:]), [r2], [loT])
            for (row, src_) in ((0, hiT), (32, miT), (64, loT)):
                P.op(dve, lambda e: e.tensor_copy(
                    out=cr3[s][row:row + 1, :, :],
                    in_=src_[row:row + 1, :].unsqueeze(2).to_broadcast([1, NB, 128])), [src_], [cr3[s]])
            cr3f = cr3[s][:].rearrange("p a b -> p (a b)")
            items = [(qg, kb) for qg in range(NB // 4) for kb in range(4 * qg + 4)]

            def emit_qk(i_):
                qg_, kb_ = items[i_]
                q0_ = max(4 * qg_, kb_)
                nq_ = 4 * qg_ + 4 - q0_
                st_ = sT[i_ % 2]
                P.op(pe, lambda e: e.matmul(st_[:, 0:nq_ * 128],
                                            lhsT=kh[s][:, kb_ * 128:(kb_ + 1) * 128],
                                            rhs=qh[s][:, q0_ * 128:(q0_ + nq_) * 128],
                                            start=True, stop=False), [kh[s], qh[s]], [st_])
                P.op(pe, lambda e: e.matmul(st_[:, 0:nq_ * 128], lhsT=o96[:],
                                            rhs=cr3f[:, q0_ * 128:(q0_ + nq_) * 128],
                                            start=False, stop=True), [o96, cr3[s]], [st_])

            def f_norm(qg_):
                par_ = qg_ % 2
                for jj in range(4):
                    ob_ = oacc[par_][jj // 2]
                    c0 = (jj % 2) * 256
                    rd_ = rden[par_ * 4 + jj]
                    on_ = on[par_ * 4 + jj]
                    P.op(dve, lambda e: e.reciprocal(out=rd_[:], in_=ob_[:, c0 + 128:c0 + 129]), [ob_], [rd_])
                    P.op(dve, lambda e: e.tensor_scalar(out=on_[:], in0=ob_[:, c0:c0 + 128],
                                                        scalar1=rd_[:, 0:1], scalar2=None,
                                                        op0=ALU.mult), [ob_, rd_], [on_])

            def f_fin(qg_):
                par_ = qg_ % 2
                otb = oTb[par_]
                otv = otb[:].bitcast(BF16)
                for jj in range(4):
                    on_ = on[par_ * 4 + jj]
                    P.op(pe, lambda e: e.transpose(out=otv[:, jj * 128:(jj + 1) * 128], in_=on_[:],
                                                   identity=ident_b[:]), [on_, ident_b], [otb])
                t0 = qg_ * 512
                P.op(dve, lambda e: e.tensor_tensor(out=mixh[s][:, t0:t0 + 512], in0=otv[:, 0:512],
                                                    in1=zh[s][:, t0:t0 + 512], op=ALU.mult),
                     [otb, zh[s]], [mixh[s]])

            emit_qk(0)
            for i_, (qg, kb) in enumerate(items):
                par = qg % 2
                q0 = max(4 * qg, kb)
                nq = 4 * qg + 4 - q0
                st = sT[i_ % 2]
                if i_ + 1 < len(items):
                    emit_qk(i_ + 1)
                p = pT[npt % 4]
                npt += 1
                P.op(act, lambda e: e.activation(out=p[:, 0:nq * 128], in_=st[:, 0:nq * 128],
                                                 func=AF.Exp, bias=cT[:, kb, h:h + 1]), [st, cT], [p])
                if kb >= 4 * qg:
                    P.op(pool, lambda e: e.tensor_tensor(out=p[:, 0:128], in0=p[:, 0:128], in1=tri_b[:],
                                                         op=ALU.mult), [p, tri_b], [p])
                for j in range(nq):
                    qb = q0 + j
                    jj = qb - 4 * qg
                    ob_ = oacc[par][jj // 2]
                    P.op(pe, lambda e: e.matmul(ob_[:, (jj % 2) * 256:(jj % 2) * 256 + 129],
                                                lhsT=p[:, j * 128:(j + 1) * 128], rhs=vh[s][:, kb, :],
                                                start=(kb == 0), stop=(kb == qb)),
                         [p, vh[s]], [ob_])
                if kb == 4 * qg + 3:
                    f_norm(qg)
                    if qg > 0:
                        f_fin(qg - 1)
            f_fin(NB // 4 - 1)
            P.dma(mixT[r0:r0 + 128, :], mixh[s][:], reads=[mixh[s]], writes=[mixT], q=pool)
        P.phase_end()

    def attn_a():
        P.phase_begin()
        NB = L // 128
        abc = P.sb("abc", [128, 16], F32)
        am = P.sb("am", [128, 3, 128], F32)
        P.dma(abc[:], abc_in[:], reads=[abc_in], writes=[abc])
        P.dma(am[:], amask_in.h.rearrange("r k q -> k r q"), reads=[amask_in], writes=[am])
        abt = [P.sb("abt", [128, 3, 128], F32) for i in range(2)]
        zero_c = P.sb("zero_c", [128, 128], F32)
        P.op(dve, lambda e: e.memset(zero_c[:], 0.0), [], [zero_c])
        E = [P.sb("E", [128, 5, 2, 128], BF16) for i in range(2)]
        khp = [P.sb("khp", [128, L], BF16) for i in range(2)]
        qhp = [P.sb("qz", [128, 2, L], BF16) for i in range(2)]
        for i in range(2):
            P.op(pool, lambda e: e.memset(qhp[i][64:128, 0, :], 0.0), [], [qhp[i]])
            P.op(pool, lambda e: e.memset(qhp[i][0:64, 1, :], 0.0), [], [qhp[i]])
        zhp = [P.sb("zhp", [128, L], BF16) for i in range(2)]
        vhp = [P.sb("vhp", [128, NB, 2, 65], BF16) for i in range(2)]
        mixh = [P.sb("mixh", [128, L], BF16) for i in range(2)]
        pA = [[P.sb("pA", [128, 512], BF16) for h2 in range(2)] for i in range(2)]
        pB = [P.sb("pB", [128, 256], BF16) for i in range(2)]
        rden = [P.sb("rden", [128, 1], F32) for i in range(4)]
        on = [P.sb("on", [128, 128], BF16) for i in range(2)]
        for i in range(2):
            P.op(pool, lambda e: e.memset(vhp[i][:, :, :, 64:65], 1.0), [], [vhp[i]])
        sA = [[PS[0], PS[1]], [PS[2], PS[3]]]
        sBb = PS[4]
        oTb = PS[5]
        oacc = [PS[6], PS[7]]

        def a_load(hp_):
            s_ = hp_ % 2
            r_ = hp_ * 128
            P.dma(khp[s_][:], ka[r_:r_ + 128, :], reads=[ka], writes=[khp[s_]])
            P.dma(qhp[s_][0:64, 0, :], qa[r_:r_ + 64, :], reads=[qa], writes=[qhp[s_]])
            P.dma(qhp[s_][64:128, 1, :], qa[r_ + 64:r_ + 128, :], reads=[qa], writes=[qhp[s_]])
            for h2_ in range(2):
                P.dma(vhp[s_][:, :, h2_, 0:64],
                      va.h[:, r_ + h2_ * 64:r_ + (h2_ + 1) * 64].rearrange("(kb p) n -> p kb n", p=128),
                      reads=[va], writes=[vhp[s_]])
            P.dma(zhp[s_][:], za[r_:r_ + 128, :], reads=[za], writes=[zhp[s_]])

        NHP = 8
        NM = NB
        a_load(0)
        for hp in range(NHP):
            s = hp % 2
            r0 = hp * 128
            if hp + 1 < NHP:
                a_load(hp + 1)
            for h2 in range(2):
                h = hp * 2 + h2
                P.dma(abt[h2][:], abias_in.h[h].rearrange("r k q -> k r q"), reads=[abias_in],
                      writes=[abt[h2]])
                P.op(dve, lambda e: e.tensor_tensor(out=abt[h2][:], in0=abt[h2][:], in1=am[:],
                                                    op=ALU.add), [abt[h2], am], [abt[h2]])
                for (ei, r) in ((0, 0), (1, 3), (2, 4)):
                    P.op(act, lambda e: e.activation(out=E[s][:, r, h2, :], in_=abt[h2][:, ei, :], func=AF.Exp),
                         [abt[h2]], [E[s]])
                for r in (1, 2):
                    P.op(act, lambda e: e.activation(out=E[s][:, r, h2, :], in_=zero_c[:], func=AF.Exp,
                                                     bias=abc[:, h:h + 1]), [zero_c, abc], [E[s]])
            def emit_qk(m_):
                par_ = m_ % 2
                rlo_ = max(0, 4 - m_)
                rq_ = qhp[s][:, :, m_ * 128:(m_ + 1) * 128]
                for r_ in range(rlo_, 4):
                    kb_ = m_ - 4 + r_
                    sa_ = sA[par_][r_ // 2]
                    c_ = (r_ % 2) * 256
                    P.op(pe, lambda e: e.matmul(sa_[:, c_:c_ + 256], lhsT=khp[s][:, kb_ * 128:(kb_ + 1) * 128],
                                                rhs=rq_, start=True, stop=True), [khp[s], qhp[s]], [sa_])
                P.op(pe, lambda e: e.matmul(sBb[:, par_ * 256:(par_ + 1) * 256],
                                            lhsT=khp[s][:, m_ * 128:(m_ + 1) * 128],
                                            rhs=rq_, start=True, stop=True), [khp[s], qhp[s]], [sBb])

            def a_fin(m_):
                par_ = m_ % 2
                otv_ = oTb[:].bitcast(BF16)[:, par_ * 128:(par_ + 1) * 128]
                P.op(pe, lambda e: e.transpose(out=otv_, in_=on[par_][:], identity=ident_b[:]),
                     [on[par_], ident_b], [oTb])
                t0_ = m_ * 128
                P.op(dve, lambda e: e.tensor_tensor(out=mixh[s][:, t0_:t0_ + 128], in0=otv_,
                                                    in1=zhp[s][:, t0_:t0_ + 128], op=ALU.mult),
                     [oTb, zhp[s]], [mixh[s]])

            emit_qk(0)
            for m in range(NM):
                par = m % 2
                ob_ = oacc[par]
                rlo = max(0, 4 - m)
                if m + 1 < NM:
                    emit_qk(m + 1)
                for bk in range(2):
                    r0_ = max(rlo, 2 * bk)
                    if r0_ > 2 * bk + 1:
                        continue
                    cs_ = slice((r0_ - 2 * bk) * 256, 512)
                    sa = sA[par][bk]
                    pa = pA[par][bk]
                    P.op(act, lambda e: e.activation(out=pa[:, cs_], in_=sa[:, cs_], func=AF.Exp), [sa], [pa])
                    P.op(dve, lambda e: e.tensor_tensor(
                        out=pa[:, cs_], in0=pa[:, cs_],
                        in1=E[s][:, r0_:2 * bk + 2, :, :].rearrange("p a b c -> p (a b c)"), op=ALU.mult),
                        [pa, E[s]], [pa])
                pb = pB[par]
                P.op(act, lambda e: e.activation(out=pb[:], in_=sBb[:, par * 256:(par + 1) * 256], func=AF.Exp), [sBb], [pb])
                P.op(pool, lambda e: e.tensor_tensor(out=pb[:], in0=pb[:],
                                                     in1=E[s][:, 4, :, :].rearrange("p b c -> p (b c)"), op=ALU.mult),
                     [pb, E[s]], [pb])
                for h2 in range(2):
                    for r in range(rlo, 5):
                        kb = m - 4 + r
                        if r < 4:
                            pa = pA[par][r // 2]
                            c_ = ((r % 2) * 2 + h2) * 128
                            lhs = pa[:, c_:c_ + 128]
                        else:
                            pa = pb
                            lhs = pb[:, h2 * 128:(h2 + 1) * 128]
                        P.op(pe, lambda e: e.matmul(ob_[:, h2 * 128:h2 * 128 + 65], lhsT=lhs,
                                                    rhs=vhp[s][:, kb, h2, :], start=(r == rlo), stop=(r == 4)),
                             [pa, vhp[s]], [ob_])
                for h2 in range(2):
                    rd = rden[par * 2 + h2]
                    P.op(dve, lambda e: e.reciprocal(out=rd[:], in_=ob_[:, h2 * 128 + 64:h2 * 128 + 65]),
                         [ob_], [rd])
                    P.op(dve, lambda e: e.tensor_scalar(out=on[par][:, h2 * 64:(h2 + 1) * 64],
                                                        in0=ob_[:, h2 * 128:h2 * 128 + 64],
                                                        scalar1=rd[:, 0:1], scalar2=None, op0=ALU.mult),
                         [ob_, rd], [on[par]])
                if m > 0:
                    a_fin(m - 1)
            a_fin(NM - 1)
            P.dma(mixT[r0:r0 + 128, :], mixh[s][:], reads=[mixh[s]], writes=[mixT], q=pool)
        P.phase_end()

    def s5_glu():
        P.phase_begin()
        TWO_PI = 2.0 * math.pi
        T = 512
        NT = L // T
        P.phase_begin()
        iota1 = P.sb("iota1", [128, T], F32)
        pic = P.sb("pic", [128, 1], F32)
        dsk = P.sb("dsk", [128, 8], F32)
        P.dma(iota1[:], iota1_in[:], reads=[iota1_in], writes=[iota1])
        P.dma(dsk[:], dsk_in[:], reads=[dsk_in], writes=[dsk])
        P.op(dve, lambda e: e.memset(pic[:], math.pi), [], [pic])
        names = ["lr", "li", "dt", "rho", "th", "m", "sn", "cs", "are", "aim", "den", "nr", "t1", "t2",
                 "cr", "ci", "ncr"]
        sc = {n: P.sb("s5_" + n, [128, 32], F32) for n in names}
        P.dma(sc["lr"][:], slr_in[:], reads=[slr_in], writes=[sc["lr"]])
        P.dma(sc["li"][:], sli_in[:], reads=[sli_in], writes=[sc["li"]])
        P.dma(sc["dt"][:], sldt_in[:], reads=[sldt_in], writes=[sc["dt"]])

        def tt(o, a, b, op, eng=dve):
            P.op(eng, lambda e: e.tensor_tensor(out=sc[o][:], in0=sc[a][:], in1=sc[b][:], op=op),
                 [sc[a], sc[b]], [sc[o]])

        def ts(o, a, s1, s2, op0, op1=None):
            if op1 is None:
                P.op(dve, lambda e: e.tensor_scalar(out=sc[o][:], in0=sc[a][:], scalar1=s1, scalar2=None,
                                                    op0=op0), [sc[a]], [sc[o]])
            else:
                P.op(dve, lambda e: e.tensor_scalar(out=sc[o][:], in0=sc[a][:], scalar1=s1, scalar2=s2,
                                                    op0=op0, op1=op1), [sc[a]], [sc[o]])

        def sin_of(o, m_):
            P.op(act, lambda e: e.activation(out=sc[o][:], in_=sc[m_][:], func=AF.Sin, scale=-1.0,
                                             bias=pic[:, 0:1]), [sc[m_], pic], [sc[o]])

        P.op(act, lambda e: e.activation(out=sc["dt"][:], in_=sc["dt"][:], func=AF.Exp), [sc["dt"]], [sc["dt"]])
        tt("rho", "lr", "dt", ALU.mult)
        P.op(act, lambda e: e.activation(out=sc["rho"][:], in_=sc["rho"][:], func=AF.Exp), [sc["rho"]], [sc["rho"]])
        tt("th", "li", "dt", ALU.mult)
        ts("m", "th", math.pi, TWO_PI, ALU.is_gt, ALU.mult)
        ts("t1", "th", 3 * math.pi, TWO_PI, ALU.is_gt, ALU.mult)
        tt("m", "m", "t1", ALU.add)
        ts("t1", "th", 5 * math.pi, TWO_PI, ALU.is_gt, ALU.mult)
        tt("m", "m", "t1", ALU.add)
        tt("m", "th", "m", ALU.subtract)
        P.op(act, lambda e: e.activation(out=sc["sn"][:], in_=sc["m"][:], func=AF.Sin), [sc["m"]], [sc["sn"]])
        ts("t1", "m", 0.5 * math.pi, TWO_PI, ALU.is_gt, ALU.mult)
        ts("t2", "m", 0.5 * math.pi, None, ALU.add)
        tt("t2", "t2", "t1", ALU.subtract)
        P.op(act, lambda e: e.activation(out=sc["cs"][:], in_=sc["t2"][:], func=AF.Sin), [sc["t2"]], [sc["cs"]])
        tt("are", "rho", "cs", ALU.mult)
        tt("aim", "rho", "sn", ALU.mult)
        tt("den", "lr", "lr", ALU.mult)
        tt("t1", "li", "li", ALU.mult)
        tt("den", "den", "t1", ALU.add)
        P.op(dve, lambda e: e.reciprocal(out=sc["den"][:], in_=sc["den"][:]), [sc["den"]], [sc["den"]])
        ts("nr", "are", -1.0, None, ALU.add)
        tt("t1", "nr", "lr", ALU.mult)
        tt("t2", "aim", "li", ALU.mult)
        tt("cr", "t1", "t2", ALU.add)
        tt("cr", "cr", "den", ALU.mult)
        tt("t1", "aim", "lr", ALU.mult)
        tt("t2", "nr", "li", ALU.mult)
        tt("ci", "t1", "t2", ALU.subtract)
        tt("ci", "ci", "den", ALU.mult)
        ts("ncr", "cr", -1.0, None, ALU.mult)

        NK = 12
        pc = [sc["cs"]] + [P.sb("s5_pc", [128, 32], F32) for k_ in range(1, NK)]
        pn = [sc["sn"]] + [P.sb("s5_pn", [128, 32], F32) for k_ in range(1, NK)]
        nn = [P.sb("s5_nn", [128, 32], F32) for k_ in range(NK)]
        for k_ in range(NK):
            if k_ > 0:
                a_, b_ = pc[k_ - 1], pn[k_ - 1]
                P.op(dve, lambda e: e.tensor_tensor(out=sc["t1"][:], in0=a_[:], in1=a_[:], op=ALU.mult), [a_], [sc["t1"]])
                P.op(dve, lambda e: e.tensor_tensor(out=sc["t2"][:], in0=b_[:], in1=b_[:], op=ALU.mult), [b_], [sc["t2"]])
                P.op(dve, lambda e: e.tensor_tensor(out=pc[k_][:], in0=sc["t1"][:], in1=sc["t2"][:], op=ALU.subtract),
                     [sc["t1"], sc["t2"]], [pc[k_]])
                P.op(dve, lambda e: e.tensor_tensor(out=sc["t1"][:], in0=a_[:], in1=b_[:], op=ALU.mult), [a_, b_], [sc["t1"]])
                P.op(dve, lambda e: e.tensor_scalar(out=pn[k_][:], in0=sc["t1"][:], scalar1=2.0, scalar2=None,
                                                    op0=ALU.mult), [sc["t1"]], [pn[k_]])
            P.op(dve, lambda e: e.tensor_scalar(out=nn[k_][:], in0=pn[k_][:], scalar1=-1.0, scalar2=None,
                                                op0=ALU.mult), [pn[k_]], [nn[k_]])
        par_ = [None, sc["are"]] + [P.sb("s5_par", [128, 32], F32) for k_ in range(2, 9)]
        pai_ = [None, sc["aim"]] + [P.sb("s5_pai", [128, 32], F32) for k_ in range(2, 9)]
        nai_ = [None] + [P.sb("s5_nai", [128, 32], F32) for k_ in range(1, 9)]
        for k_ in range(2, 9):
            a_, b_ = par_[k_ - 1], pai_[k_ - 1]
            P.op(dve, lambda e: e.tensor_tensor(out=sc["t1"][:], in0=a_[:], in1=sc["are"][:], op=ALU.mult), [a_, sc["are"]], [sc["t1"]])
            P.op(dve, lambda e: e.tensor_tensor(out=sc["t2"][:], in0=b_[:], in1=sc["aim"][:], op=ALU.mult), [b_, sc["aim"]], [sc["t2"]])
            P.op(dve, lambda e: e.tensor_tensor(out=par_[k_][:], in0=sc["t1"][:], in1=sc["t2"][:], op=ALU.subtract),
                 [sc["t1"], sc["t2"]], [par_[k_]])
            P.op(dve, lambda e: e.tensor_tensor(out=sc["t1"][:], in0=a_[:], in1=sc["aim"][:], op=ALU.mult), [a_, sc["aim"]], [sc["t1"]])
            P.op(dve, lambda e: e.tensor_tensor(out=sc["t2"][:], in0=b_[:], in1=sc["are"][:], op=ALU.mult), [b_, sc["are"]], [sc["t2"]])
            P.op(dve, lambda e: e.tensor_tensor(out=pai_[k_][:], in0=sc["t1"][:], in1=sc["t2"][:], op=ALU.add),
                 [sc["t1"], sc["t2"]], [pai_[k_]])
        for k_ in range(1, 9):
            P.op(dve, lambda e: e.tensor_scalar(out=nai_[k_][:], in0=pai_[k_][:], scalar1=-1.0, scalar2=None,
                                                op0=ALU.mult), [pai_[k_]], [nai_[k_]])
        rho8 = P.sb("s5_rho8", [128, 32], F32)
        tt("t1", "rho", "rho", ALU.mult)
        tt("t2", "t1", "t1", ALU.mult)
        P.op(dve, lambda e: e.tensor_tensor(out=rho8[:], in0=sc["t2"][:], in1=sc["t2"][:], op=ALU.mult), [sc["t2"]], [rho8])
        nci = P.sb("s5_nci", [128, 32], F32)
        P.op(dve, lambda e: e.tensor_scalar(out=nci[:], in0=sc["ci"][:], scalar1=-1.0, scalar2=None, op0=ALU.mult),
             [sc["ci"]], [nci])

        LY = P.sb("LY", [128, 2, 32, 128], BF16)
        lst = [P.sb("lst", [128, 8, 128], F32) for i in range(2)]
        k = 0
        for ri in range(2):
            for q4 in range(4):
                st_ = lst[k % 2]
                k += 1
                P.dma(st_[:], ly_in.h[ri, q4 * 8:(q4 + 1) * 8].rearrange("a k m -> k a m"),
                      reads=[ly_in], writes=[st_])
                P.op(act, lambda e: e.activation(out=LY[:, ri, q4 * 8:(q4 + 1) * 8, :], in_=st_[:],
                                                 func=AF.Copy, scale=(1.0, -1.0)[ri]), [st_], [LY])

        J = L // 8
        bst = [P.sb("bst", [128, 2, 4, 128], F32)] * 2
        cst = [P.sb("cst", [128, 2, 4, 128], F32)] * 2
        Gr = P.sb("Gr", [128, 128], F32)
        Gi = P.sb("Gi", [128, 128], F32)
        Gt = P.sb("Gt", [128, 128], F32)
        Gb = P.sb("Gb", [128, 4, 8, 2, 128], BF16)
        Bs = P.sb("Bs", [128, 4, 8, 2, 128], BF16)
        CA = P.sb("CA", [128, 4, 8, 2, 128], BF16)
        Kt = P.sb("Kt", [128, 8, 128], BF16)
        dI = P.sb("dI", [128, 128], F32)
        uP = [P.sb("uP", [128, L], BF16) for i in range(2)]
        uPm = P.sb("uPm", [128, 8, L // 8], BF16)
        tabc = [P.sb("tabc", [128, J], F32) for i in range(2)]
        tabs = [P.sb("tabs", [128, J], F32) for i in range(2)]
        mm = P.sb("mm", [128, J], F32)
        tmps = []
        for i_ in range(1):
            tmps.append({n: P.sb("tmp_" + n, [128, J], F32) for n in
                         ("vr", "vi", "a1", "a2", "a3", "a4", "er", "ei", "wr", "wi", "xr", "xi")})
        tmps.append(tmps[0])
        Xb = P.sb("Xb", [128, 4, 2, J], BF16)
        ypk = P.sb("ypk", [128, L], F32)
        g1 = [P.sb("g1", [128, 512], F32) for i in range(2)]
        g2 = [P.sb("g2", [128, 512], F32) for i in range(2)]
        ygo = [P.sb("ygo", [128, 512], BF16) for i in range(2)]
        P.op(pool, lambda e: e.memset(Xb[:, :, :, 0:1], 0.0), [], [Xb])

        def T2(eng, o, a, b, op):
            P.op(eng, lambda e: e.tensor_tensor(out=o[:], in0=a[:], in1=b[:], op=op), [a, b], [o])

        def cmul(o_r, o_i, i_r, i_i, sr, si, nsi, tmp_):
            P.op(dve, lambda e: e.tensor_scalar(out=tmp_[:], in0=i_r, scalar1=sr, scalar2=None, op0=ALU.mult),
                 [], [tmp_])
            P.op(dve, lambda e: e.scalar_tensor_tensor(out=o_r, in0=i_i, scalar=nsi, in1=tmp_[:],
                                                       op0=ALU.mult, op1=ALU.add), [tmp_], [])
            P.op(dve, lambda e: e.tensor_scalar(out=tmp_[:], in0=i_i, scalar1=sr, scalar2=None, op0=ALU.mult),
                 [], [tmp_])
            P.op(dve, lambda e: e.scalar_tensor_tensor(out=o_i, in0=i_r, scalar=si, in1=tmp_[:],
                                                       op0=ALU.mult, op1=ALU.add), [tmp_], [])

        def s_load1(pk):
            up_ = uP[pk % 2]
            bs_, cs2_ = bst[pk % 2], cst[pk % 2]
            upv = uPm
            P.dma(up_[:], ut[pk * 128:(pk + 1) * 128, :], reads=[ut], writes=[up_])
            for ri in range(2):
                P.dma(bs_[:, ri, :, :], lvt_in.h[ri, pk * 4:(pk + 1) * 4].rearrange("a k m -> k a m"),
                      reads=[lvt_in], writes=[bs_])

        def s_g(pk):
            up_ = uP[pk % 2]
            bs_, cs2_ = bst[pk % 2], cst[pk % 2]
            upv = uPm
            for p4 in range(4):
                gp = pk * 4 + p4
                g_ = slice(gp, gp + 1)
                P.op(dve, lambda e: e.tensor_scalar(out=Gt[:], in0=bs_[:, 0, p4, :], scalar1=sc["cr"][:, g_],
                                                    scalar2=None, op0=ALU.mult), [bs_, sc["cr"]], [Gt])
                P.op(dve, lambda e: e.scalar_tensor_tensor(out=Gr[:], in0=bs_[:, 1, p4, :], scalar=nci[:, g_],
                                                           in1=Gt[:], op0=ALU.mult, op1=ALU.add), [bs_, nci, Gt], [Gr])
                P.op(dve, lambda e: e.tensor_scalar(out=Gt[:], in0=bs_[:, 1, p4, :], scalar1=sc["cr"][:, g_],
                                                    scalar2=None, op0=ALU.mult), [bs_, sc["cr"]], [Gt])
                P.op(dve, lambda e: e.scalar_tensor_tensor(out=Gi[:], in0=bs_[:, 0, p4, :], scalar=sc["ci"][:, g_],
                                                           in1=Gt[:], op0=ALU.mult, op1=ALU.add), [bs_, sc["ci"], Gt], [Gi])
                for tau in range(8):
                    if tau > 0:
                        P.op(dve, lambda e: e.tensor_scalar(out=Gt[:], in0=Gr[:], scalar1=sc["aim"][:, g_],
                                                            scalar2=None, op0=ALU.mult), [Gr, sc["aim"]], [Gt])
                        P.op(dve, lambda e: e.tensor_scalar(out=Gr[:], in0=Gr[:], scalar1=sc["are"][:, g_],
                                                            scalar2=None, op0=ALU.mult), [Gr, sc["are"]], [Gr])
                        P.op(dve, lambda e: e.scalar_tensor_tensor(out=Gr[:], in0=Gi[:], scalar=nai_[1][:, g_],
                                                                   in1=Gr[:], op0=ALU.mult, op1=ALU.add),
                             [Gi, nai_[1], Gr], [Gr])
                        P.op(dve, lambda e: e.scalar_tensor_tensor(out=Gi[:], in0=Gi[:], scalar=sc["are"][:, g_],
                                                                   in1=Gt[:], op0=ALU.mult, op1=ALU.add),
                             [Gi, sc["are"], Gt], [Gi])
                    for ri, G_ in ((0, Gr), (1, Gi)):
                        P.op(act, lambda e: e.activation(out=Gb[:, p4, tau, ri, :], in_=G_[:], func=AF.Copy),
                             [G_], [Gb])
                        tb_ = PS[6 + (ri % 2)]
                        P.op(pe, lambda e: e.transpose(out=tb_[:, 0:128], in_=G_[:], identity=ident_f[:]),
                             [G_, ident_f], [tb_])
                        P.op(act, lambda e: e.activation(out=Bs[:, p4, tau, ri, :], in_=tb_[:, 0:128], func=AF.Copy),
                             [tb_], [Bs])

        def s_load2(pk):
            up_ = uP[pk % 2]
            bs_, cs2_ = bst[pk % 2], cst[pk % 2]
            upv = uPm
            for ri in range(2):
                P.dma(cs2_[:, ri, :, :], ly_in.h[ri, pk * 4:(pk + 1) * 4].rearrange("a k m -> k a m"),
                      reads=[ly_in], writes=[cs2_])
            P.op(dve, lambda e: e.tensor_scalar(out=cs2_[:, 1, :, :], in0=cs2_[:, 1, :, :], scalar1=-1.0, scalar2=None,
                                                op0=ALU.mult), [cs2_], [cs2_])
            upv0 = up_[:].rearrange("p (j s) -> p s j", s=8)
            P.op(act, lambda e: e.activation(out=uPm[:], in_=upv0, func=AF.Copy), [up_], [uPm])

        def s_ca(pk):
            up_ = uP[pk % 2]
            bs_, cs2_ = bst[pk % 2], cst[pk % 2]
            upv = uPm
            for p4 in range(4):
                gp = pk * 4 + p4
                g_ = slice(gp, gp + 1)
                for tl in range(8):
                    ar_, ai_, nai2 = par_[tl + 1][:, g_], pai_[tl + 1][:, g_], nai_[tl + 1][:, g_]
                    P.op(dve, lambda e: e.tensor_scalar(out=Gt[:], in0=cs2_[:, 0, p4, :], scalar1=ar_, scalar2=None,
                                                        op0=ALU.mult), [cs2_, par_[tl + 1]], [Gt])
                    P.op(dve, lambda e: e.scalar_tensor_tensor(out=CA[:, p4, tl, 0, :], in0=cs2_[:, 1, p4, :], scalar=ai_,
                                                               in1=Gt[:], op0=ALU.mult, op1=ALU.add),
                         [cs2_, pai_[tl + 1], Gt], [CA])
                    P.op(dve, lambda e: e.tensor_scalar(out=Gt[:], in0=cs2_[:, 0, p4, :], scalar1=nai2, scalar2=None,
                                                        op0=ALU.mult), [cs2_, nai_[tl + 1]], [Gt])
                    P.op(dve, lambda e: e.scalar_tensor_tensor(out=CA[:, p4, tl, 1, :], in0=cs2_[:, 1, p4, :], scalar=ar_,
                                                               in1=Gt[:], op0=ALU.mult, op1=ALU.add),
                         [cs2_, par_[tl + 1], Gt], [CA])

        def s_k(pk):
            up_ = uP[pk % 2]
            bs_, cs2_ = bst[pk % 2], cst[pk % 2]
            upv = uPm
            P.op(dve, lambda e: e.tensor_scalar(out=dI[:], in0=ident_f[:], scalar1=dsk[:, pk:pk + 1], scalar2=None,
                                                op0=ALU.mult), [ident_f, dsk], [dI])
            for tau in range(8):
                kb_ = PS[6 + (tau % 2)]
                for p4 in range(4):
                    gp = pk * 4 + p4
                    P.op(pe, lambda e: e.matmul(kb_[:, 0:128], lhsT=Gb[:, p4, tau, 0, :], rhs=LY[:, 0, gp, :],
                                                start=(p4 == 0), stop=False), [Gb, LY], [kb_])
                    P.op(pe, lambda e: e.matmul(kb_[:, 0:128], lhsT=Gb[:, p4, tau, 1, :], rhs=LY[:, 1, gp, :],
                                                start=False, stop=(p4 == 3)), [Gb, LY], [kb_])
                if tau == 0:
                    P.op(dve, lambda e: e.tensor_tensor(out=Kt[:, 0, :], in0=kb_[:, 0:128], in1=dI[:], op=ALU.add),
                         [kb_, dI], [Kt])
                else:
                    P.op(act, lambda e: e.activation(out=Kt[:, tau, :], in_=kb_[:, 0:128], func=AF.Copy), [kb_], [Kt])

        def s_main(pk):
            up_ = uP[pk % 2]
            bs_, cs2_ = bst[pk % 2], cst[pk % 2]
            upv = uPm
            def emit_e(p4_):
                for ri_ in range(2):
                    pb_ = PS[2 * (p4_ % 2) + ri_]
                    for s_ in range(8):
                        P.op(pe, lambda e: e.matmul(pb_[:], lhsT=Bs[:, p4_, 7 - s_, ri_, :], rhs=upv[:, s_, :],
                                                    start=(s_ == 0), stop=(s_ == 7)), [Bs, uPm], [pb_])

            def st_tab(p4_):
                gp_ = pk * 4 + p4_
                g2_ = slice(gp_, gp_ + 1)
                tc_, ts_ = tabc[p4_ % 2], tabs[p4_ % 2]
                P.op(dve, lambda e: e.tensor_copy(out=tc_[:, 0:1], in_=pc[3][:, g2_]), [pc[3]], [tc_])
                P.op(dve, lambda e: e.tensor_copy(out=ts_[:, 0:1], in_=pn[3][:, g2_]), [pn[3]], [ts_])
                for k_ in range(9):
                    n_ = 1 << k_
                    lo = slice(0, n_)
                    hi = slice(n_, 2 * n_)
                    kk = k_ + 3
                    P.op(dve, lambda e: e.tensor_scalar(out=mm[:, lo], in0=tc_[:, lo], scalar1=pc[kk][:, g2_],
                                                        scalar2=None, op0=ALU.mult), [tc_, pc[kk]], [mm])
                    P.op(dve, lambda e: e.scalar_tensor_tensor(out=tc_[:, hi], in0=ts_[:, lo], scalar=nn[kk][:, g2_],
                                                               in1=mm[:, lo], op0=ALU.mult, op1=ALU.add),
                         [ts_, nn[kk], mm], [tc_])
                    P.op(dve, lambda e: e.tensor_scalar(out=mm[:, lo], in0=ts_[:, lo], scalar1=pc[kk][:, g2_],
                                                        scalar2=None, op0=ALU.mult), [ts_, pc[kk]], [mm])
                    P.op(dve, lambda e: e.scalar_tensor_tensor(out=ts_[:, hi], in0=tc_[:, lo], scalar=pn[kk][:, g2_],
                                                               in1=mm[:, lo], op0=ALU.mult, op1=ALU.add),
                         [tc_, pn[kk], mm], [ts_])

            emit_e(0)
            for p4 in range(4):
                gp = pk * 4 + p4
                if p4 + 1 < 4:
                    emit_e(p4 + 1)
                st_tab(p4)
                tmp = tmps[p4 % 2]
                psa, psb = PS[2 * (p4 % 2)], PS[2 * (p4 % 2) + 1]
                tc_, ts_ = tabc[p4 % 2], tabs[p4 % 2]
                P.op(act, lambda e: e.activation(out=tmp["vr"][:], in_=psa[:], func=AF.Copy), [psa], [tmp["vr"]])
                P.op(act, lambda e: e.activation(out=tmp["vi"][:], in_=psb[:], func=AF.Copy), [psb], [tmp["vi"]])
                T2(dve, tmp["a1"], tc_, tmp["vr"], ALU.mult)
                T2(dve, tmp["a2"], ts_, tmp["vi"], ALU.mult)
                T2(dve, tmp["a3"], tc_, tmp["vi"], ALU.mult)
                T2(dve, tmp["a4"], ts_, tmp["vr"], ALU.mult)
                T2(dve, tmp["er"], tmp["a1"], tmp["a2"], ALU.add)
                T2(dve, tmp["ei"], tmp["a3"], tmp["a4"], ALU.subtract)
                rhoc = rho8[:, gp:gp + 1].to_broadcast([128, J])
                for (w_, e_) in ((tmp["wr"], tmp["er"]), (tmp["wi"], tmp["ei"])):
                    P.op(dve, lambda e: e.tensor_tensor_scan(out=w_[:], data0=rhoc, data1=e_[:], initial=0.0,
                                                             op0=ALU.mult, op1=ALU.add), [rho8, e_], [w_])
                T2(dve, tmp["a1"], tc_, tmp["wr"], ALU.mult)
                T2(dve, tmp["a2"], ts_, tmp["wi"], ALU.mult)
                T2(dve, tmp["a3"], ts_, tmp["wr"], ALU.mult)
                T2(dve, tmp["a4"], tc_, tmp["wi"], ALU.mult)
                T2(dve, tmp["xr"], tmp["a1"], tmp["a2"], ALU.subtract)
                T2(dve, tmp["xi"], tmp["a3"], tmp["a4"], ALU.add)
                P.op(act, lambda e: e.activation(out=Xb[:, p4, 0, 1:J], in_=tmp["xr"][:, 0:J - 1], func=AF.Copy),
                     [tmp["xr"]], [Xb])
                P.op(act, lambda e: e.activation(out=Xb[:, p4, 1, 1:J], in_=tmp["xi"][:, 0:J - 1], func=AF.Copy),
                     [tmp["xi"]], [Xb])

        def s_y(pk):
            up_ = uP[pk % 2]
            bs_, cs2_ = bst[pk % 2], cst[pk % 2]
            upv = uPm
            ypv = ypk[:].rearrange("p (j s) -> p s j", s=8)
            for tl in range(8):
                yb_ = PS[4 + (tl % 2)]
                first = True
                for p4 in range(4):
                    for ri in range(2):
                        P.op(pe, lambda e: e.matmul(yb_[:], lhsT=CA[:, p4, tl, ri, :], rhs=Xb[:, p4, ri, :],
                                                    start=first, stop=False), [CA, Xb], [yb_])
                        first = False
                for s_ in range(tl + 1):
                    P.op(pe, lambda e: e.matmul(yb_[:], lhsT=Kt[:, tl - s_, :], rhs=upv[:, s_, :],
                                                start=False, stop=(s_ == tl)), [Kt, uPm], [yb_])
                P.op(act, lambda e: e.activation(out=ypv[:, tl, :], in_=yb_[:], func=AF.Copy), [yb_], [ypk])

        def s_gelu(pk):
            for cch in range(8):
                cs_ = slice(cch * 512, (cch + 1) * 512)
                g1_, g2_, yo_ = g1[cch % 2], g2[cch % 2], ygo[cch % 2]
                P.op(act, lambda e: e.activation(out=g1_[:], in_=ypk[:, cs_], func=AF.Square), [ypk], [g1_])
                P.op(dve, lambda e: e.tensor_scalar(out=g1_[:], in0=g1_[:], scalar1=0.044715, scalar2=1.0,
                                                    op0=ALU.mult, op1=ALU.add), [g1_], [g1_])
                P.op(dve, lambda e: e.tensor_tensor(out=g2_[:], in0=g1_[:], in1=ypk[:, cs_], op=ALU.mult),
                     [g1_, ypk], [g2_])
                P.op(act, lambda e: e.activation(out=g2_[:], in_=g2_[:], func=AF.Sigmoid,
                                                 scale=1.5957691216057308), [g2_], [g2_])
                P.op(dve, lambda e: e.tensor_tensor(out=yo_[:], in0=g2_[:], in1=ypk[:, cs_], op=ALU.mult),
                     [g2_, ypk], [yo_])
                P.dma(ygd[pk * 128:(pk + 1) * 128, cs_], yo_[:], reads=[yo_], writes=[ygd], q=act)

        s_load1(0)
        s_g(0)
        s_load2(0)
        s_ca(0)
        s_k(0)
        for pk in range(8):
            s_main(pk)
            s_y(pk)
            if pk + 1 < 8:
                s_load1(pk + 1)
                s_g(pk + 1)
            s_gelu(pk)
            if pk + 1 < 8:
                s_load2(pk + 1)
                s_ca(pk + 1)
                s_k(pk + 1)
        P.phase_end()
        lst = [P.sb("lst2", [128, 8, 128], F32) for i in range(2)]
        wg = P.sb("wg", [128, 8, 1024], BF16)
        bgl = P.sb("bgl", [128, 8], F32)
        P.dma(bgl[:], bglu_in[:], reads=[bglu_in], writes=[bgl])
        wgv = wglu_in.h.rearrange("(c p) n -> p c n", p=128)
        for c in range(8):
            st_ = lst[c % 2]
            P.dma(st_[:].rearrange("p a b -> p (a b)"), wgv[:, c, :], reads=[wglu_in], writes=[st_])
            P.op(pool, lambda e: e.tensor_copy(out=wg[:, c, :], in_=st_[:].rearrange("p a b -> p (a b)")),
                 [st_], [wg])
        zt = [P.sb("zt", [128, T], BF16) for i in range(2)]
        gt = [P.sb("gt", [128, T], BF16) for i in range(2)]
        ygt = [P.sb("ygt", [128, 8, T], BF16) for i in range(2)]
        ygv = ygd.h.rearrange("(c p) t -> p c t", p=128)
        k = 0
        for ti in range(NT):
            t0 = ti * T
            yg_ = ygt[ti % 2]
            P.dma(yg_[:], ygv[:, :, t0:t0 + T], reads=[ygd], writes=[yg_])
            for nb in range(8):
                s_ = k % 2
                k += 1
                a = PS[k % 4]
                P.dma(zt[s_][:], zb[nb * 128:(nb + 1) * 128, t0:t0 + T], reads=[zb], writes=[zt[s_]])
                for jc in range(8):
                    P.op(pe, lambda e: e.matmul(a[:], lhsT=wg[:, jc, nb * 128:(nb + 1) * 128],
                                                rhs=yg_[:, jc, :], start=(jc == 0), stop=(jc == 7)),
                         [wg, yg_], [a])
                P.op(act, lambda e: e.activation(out=gt[s_][:], in_=a[:], func=AF.Sigmoid,
                                                 bias=bgl[:, nb:nb + 1]), [a, bgl], [gt[s_]])
                P.op(dve, lambda e: e.tensor_tensor(out=gt[s_][:], in0=gt[s_][:], in1=yg_[:, nb, :],
                                                    op=ALU.mult), [gt[s_], yg_], [gt[s_]])
                P.op(dve, lambda e: e.tensor_tensor(out=gt[s_][:], in0=gt[s_][:], in1=zt[s_][:],
                                                    op=ALU.mult), [gt[s_], zt[s_]], [gt[s_]])
                P.dma(mixT[1024 + nb * 128:1024 + (nb + 1) * 128, t0:t0 + T], gt[s_][:], reads=[gt[s_]],
                      writes=[mixT], q=act)
        P.phase_end()

    def back(w_out, resid, dst, final):
        P.phase_begin()
        wo = P.sb("wo", [128, NCH, D], BF16)
        wst = [P.sb("wst", [128, D], F32) for i in range(2)]
        mt = [P.sb("mt", [128, NCH, 512], BF16) for i in range(2)]
        xr = [P.sb("xr", [128, D], F32) for i in range(2)]
        xo = [P.sb("xo", [128, D], F32) for i in range(2)]
        wov = w_out.h.rearrange("(c p) n -> p c n", p=128)
        for c in range(NCH):
            P.dma(wst[c % 2][:], wov[:, c, :], reads=[w_out], writes=[wst[c % 2]])
            P.op(pool, lambda e: e.tensor_copy(out=wo[:, c, :], in_=wst[c % 2][:]), [wst[c % 2]], [wo])
        if final:
            gB = P.sb("gB", [128, D], F32)
            junk = P.sb("junk", [128, D], BF16)
            ss = [P.sb("ss", [128, 1], F32) for i in range(2)]
            rs = [P.sb("rs", [128, 1], F32) for i in range(2)]
            P.dma(gB[:], gf_in[0:1, :].partition_broadcast(128), reads=[gf_in], writes=[gB])
        mv = mixT.h.rearrange("(c p) t -> p c t", p=128)
        for tb in range(L // 128):
            s = tb % 2
            t0 = tb * 128
            ms = (tb // 4) % 2
            mo = (tb % 4) * 128
            if tb % 4 == 0:
                P.dma(mt[ms][:], mv[:, :, t0:t0 + 512], reads=[mixT], writes=[mt[ms]])
            P.dma(xr[s][:], resid[t0:t0 + 128, :], reads=[resid], writes=[xr[s]])
            for ng in range(4):
                a = PS[(tb % 2) * 4 + ng]
                for c in range(NCH):
                    P.op(pe, lambda e: e.matmul(a[:], lhsT=mt[ms][:, c, mo:mo + 128],
                                                rhs=wo[:, c, ng * 512:(ng + 1) * 512],
                                                start=(c == 0), stop=(c == NCH - 1)), [mt[ms], wo], [a])
                P.op(dve, lambda e: e.tensor_tensor(out=xo[s][:, ng * 512:(ng + 1) * 512], in0=a[:],
                                                    in1=xr[s][:, ng * 512:(ng + 1) * 512], op=ALU.add),
                     [a, xr[s]], [xo[s]])
            if final:
                P.op(act, lambda e: e.activation(out=junk[:], in_=xo[s][:], func=AF.Square,
                                                 accum_out=ss[s][:]), [xo[s]], [junk, ss[s]])
                P.op(dve, lambda e: e.tensor_scalar(out=rs[s][:], in0=ss[s][:], scalar1=1.0 / D,
                                                    scalar2=EPS, op0=ALU.mult, op1=ALU.add),
                     [ss[s]], [rs[s]])
                P.op(act, lambda e: e.activation(out=rs[s][:], in_=rs[s][:], func=AF.Sqrt),
                     [rs[s]], [rs[s]])
                P.op(dve, lambda e: e.reciprocal(out=rs[s][:], in_=rs[s][:]), [rs[s]], [rs[s]])
                P.op(dve, lambda e: e.scalar_tensor_tensor(out=xo[s][:], in0=xo[s][:],
                                                            scalar=rs[s][:, 0:1], in1=gB[:],
                                                            op0=ALU.mult, op1=ALU.mult),
                     [xo[s], rs[s], gB], [xo[s]])
            t = P.dma(dst[t0:t0 + 128, :], xo[s][:], reads=[xo[s]], writes=[dst], q=act)
            if final:
                P.out_toks.append(t)
        P.phase_end()

    if allp or "front0" in phases:
        front(x_in, w0_in, g0_in, [
            (0, 1024, qa, "fm", ("scale", 0.125)),
            (1024, 1024, ka, "fm", ("copy",)),
            (2048, 1024, va, "tm", ("copy",)),
            (3072, 1024, ut, "fm", ("copy",)),
            (4096, 1024, za, "fm", ("silu",)),
            (5120, 1024, zb, "fm", ("silu",)),
        ])
    if allp or "attn_a" in phases:
        attn_a()
    if allp or "s5" in phases:
        s5_glu()
    if allp or "back0" in phases:
        back(wo0_in, x_in, x1, False)
    if allp or "front1" in phases:
        sc = 1.0 / math.sqrt(128.0)
        front(x1, w1_in, g1_in, [
            (0, 2048, qc, "fm", ("scale", sc)),
            (2048, 2048, kc, "fm", ("copy",)),
            (4096, 2048, vc, "tm", ("copy",)),
            (6144, 2048, zc, "fm", ("silu",)),
        ], fcol=8192)
    if allp or "fox" in phases:
        fox()
    if allp or "back1" in phases:
        back(wo1_in, x1, out_d, True)

    name2buf = dict(qa=qa, ka=ka, va=va, ut=ut, za=za, zb=zb, mixT=mixT, ygd=ygd)
    if dbg:
        P.phase_begin()
        stage = P.sb("dbg_stage", [128, 4096], BF16)
        for nm in dbg:
            src = name2buf[nm]
            shp = src.h.shape
            dd = P.dram("dbg_" + nm, list(shp), BF16, kind="ExternalOutput")
            for r in range(0, shp[0], 128):
                for c in range(0, shp[1], 4096):
                    w = min(4096, shp[1] - c)
                    P.dma(stage[:, 0:w], src[r:r + 128, c:c + w], reads=[src], writes=[stage])
                    t = P.dma(dd[r:r + 128, c:c + w], stage[:, 0:w], reads=[stage], writes=[dd])
                    P.out_toks.append(t)
        P.phase_end()
    for t in P.out_toks:
        sp.wait(t)
    return P


def host_inputs(inp, b):
    f = np.float32
    A = lambda a: np.ascontiguousarray(a, dtype=f)
    m = {}
    m["x"] = A(inp["x"][b])
    m["ident"] = np.eye(128, dtype=f)
    sel = np.zeros((128, 128), f); sel[127, :] = 1
    m["sel127"] = sel
    m["tri"] = np.triu(np.ones((128, 128), f))
    o96 = np.zeros((96, 128), f); o96[[0, 32, 64], :] = 1
    m["ones96"] = o96
    m["g0"] = A(inp["norm_even_g"][0].reshape(16, 128).T)
    m["w0"] = A(inp["w_in_even"][0])
    m["g1"] = A(inp["norm_odd_g"][0].reshape(16, 128).T)
    m["w1"] = A(inp["w_in_odd"][0])
    m["bf"] = A(inp["b_forget"][0].reshape(16, 1))
    m["wo0"] = A(inp["w_out_even"][0])
    m["wo1"] = A(inp["w_out_odd"][0])
    m["gf"] = A(inp["final_norm_g"].reshape(1, 2048))
    rb = np.asarray(inp["rel_bias"][0])
    kl = np.arange(128)[:, None]; ql = np.arange(128)[None, :]
    ab = np.zeros((16, 3, 128, 128), f); am = np.zeros((3, 128, 128), f)
    for i, r in enumerate((0, 3, 4)):
        rel = (ql - kl) + 128 * (4 - r)
        idx = np.clip(rel, -128, 128) + 128
        ab[:, i] = rb[:, idx]
    am[0][(kl < 64) & (ql >= 64)] = -30000.0
    am[2][(kl >= 64) & (ql < 64)] = -30000.0
    m["abias"] = ab; m["amask"] = am
    m["abc"] = A(np.broadcast_to(rb[:, 256][None, :], (128, 16)))
    bre = np.asarray(inp["s5_b_re"][0]); bim = np.asarray(inp["s5_b_im"][0])
    cre = np.asarray(inp["s5_c_re"][0]); cim = np.asarray(inp["s5_c_im"][0])
    lv = np.zeros((2, 32, 128, 128), f); ly = np.zeros((2, 32, 128, 128), f); lvt = np.zeros((2, 32, 128, 128), f)
    slr = np.zeros((128, 32), f); sli = np.zeros((128, 32), f); sldt = np.zeros((128, 32), f)
    dsk = np.zeros((128, 8), f)
    lre = np.asarray(inp["s5_lambda_re"][0]); lim = np.asarray(inp["s5_lambda_im"][0])
    ldt = np.asarray(inp["s5_log_dt"][0]); dsp = np.asarray(inp["s5_d"][0])
    for g in range(64):
        gp = g // 2; slot = g % 2; gl = g % 8
        rs = slice(16 * gl, 16 * gl + 16); ps_ = slice(64 * slot, 64 * slot + 64)
        lv[0, gp, rs, ps_] = bre[g].T
        lv[1, gp, rs, ps_] = bim[g].T
        lvt[0, gp, ps_, rs] = bre[g]
        lvt[1, gp, ps_, rs] = bim[g]
        ly[0, gp, ps_, rs] = cre[g].T
        ly[1, gp, ps_, rs] = cim[g].T
        slr[ps_, gp] = lre[g]; sli[ps_, gp] = lim[g]; sldt[ps_, gp] = ldt[g]
        dsk[rs, g // 8] = dsp[g]
    m.update(lv=lv, ly=ly, lvt=lvt, slr=slr, sli=sli, sldt=sldt, dsk=dsk)
    m["iota1"] = A(np.broadcast_to(np.arange(1, 513, dtype=f)[None, :], (128, 512)))
    m["wglu"] = A(inp["w_glu"][0])
    m["bglu"] = A(inp["b_glu"][0].reshape(8, 128).T)
    return m


_PROG = None


def kernel(**inputs):
    global _PROG
    inp = {k: np.asarray(v) for k, v in inputs.items()}
    if _PROG is None:
        _PROG = build()
    nb = inp["x"].shape[0]
    in_maps = [host_inputs(inp, b) for b in range(nb)]
    res = run_bass_kernel_spmd(_PROG.nc, in_maps, core_ids=list(range(nb)))
    return np.stack([np.asarray(r["out"]) for r in res.results], axis=0).astype(np.float32)
```

```python
import math
from contextlib import ExitStack
import numpy as np
import concourse.bass as bass
import concourse.mybir as mybir
from concourse.bass_utils import run_bass_kernel_spmd

F32 = mybir.dt.float32
BF16 = mybir.dt.bfloat16
AF = mybir.ActivationFunctionType
ALU = mybir.AluOpType

D = 2048
L = 4096
NCH = D // 128
EPS = 1e-6
SAME_ENG_SYNC = True
DMA_RING = 12


class Buf:
    def __init__(self, name, h):
        self.name = name
        self.h = h
        self.last_write = None
        self.reads = {}

    def __getitem__(self, idx):
        return self.h[idx]


class Eng:
    def __init__(self, nc, handle, name, is_pe=False):
        self.nc = nc
        self.h = handle
        self.name = name
        self.sem = nc.alloc_semaphore("sem_" + name)
        self.cnt = 0
        self.waited = {}
        self.is_pe = is_pe
        self.ring = None
        self.dma_i = 0

    def wait(self, tok):
        if tok is None:
            return
        sem, val = tok
        if self.waited.get(sem.num, 0) >= val:
            return
        self.h.wait_ge(sem, val)
        self.waited[sem.num] = val


class Prog:
    def __init__(self):
        nc = bass.Bass("TRN2", target_bir_lowering=False)
        self.nc = nc
        self.pe = Eng(nc, nc.tensor, "pe", is_pe=True)
        self.act = Eng(nc, nc.scalar, "act")
        self.dve = Eng(nc, nc.vector, "dve")
        self.pool = Eng(nc, nc.gpsimd, "pool")
        self.sp = Eng(nc, nc.sync, "sp")
        self.nbuf = 0
        self.stack = []
        self.out_toks = []

    def sb(self, name, shape, dt):
        self.nbuf += 1
        nm = "%s_%d" % (name, self.nbuf)
        if self.stack:
            h = self.stack[-1].enter_context(self.nc.sbuf_tensor(nm, list(shape), dt))
            return Buf(name, h)
        return Buf(name, self.nc.alloc_sbuf_tensor(nm, list(shape), dt))

    def phase_begin(self):
        self.stack.append(ExitStack())

    def phase_end(self):
        self.barrier()
        self.stack.pop().close()

    def barrier(self):
        engs = [self.pe, self.act, self.dve, self.pool, self.sp]
        toks = [(e.sem, e.cnt) for e in engs if e.cnt > 0]
        for q in engs:
            if q.ring is not None:
                for i in range(max(0, q.dma_i - DMA_RING), q.dma_i):
                    toks.append((q.ring[i % DMA_RING], 16 * (i // DMA_RING + 1)))
        for e in engs:
            for t in toks:
                if t[0].num != e.sem.num:
                    e.wait(t)

    def ps(self, name, shape, dt=F32):
        return Buf(name, self.nc.alloc_psum_tensor(name, list(shape), dt))

    def dram(self, name, shape, dt, kind="Internal"):
        t = self.nc.dram_tensor(name, list(shape), dt, kind=kind)
        return Buf(name, t.ap())

    def _deps(self, eng, reads, writes):
        toks = []
        for b in reads:
            if b.last_write is not None:
                toks.append(b.last_write)
        for b in writes:
            if b.last_write is not None:
                toks.append(b.last_write)
            toks.extend(b.reads.values())
        for t in toks:
            if t[0].num == eng.sem.num and (eng.is_pe or not SAME_ENG_SYNC):
                continue
            eng.wait(t)

    def _mark(self, tok, reads, writes):
        for b in reads:
            b.reads[tok[0].num] = tok
        for b in writes:
            b.last_write = tok
            b.reads = {}

    def op(self, eng, fn, reads=(), writes=()):
        self._deps(eng, reads, writes)
        ins = fn(eng.h)
        eng.cnt += 1
        ins.then_inc(eng.sem, 1)
        tok = (eng.sem, eng.cnt)
        self._mark(tok, reads, writes)
        return tok

    def dma(self, out, in_, reads=(), writes=(), q=None, **kw):
        q = q or self.sp
        if q.ring is None:
            q.ring = [self.nc.alloc_semaphore("dq_%s_%d" % (q.name, i)) for i in range(DMA_RING)]
        i = q.dma_i
        q.dma_i += 1
        sem = q.ring[i % DMA_RING]
        val = 16 * (i // DMA_RING + 1)
        if i >= DMA_RING:
            q.wait((sem, val - 16))
        self._deps(q, reads, writes)
        q.h.dma_start(out=out, in_=in_, **kw).then_inc(sem, 16)
        tok = (sem, val)
        self._mark(tok, reads, writes)
        return tok


class PSView:
    pass


def build(phases=("all",), dbg=(), x1_is_input=False):
    P = Prog()
    nc = P.nc
    pe, act, dve, pool, sp = P.pe, P.act, P.dve, P.pool, P.sp
    allp = "all" in phases

    def ein(name, shape):
        return P.dram(name, shape, F32, kind="ExternalInput")

    x_in = ein("x", [L, D])
    ident_in = ein("ident", [128, 128])
    sel_in = ein("sel127", [128, 128])
    tri_in = ein("tri", [128, 128])
    ones96_in = ein("ones96", [96, 128])
    g0_in = ein("g0", [128, NCH])
    w0_in = ein("w0", [48, 128, NCH, 128])
    g1_in = ein("g1", [128, NCH])
    w1_in = ein("w1", [65, 128, NCH, 128])
    bf_in = ein("bf", [16, 1])
    wo0_in = ein("wo0", [D, D])
    wo1_in = ein("wo1", [D, D])
    gf_in = ein("gf", [1, D])
    abias_in = ein("abias", [16, 3, 128, 128])
    amask_in = ein("amask", [3, 128, 128])
    abc_in = ein("abc", [128, 16])
    lv_in = ein("lv", [2, 32, 128, 128])
    ly_in = ein("ly", [2, 32, 128, 128])
    lvt_in = ein("lvt", [2, 32, 128, 128])
    slr_in = ein("slr", [128, 32])
    sli_in = ein("sli", [128, 32])
    sldt_in = ein("sldt", [128, 32])
    dsk_in = ein("dsk", [128, 8])
    iota1_in = ein("iota1", [128, 512])
    wglu_in = ein("wglu", [1024, 1024])
    bglu_in = ein("bglu", [128, 8])
    out_d = P.dram("out", [L, D], F32, kind="ExternalOutput")

    x1 = ein("x1", [L, D]) if x1_is_input else P.dram("x1", [L, D], F32)
    mixT = P.dram("mixT", [D, L], BF16)
    qa = P.dram("qa", [1024, L], BF16)
    ka = P.dram("ka", [1024, L], BF16)
    va = P.dram("va", [L, 1024], BF16)
    ut = P.dram("ut", [1024, L], BF16)
    za = P.dram("za", [1024, L], BF16)
    zb = P.dram("zb", [1024, L], BF16)
    ygd = P.dram("ygd", [1024, L], BF16)
    qc = P.dram("qc", [D, L], BF16)
    kc = P.dram("kc", [D, L], BF16)
    vc = P.dram("vc", [L, D], BF16)
    zc = P.dram("zc", [D, L], BF16)

    ident_f = P.sb("ident_f", [128, 128], F32)
    ident_b = P.sb("ident_b", [128, 128], BF16)
    sel_f = P.sb("sel_f", [128, 128], F32)
    tri_f = P.sb("tri_f", [128, 128], F32)
    tri_b = P.sb("tri_b", [128, 128], BF16)
    flog = P.sb("flog", [16, L], F32)
    P.dma(ident_f[:], ident_in[:], reads=[ident_in], writes=[ident_f])
    P.dma(sel_f[:], sel_in[:], reads=[sel_in], writes=[sel_f])
    P.dma(tri_f[:], tri_in[:], reads=[tri_in], writes=[tri_f])
    P.op(dve, lambda e: e.tensor_copy(out=ident_b[:], in_=ident_f[:]), [ident_f], [ident_b])
    P.op(dve, lambda e: e.tensor_copy(out=tri_b[:], in_=tri_f[:]), [tri_f], [tri_b])
    PS = [P.ps("bank%d" % i, [128, 512], F32) for i in range(8)]

    def front(src, w_in, g_in, specs, fcol=None):
        P.phase_begin()
        ST = 2048
        xt = [P.sb("xt", [128, D], F32) for i in range(2)]
        xn = [P.sb("xn", [128, D], BF16) for i in range(2)]
        ss = [P.sb("ss", [128, 1], F32) for i in range(2)]
        rstd = [P.sb("rstd", [128, 1], F32) for i in range(2)]
        hnTs = [P.sb("hnT", [128, NCH, ST], BF16) for i in range(2)]
        gfull = P.sb("gfull", [128, NCH, 128], F32)
        gcol = P.sb("gcol", [128, NCH], F32)
        wf = [P.sb("wf", [128, NCH, 128], F32) for i in range(2)]
        wb = [P.sb("wb", [128, NCH, 128], BF16) for i in range(2)]
        ob = [P.sb("ob", [128, 512], BF16) for i in range(4)]
        accb = PS[2:8]
        nacc = [0]
        tpb = [PS[0], PS[1]]

        def nt_pre(t0, i):
            s = i % 2
            P.dma(xt[s][:], src[t0:t0 + 128, :], reads=[src], writes=[xt[s]])
            P.op(act, lambda e: e.activation(out=xn[s][:], in_=xt[s][:], func=AF.Square,
                                             accum_out=ss[s][:]), [xt[s]], [xn[s], ss[s]])
            P.op(dve, lambda e: e.tensor_scalar(out=rstd[s][:], in0=ss[s][:], scalar1=1.0 / D,
                                                scalar2=EPS, op0=ALU.mult, op1=ALU.add),
                 [ss[s]], [rstd[s]])
            P.op(act, lambda e: e.activation(out=rstd[s][:], in_=rstd[s][:], func=AF.Sqrt),
                 [rstd[s]], [rstd[s]])
            P.op(dve, lambda e: e.reciprocal(out=rstd[s][:], in_=rstd[s][:]), [rstd[s]], [rstd[s]])
            P.op(act, lambda e: e.activation(out=xn[s][:], in_=xt[s][:], func=AF.Copy,
                                             scale=rstd[s][:, 0:1]), [xt[s], rstd[s]], [xn[s]])

        def nt_post(tl, i, hb):
            s = i % 2
            for hf in range(2):
                tv = tpb[hf][:].bitcast(BF16)
                for c8 in range(8):
                    c = hf * 8 + c8
                    P.op(pe, lambda e: e.transpose(out=tv[:, c8 * 128:(c8 + 1) * 128],
                                                   in_=xn[s][:, c * 128:(c + 1) * 128],
                                                   identity=ident_b[:]), [xn[s], ident_b], [tpb[hf]])
                P.op(dve, lambda e: e.tensor_copy(
                    out=hb[:, hf * 8:(hf + 1) * 8, tl:tl + 128],
                    in_=tv.rearrange("p (c t) -> p c t", c=8)), [tpb[hf]], [hb])

        P.dma(gcol[:], g_in[:], reads=[g_in], writes=[gcol])
        for c in range(NCH):
            P.op(dve, lambda e: e.tensor_copy(out=gfull[:, c, :],
                                              in_=gcol[:, c:c + 1].to_broadcast([128, 128])),
                 [gcol], [gfull])
        blocks = []
        for (c0, nco, dst, kind, evac) in specs:
            for cb in range(nco // 128):
                blocks.append((c0 + cb * 128, cb * 128, dst, kind, evac))
        nblk = len(blocks)
        gblk = [0]
        NP = ST // 128
        NSUP = L // ST

        wsc = P.dram("wsc_%d" % id(w_in), [nblk, 128, NCH * 128], BF16)
        cur_sup = [0]

        def wload(bi):
            s_ = gblk[0] % 2
            gblk[0] += 1
            cc = blocks[bi][0]
            if cur_sup[0] == 0:
                P.dma(wf[s_][:, 0:8, :], w_in.h[cc // 128][:, 0:8, :], reads=[w_in], writes=[wf[s_]])
                P.dma(wf[s_][:, 8:16, :], w_in.h[cc // 128][:, 8:16, :], reads=[w_in], writes=[wf[s_]], q=act)
                P.op(pool, lambda e: e.tensor_tensor(out=wb[s_][:], in0=wf[s_][:], in1=gfull[:],
                                                     op=ALU.mult), [wf[s_], gfull], [wb[s_]])
                if NSUP > 1:
                    P.dma(wsc.h[bi], wb[s_][:].rearrange("p c n -> p (c n)"), reads=[wb[s_]], writes=[wsc], q=pool)
            else:
                P.dma(wb[s_][:].rearrange("p c n -> p (c n)"), wsc.h[bi], reads=[wsc], writes=[wb[s_]])
            return s_

        for i in range(NP):
            nt_pre(i * 128, i)
            nt_post(i * 128, i, hnTs[0])
        for sup in range(NSUP):
            cur_sup[0] = sup
            hnT = hnTs[sup % 2]
            hnN = hnTs[(sup + 1) % 2]
            more = sup + 1 < NSUP
            nxt = wload(0)
            for bi in range(nblk):
                (cc, r0, dst, kind, evac) = blocks[bi]
                s = nxt
                if bi + 1 < nblk:
                    nxt = wload(bi + 1)
                if more and bi < NP:
                    nt_pre((sup + 1) * ST + bi * 128, bi)
                NT4 = ST // 512
                for tt in range(NT4):
                    a = accb[nacc[0] % 6]
                    nacc[0] += 1
                    o = ob[tt % 4]
                    tg = sup * ST + tt * 512
                    if kind == "fm":
                        for c in range(NCH):
                            P.op(pe, lambda e: e.matmul(a[:], lhsT=wb[s][:, c, :],
                                                        rhs=hnT[:, c, tt * 512:(tt + 1) * 512],
                                                        start=(c == 0), stop=(c == NCH - 1)),
                                 [wb[s], hnT], [a])
                    else:
                        for j in range(4):
                            for c in range(NCH):
                                P.op(pe, lambda e: e.matmul(
                                    a[:, j * 128:(j + 1) * 128],
                                    lhsT=hnT[:, c, tt * 512 + j * 128: tt * 512 + (j + 1) * 128],
                                    rhs=wb[s][:, c, :], start=(c == 0), stop=(c == NCH - 1)),
                                    [wb[s], hnT], [a])
                    if evac[0] == "copy":
                        P.op(act, lambda e: e.activation(out=o[:], in_=a[:], func=AF.Copy), [a], [o])
                    elif evac[0] == "scale":
                        P.op(act, lambda e: e.activation(out=o[:], in_=a[:], func=AF.Copy,
                                                         scale=evac[1]), [a], [o])
                    elif evac[0] == "silu":
                        P.op(act, lambda e: e.activation(out=o[:], in_=a[:], func=AF.Silu),
                             [a], [o])
                    if kind == "fm":
                        P.dma(dst[r0:r0 + 128, tg:tg + 512], o[:], reads=[o], writes=[dst], q=act)
                    else:
                        P.dma(dst.h[tg:tg + 512, r0:r0 + 128].rearrange("(j p) n -> p j n", p=128),
                              o[:].rearrange("p (j n) -> p j n", j=4), reads=[o], writes=[dst], q=act)
                if more and 1 <= bi <= NP:
                    nt_post((bi - 1) * 128, bi - 1, hnN)
            if fcol is not None:
                s = gblk[0] % 2
                gblk[0] += 1
                P.dma(wf[s][:, :, 0:16], w_in.h[fcol // 128][:, :, 0:16], reads=[w_in], writes=[wf[s]])
                P.op(pool, lambda e: e.tensor_tensor(out=wb[s][:, :, 0:16], in0=wf[s][:, :, 0:16],
                                                     in1=gfull[:, :, 0:16], op=ALU.mult),
                     [wf[s], gfull], [wb[s]])
                for tt in range(ST // 512):
                    a = accb[nacc[0] % 6]
                    nacc[0] += 1
                    tg = sup * ST + tt * 512
                    for c in range(NCH):
                        P.op(pe, lambda e: e.matmul(a[0:16, :], lhsT=wb[s][:, c, 0:16],
                                                    rhs=hnT[:, c, tt * 512:(tt + 1) * 512],
                                                    start=(c == 0), stop=(c == NCH - 1)),
                             [wb[s], hnT], [a])
                    P.op(dve, lambda e: e.tensor_copy(out=flog[:, tg:tg + 512], in_=a[0:16, :]),
                         [a], [flog])
        P.phase_end()

    def fox():
        P.phase_begin()
        NB = L // 128
        nbf = P.sb("nbf", [16, 1], F32)
        onec = P.sb("onec", [16, 1], F32)
        cpos = P.sb("cpos", [16, L], F32)
        cT = P.sb("cT", [128, NB, 16], F32)
        crefB = P.sb("crefB", [128, NB, 16], F32)
        P.dma(nbf[:], bf_in[:], reads=[bf_in], writes=[nbf])
        P.op(dve, lambda e: e.tensor_scalar(out=nbf[:], in0=nbf[:], scalar1=-1.0, scalar2=None,
                                            op0=ALU.mult), [nbf], [nbf])
        P.op(dve, lambda e: e.memset(onec[:], 1.0), [], [onec])
        P.op(act, lambda e: e.activation(out=flog[:], in_=flog[:], func=AF.Exp, scale=-1.0,
                                         bias=nbf[:, 0:1]), [flog, nbf], [flog])
        P.op(act, lambda e: e.activation(out=flog[:], in_=flog[:], func=AF.Ln, bias=onec[:, 0:1]),
             [flog, onec], [flog])
        P.op(dve, lambda e: e.tensor_tensor_scan(out=cpos[:], data0=onec[:, 0:1].to_broadcast([16, L]),
                                                 data1=flog[:], initial=0.0, op0=ALU.mult,
                                                 op1=ALU.add), [flog, onec], [cpos])
        for kb in range(NB):
            P.op(pe, lambda e: e.transpose(out=PS[0][:, kb * 16:(kb + 1) * 16],
                                           in_=cpos[0:16, kb * 128:(kb + 1) * 128],
                                           identity=ident_f[0:16, 0:16]), [cpos, ident_f], [PS[0]])
        P.op(dve, lambda e: e.tensor_copy(out=cT[:].rearrange("p a b -> p (a b)"), in_=PS[0][:]),
             [PS[0]], [cT])
        P.op(pe, lambda e: e.matmul(PS[1][:], lhsT=sel_f[:], rhs=cT[:].rearrange("p a b -> p (a b)"),
                                    start=True, stop=True), [sel_f, cT], [PS[1]])
        P.op(dve, lambda e: e.tensor_copy(out=crefB[:].rearrange("p a b -> p (a b)"), in_=PS[1][:]),
             [PS[1]], [crefB])

        kh = [P.sb("kh", [128, L], BF16) for i in range(2)]
        qh = [P.sb("qh", [128, L], BF16) for i in range(2)]
        zh = [P.sb("zh", [128, L], BF16) for i in range(2)]
        vh = [P.sb("vh", [128, NB, 129], BF16) for i in range(2)]
        mixh = [P.sb("mixh", [128, L], BF16) for i in range(2)]
        cr3 = [P.sb("cr3", [96, NB, 128], BF16) for i in range(2)]
        o96f = P.sb("o96f", [96, 128], F32)
        o96 = P.sb("o96", [96, 128], BF16)
        hiT = P.sb("hiT", [128, NB], BF16)
        miT = P.sb("miT", [128, NB], BF16)
        loT = P.sb("loT", [128, NB], BF16)
        r1 = P.sb("r1", [128, NB], F32)
        r2 = P.sb("r2", [128, NB], F32)
        pT = [P.sb("pT", [128, 512], BF16) for i in range(4)]
        rden = [P.sb("rden", [128, 1], F32) for i in range(8)]
        on = [P.sb("on", [128, 128], BF16) for i in range(8)]
        P.dma(o96f[:], ones96_in[:], reads=[ones96_in], writes=[o96f])
        P.op(dve, lambda e: e.tensor_copy(out=o96[:], in_=o96f[:]), [o96f], [o96])
        for i in range(2):
            P.op(pool, lambda e: e.memset(vh[i][:, :, 128:129], 1.0), [], [vh[i]])
            P.op(pool, lambda e: e.memset(cr3[i][:], 0.0), [], [cr3[i]])
        sT = [PS[0], PS[1]]
        oacc = [[PS[2], PS[3]], [PS[4], PS[5]]]
        oTb = [PS[6], PS[7]]
        npt = 0
        nst = 0

        def fox_load(h_):
            s_ = h_ % 2
            r_ = h_ * 128
            P.dma(kh[s_][:], kc[r_:r_ + 128, :], reads=[kc], writes=[kh[s_]])
            P.dma(qh[s_][:], qc[r_:r_ + 128, :], reads=[qc], writes=[qh[s_]])
            P.dma(vh[s_][:, :, 0:128], vc.h[:, r_:r_ + 128].rearrange("(kb p) n -> p kb n", p=128),
                  reads=[vc], writes=[vh[s_]])
            P.dma(zh[s_][:], zc[r_:r_ + 128, :], reads=[zc], writes=[zh[s_]])

        fox_load(0)
        for h in range(16):
            s = h % 2
            r0 = h * 128
            if h + 1 < 16:
                fox_load(h + 1)
            cb_ = crefB[:, :, h]
            P.op(dve, lambda e: e.tensor_scalar(out=hiT[:], in0=cb_, scalar1=-1.0, scalar2=None, op0=ALU.mult),
                 [crefB], [hiT])
            P.op(dve, lambda e: e.scalar_tensor_tensor(out=r1[:], in0=cb_, scalar=-1.0, in1=hiT[:],
                                                       op0=ALU.mult, op1=ALU.subtract), [crefB, hiT], [r1])
            P.op(dve, lambda e: e.tensor_copy(out=miT[:], in_=r1[:]), [r1], [miT])
            P.op(dve, lambda e: e.tensor_tensor(out=r2[:], in0=r1[:], in1=miT[:], op=ALU.subtract), [r1, miT], [r2])
            P.op(dve, lambda e: e.tensor_copy(out=loT[:], in_=r2[:]), [r2], [loT])
            for (row, src_) in ((0, hiT), (32, miT), (64, loT)):
                P.op(dve, lambda e: e.tensor_copy(
                    out=cr3[s][row:row + 1, :, :],
                    in_=src_[row:row + 1, :].unsqueeze(2).to_broadcast([1, NB, 128])), [src_], [cr3[s]])
            cr3f = cr3[s][:].rearrange("p a b -> p (a b)")
            items = [(qg, kb) for qg in range(NB // 4) for kb in range(4 * qg + 4)]

            def emit_qk(i_):
                qg_, kb_ = items[i_]
                q0_ = max(4 * qg_, kb_)
                nq_ = 4 * qg_ + 4 - q0_
                st_ = sT[i_ % 2]
                P.op(pe, lambda e: e.matmul(st_[:, 0:nq_ * 128],
                                            lhsT=kh[s][:, kb_ * 128:(kb_ + 1) * 128],
                                            rhs=qh[s][:, q0_ * 128:(q0_ + nq_) * 128],
                                            start=True, stop=False), [kh[s], qh[s]], [st_])
                P.op(pe, lambda e: e.matmul(st_[:, 0:nq_ * 128], lhsT=o96[:],
                                            rhs=cr3f[:, q0_ * 128:(q0_ + nq_) * 128],
                                            start=False, stop=True), [o96, cr3[s]], [st_])

            def f_norm(qg_):
                par_ = qg_ % 2
                for jj in range(4):
                    ob_ = oacc[par_][jj // 2]
                    c0 = (jj % 2) * 256
                    rd_ = rden[par_ * 4 + jj]
                    on_ = on[par_ * 4 + jj]
                    P.op(dve, lambda e: e.reciprocal(out=rd_[:], in_=ob_[:, c0 + 128:c0 + 129]), [ob_], [rd_])
                    P.op(dve, lambda e: e.tensor_scalar(out=on_[:], in0=ob_[:, c0:c0 + 128],
                                                        scalar1=rd_[:, 0:1], scalar2=None,
                                                        op0=ALU.mult), [ob_, rd_], [on_])

            def f_fin(qg_):
                par_ = qg_ % 2
                otb = oTb[par_]
                otv = otb[:].bitcast(BF16)
                for jj in range(4):
                    on_ = on[par_ * 4 + jj]
                    P.op(pe, lambda e: e.transpose(out=otv[:, jj * 128:(jj + 1) * 128], in_=on_[:],
                                                   identity=ident_b[:]), [on_, ident_b], [otb])
                t0 = qg_ * 512
                P.op(dve, lambda e: e.tensor_tensor(out=mixh[s][:, t0:t0 + 512], in0=otv[:, 0:512],
                                                    in1=zh[s][:, t0:t0 + 512], op=ALU.mult),
                     [otb, zh[s]], [mixh[s]])

            emit_qk(0)
            for i_, (qg, kb) in enumerate(items):
                par = qg % 2
                q0 = max(4 * qg, kb)
                nq = 4 * qg + 4 - q0
                st = sT[i_ % 2]
                if i_ + 1 < len(items):
                    emit_qk(i_ + 1)
                p = pT[npt % 4]
                npt += 1
                P.op(act, lambda e: e.activation(out=p[:, 0:nq * 128], in_=st[:, 0:nq * 128],
                                                 func=AF.Exp, bias=cT[:, kb, h:h + 1]), [st, cT], [p])
                if kb >= 4 * qg:
                    P.op(pool, lambda e: e.tensor_tensor(out=p[:, 0:128], in0=p[:, 0:128], in1=tri_b[:],
                                                         op=ALU.mult), [p, tri_b], [p])
                for j in range(nq):
                    qb = q0 + j
                    jj = qb - 4 * qg
                    ob_ = oacc[par][jj // 2]
                    P.op(pe, lambda e: e.matmul(ob_[:, (jj % 2) * 256:(jj % 2) * 256 + 129],
                                                lhsT=p[:, j * 128:(j + 1) * 128], rhs=vh[s][:, kb, :],
                                                start=(kb == 0), stop=(kb == qb)),
                         [p, vh[s]], [ob_])
                if kb == 4 * qg + 3:
                    f_norm(qg)
                    if qg > 0:
                        f_fin(qg - 1)
            f_fin(NB // 4 - 1)
            P.dma(mixT[r0:r0 + 128, :], mixh[s][:], reads=[mixh[s]], writes=[mixT], q=pool)
        P.phase_end()

    def attn_a():
        P.phase_begin()
        NB = L // 128
        abc = P.sb("abc", [128, 16], F32)
        am = P.sb("am", [128, 3, 128], F32)
        P.dma(abc[:], abc_in[:], reads=[abc_in], writes=[abc])
        P.dma(am[:], amask_in.h.rearrange("r k q -> k r q"), reads=[amask_in], writes=[am])
        abt = [P.sb("abt", [128, 3, 128], F32) for i in range(2)]
        zero_c = P.sb("zero_c", [128, 128], F32)
        P.op(dve, lambda e: e.memset(zero_c[:], 0.0), [], [zero_c])
        E = [P.sb("E", [128, 5, 2, 128], BF16) for i in range(2)]
        khp = [P.sb("khp", [128, L], BF16) for i in range(2)]
        qhp = [P.sb("qz", [128, 2, L], BF16) for i in range(2)]
        for i in range(2):
            P.op(pool, lambda e: e.memset(qhp[i][64:128, 0, :], 0.0), [], [qhp[i]])
            P.op(pool, lambda e: e.memset(qhp[i][0:64, 1, :], 0.0), [], [qhp[i]])
        zhp = [P.sb("zhp", [128, L], BF16) for i in range(2)]
        vhp = [P.sb("vhp", [128, NB, 2, 65], BF16) for i in range(2)]
        mixh = [P.sb("mixh", [128, L], BF16) for i in range(2)]
        pA = [[P.sb("pA", [128, 512], BF16) for h2 in range(2)] for i in range(2)]
        pB = [P.sb("pB", [128, 256], BF16) for i in range(2)]
        rden = [P.sb("rden", [128, 1], F32) for i in range(4)]
        on = [P.sb("on", [128, 128], BF16) for i in range(2)]
        for i in range(2):
            P.op(pool, lambda e: e.memset(vhp[i][:, :, :, 64:65], 1.0), [], [vhp[i]])
        sA = [[PS[0], PS[1]], [PS[2], PS[3]]]
        sBb = PS[4]
        oTb = PS[5]
        oacc = [PS[6], PS[7]]

        def a_load(hp_):
            s_ = hp_ % 2
            r_ = hp_ * 128
            P.dma(khp[s_][:], ka[r_:r_ + 128, :], reads=[ka], writes=[khp[s_]])
            P.dma(qhp[s_][0:64, 0, :], qa[r_:r_ + 64, :], reads=[qa], writes=[qhp[s_]])
            P.dma(qhp[s_][64:128, 1, :], qa[r_ + 64:r_ + 128, :], reads=[qa], writes=[qhp[s_]])
            for h2_ in range(2):
                P.dma(vhp[s_][:, :, h2_, 0:64],
                      va.h[:, r_ + h2_ * 64:r_ + (h2_ + 1) * 64].rearrange("(kb p) n -> p kb n", p=128),
                      reads=[va], writes=[vhp[s_]])
            P.dma(zhp[s_][:], za[r_:r_ + 128, :], reads=[za], writes=[zhp[s_]])

        NHP = 8
        NM = NB
        a_load(0)
        for hp in range(NHP):
            s = hp % 2
            r0 = hp * 128
            if hp + 1 < NHP:
                a_load(hp + 1)
            for h2 in range(2):
                h = hp * 2 + h2
                P.dma(abt[h2][:], abias_in.h[h].rearrange("r k q -> k r q"), reads=[abias_in],
                      writes=[abt[h2]])
                P.op(dve, lambda e: e.tensor_tensor(out=abt[h2][:], in0=abt[h2][:], in1=am[:],
                                                    op=ALU.add), [abt[h2], am], [abt[h2]])
                for (ei, r) in ((0, 0), (1, 3), (2, 4)):
                    P.op(act, lambda e: e.activation(out=E[s][:, r, h2, :], in_=abt[h2][:, ei, :], func=AF.Exp),
                         [abt[h2]], [E[s]])
                for r in (1, 2):
                    P.op(act, lambda e: e.activation(out=E[s][:, r, h2, :], in_=zero_c[:], func=AF.Exp,
                                                     bias=abc[:, h:h + 1]), [zero_c, abc], [E[s]])
            def emit_qk(m_):
                par_ = m_ % 2
                rlo_ = max(0, 4 - m_)
                rq_ = qhp[s][:, :, m_ * 128:(m_ + 1) * 128]
                for r_ in range(rlo_, 4):
                    kb_ = m_ - 4 + r_
                    sa_ = sA[par_][r_ // 2]
                    c_ = (r_ % 2) * 256
                    P.op(pe, lambda e: e.matmul(sa_[:, c_:c_ + 256], lhsT=khp[s][:, kb_ * 128:(kb_ + 1) * 128],
                                                rhs=rq_, start=True, stop=True), [khp[s], qhp[s]], [sa_])
                P.op(pe, lambda e: e.matmul(sBb[:, par_ * 256:(par_ + 1) * 256],
                                            lhsT=khp[s][:, m_ * 128:(m_ + 1) * 128],
                                            rhs=rq_, start=True, stop=True), [khp[s], qhp[s]], [sBb])

            def a_fin(m_):
                par_ = m_ % 2
                otv_ = oTb[:].bitcast(BF16)[:, par_ * 128:(par_ + 1) * 128]
                P.op(pe, lambda e: e.transpose(out=otv_, in_=on[par_][:], identity=ident_b[:]),
                     [on[par_], ident_b], [oTb])
                t0_ = m_ * 128
                P.op(dve, lambda e: e.tensor_tensor(out=mixh[s][:, t0_:t0_ + 128], in0=otv_,
                                                    in1=zhp[s][:, t0_:t0_ + 128], op=ALU.mult),
                     [oTb, zhp[s]], [mixh[s]])

            emit_qk(0)
            for m in range(NM):
                par = m % 2
                ob_ = oacc[par]
                rlo = max(0, 4 - m)
                if m + 1 < NM:
                    emit_qk(m + 1)
                for bk in range(2):
                    r0_ = max(rlo, 2 * bk)
                    if r0_ > 2 * bk + 1:
                        continue
                    cs_ = slice((r0_ - 2 * bk) * 256, 512)
                    sa = sA[par][bk]
                    pa = pA[par][bk]
                    P.op(act, lambda e: e.activation(out=pa[:, cs_], in_=sa[:, cs_], func=AF.Exp), [sa], [pa])
                    P.op(dve, lambda e: e.tensor_tensor(
                        out=pa[:, cs_], in0=pa[:, cs_],
                        in1=E[s][:, r0_:2 * bk + 2, :, :].rearrange("p a b c -> p (a b c)"), op=ALU.mult),
                        [pa, E[s]], [pa])
                pb = pB[par]
                P.op(act, lambda e: e.activation(out=pb[:], in_=sBb[:, par * 256:(par + 1) * 256], func=AF.Exp), [sBb], [pb])
                P.op(pool, lambda e: e.tensor_tensor(out=pb[:], in0=pb[:],
                                                     in1=E[s][:, 4, :, :].rearrange("p b c -> p (b c)"), op=ALU.mult),
                     [pb, E[s]], [pb])
                for h2 in range(2):
                    for r in range(rlo, 5):
                        kb = m - 4 + r
                        if r < 4:
                            pa = pA[par][r // 2]
                            c_ = ((r % 2) * 2 + h2) * 128
                            lhs = pa[:, c_:c_ + 128]
                        else:
                            pa = pb
                            lhs = pb[:, h2 * 128:(h2 + 1) * 128]
                        P.op(pe, lambda e: e.matmul(ob_[:, h2 * 128:h2 * 128 + 65], lhsT=lhs,
                                                    rhs=vhp[s][:, kb, h2, :], start=(r == rlo), stop=(r == 4)),
                             [pa, vhp[s]], [ob_])
                for h2 in range(2):
                    rd = rden[par * 2 + h2]
                    P.op(dve, lambda e: e.reciprocal(out=rd[:], in_=ob_[:, h2 * 128 + 64:h2 * 128 + 65]),
                         [ob_], [rd])
                    P.op(dve, lambda e: e.tensor_scalar(out=on[par][:, h2 * 64:(h2 + 1) * 64],
                                                        in0=ob_[:, h2 * 128:h2 * 128 + 64],
                                                        scalar1=rd[:, 0:1], scalar2=None, op0=ALU.mult),
                         [ob_, rd], [on[par]])
                if m > 0:
                    a_fin(m - 1)
            a_fin(NM - 1)
            P.dma(mixT[r0:r0 + 128, :], mixh[s][:], reads=[mixh[s]], writes=[mixT], q=pool)
        P.phase_end()

    def s5_glu():
        P.phase_begin()
        TWO_PI = 2.0 * math.pi
        T = 512
        NT = L // T
        P.phase_begin()
        iota1 = P.sb("iota1", [128, T], F32)
        pic = P.sb("pic", [128, 1], F32)
        dsk = P.sb("dsk", [128, 8], F32)
        P.dma(iota1[:], iota1_in[:], reads=[iota1_in], writes=[iota1])
        P.dma(dsk[:], dsk_in[:], reads=[dsk_in], writes=[dsk])
        P.op(dve, lambda e: e.memset(pic[:], math.pi), [], [pic])
        names = ["lr", "li", "dt", "rho", "th", "m", "sn", "cs", "are", "aim", "den", "nr", "t1", "t2",
                 "cr", "ci", "ncr"]
        sc = {n: P.sb("s5_" + n, [128, 32], F32) for n in names}
        P.dma(sc["lr"][:], slr_in[:], reads=[slr_in], writes=[sc["lr"]])
        P.dma(sc["li"][:], sli_in[:], reads=[sli_in], writes=[sc["li"]])
        P.dma(sc["dt"][:], sldt_in[:], reads=[sldt_in], writes=[sc["dt"]])

        def tt(o, a, b, op, eng=dve):
            P.op(eng, lambda e: e.tensor_tensor(out=sc[o][:], in0=sc[a][:], in1=sc[b][:], op=op),
                 [sc[a], sc[b]], [sc[o]])

        def ts(o, a, s1, s2, op0, op1=None):
            if op1 is None:
                P.op(dve, lambda e: e.tensor_scalar(out=sc[o][:], in0=sc[a][:], scalar1=s1, scalar2=None,
                                                    op0=op0), [sc[a]], [sc[o]])
            else:
                P.op(dve, lambda e: e.tensor_scalar(out=sc[o][:], in0=sc[a][:], scalar1=s1, scalar2=s2,
                                                    op0=op0, op1=op1), [sc[a]], [sc[o]])

        def sin_of(o, m_):
            P.op(act, lambda e: e.activation(out=sc[o][:], in_=sc[m_][:], func=AF.Sin, scale=-1.0,
                                             bias=pic[:, 0:1]), [sc[m_], pic], [sc[o]])

        P.op(act, lambda e: e.activation(out=sc["dt"][:], in_=sc["dt"][:], func=AF.Exp), [sc["dt"]], [sc["dt"]])
        tt("rho", "lr", "dt", ALU.mult)
        P.op(act, lambda e: e.activation(out=sc["rho"][:], in_=sc["rho"][:], func=AF.Exp), [sc["rho"]], [sc["rho"]])
        tt("th", "li", "dt", ALU.mult)
        ts("m", "th", math.pi, TWO_PI, ALU.is_gt, ALU.mult)
        ts("t1", "th", 3 * math.pi, TWO_PI, ALU.is_gt, ALU.mult)
        tt("m", "m", "t1", ALU.add)
        ts("t1", "th", 5 * math.pi, TWO_PI, ALU.is_gt, ALU.mult)
        tt("m", "m", "t1", ALU.add)
        tt("m", "th", "m", ALU.subtract)
        P.op(act, lambda e: e.activation(out=sc["sn"][:], in_=sc["m"][:], func=AF.Sin), [sc["m"]], [sc["sn"]])
        ts("t1", "m", 0.5 * math.pi, TWO_PI, ALU.is_gt, ALU.mult)
        ts("t2", "m", 0.5 * math.pi, None, ALU.add)
        tt("t2", "t2", "t1", ALU.subtract)
        P.op(act, lambda e: e.activation(out=sc["cs"][:], in_=sc["t2"][:], func=AF.Sin), [sc["t2"]], [sc["cs"]])
        tt("are", "rho", "cs", ALU.mult)
        tt("aim", "rho", "sn", ALU.mult)
        tt("den", "lr", "lr", ALU.mult)
        tt("t1", "li", "li", ALU.mult)
        tt("den", "den", "t1", ALU.add)
        P.op(dve, lambda e: e.reciprocal(out=sc["den"][:], in_=sc["den"][:]), [sc["den"]], [sc["den"]])
        ts("nr", "are", -1.0, None, ALU.add)
        tt("t1", "nr", "lr", ALU.mult)
        tt("t2", "aim", "li", ALU.mult)
        tt("cr", "t1", "t2", ALU.add)
        tt("cr", "cr", "den", ALU.mult)
        tt("t1", "aim", "lr", ALU.mult)
        tt("t2", "nr", "li", ALU.mult)
        tt("ci", "t1", "t2", ALU.subtract)
        tt("ci", "ci", "den", ALU.mult)
        ts("ncr", "cr", -1.0, None, ALU.mult)

        NK = 12
        pc = [sc["cs"]] + [P.sb("s5_pc", [128, 32], F32) for k_ in range(1, NK)]
        pn = [sc["sn"]] + [P.sb("s5_pn", [128, 32], F32) for k_ in range(1, NK)]
        nn = [P.sb("s5_nn", [128, 32], F32) for k_ in range(NK)]
        for k_ in range(NK):
            if k_ > 0:
                a_, b_ = pc[k_ - 1], pn[k_ - 1]
                P.op(dve, lambda e: e.tensor_tensor(out=sc["t1"][:], in0=a_[:], in1=a_[:], op=ALU.mult), [a_], [sc["t1"]])
                P.op(dve, lambda e: e.tensor_tensor(out=sc["t2"][:], in0=b_[:], in1=b_[:], op=ALU.mult), [b_], [sc["t2"]])
                P.op(dve, lambda e: e.tensor_tensor(out=pc[k_][:], in0=sc["t1"][:], in1=sc["t2"][:], op=ALU.subtract),
                     [sc["t1"], sc["t2"]], [pc[k_]])
                P.op(dve, lambda e: e.tensor_tensor(out=sc["t1"][:], in0=a_[:], in1=b_[:], op=ALU.mult), [a_, b_], [sc["t1"]])
                P.op(dve, lambda e: e.tensor_scalar(out=pn[k_][:], in0=sc["t1"][:], scalar1=2.0, scalar2=None,
                                                    op0=ALU.mult), [sc["t1"]], [pn[k_]])
            P.op(dve, lambda e: e.tensor_scalar(out=nn[k_][:], in0=pn[k_][:], scalar1=-1.0, scalar2=None,
                                                op0=ALU.mult), [pn[k_]], [nn[k_]])
        par_ = [None, sc["are"]] + [P.sb("s5_par", [128, 32], F32) for k_ in range(2, 9)]
        pai_ = [None, sc["aim"]] + [P.sb("s5_pai", [128, 32], F32) for k_ in range(2, 9)]
        nai_ = [None] + [P.sb("s5_nai", [128, 32], F32) for k_ in range(1, 9)]
        for k_ in range(2, 9):
            a_, b_ = par_[k_ - 1], pai_[k_ - 1]
            P.op(dve, lambda e: e.tensor_tensor(out=sc["t1"][:], in0=a_[:], in1=sc["are"][:], op=ALU.mult), [a_, sc["are"]], [sc["t1"]])
            P.op(dve, lambda e: e.tensor_tensor(out=sc["t2"][:], in0=b_[:], in1=sc["aim"][:], op=ALU.mult), [b_, sc["aim"]], [sc["t2"]])
            P.op(dve, lambda e: e.tensor_tensor(out=par_[k_][:], in0=sc["t1"][:], in1=sc["t2"][:], op=ALU.subtract),
                 [sc["t1"], sc["t2"]], [par_[k_]])
            P.op(dve, lambda e: e.tensor_tensor(out=sc["t1"][:], in0=a_[:], in1=sc["aim"][:], op=ALU.mult), [a_, sc["aim"]], [sc["t1"]])
            P.op(dve, lambda e: e.tensor_tensor(out=sc["t2"][:], in0=b_[:], in1=sc["are"][:], op=ALU.mult), [b_, sc["are"]], [sc["t2"]])
            P.op(dve, lambda e: e.tensor_tensor(out=pai_[k_][:], in0=sc["t1"][:], in1=sc["t2"][:], op=ALU.add),
                 [sc["t1"], sc["t2"]], [pai_[k_]])
        for k_ in range(1, 9):
            P.op(dve, lambda e: e.tensor_scalar(out=nai_[k_][:], in0=pai_[k_][:], scalar1=-1.0, scalar2=None,
                                                op0=ALU.mult), [pai_[k_]], [nai_[k_]])
        rho8 = P.sb("s5_rho8", [128, 32], F32)
        tt("t1", "rho", "rho", ALU.mult)
        tt("t2", "t1", "t1", ALU.mult)
        P.op(dve, lambda e: e.tensor_tensor(out=rho8[:], in0=sc["t2"][:], in1=sc["t2"][:], op=ALU.mult), [sc["t2"]], [rho8])
        nci = P.sb("s5_nci", [128, 32], F32)
        P.op(dve, lambda e: e.tensor_scalar(out=nci[:], in0=sc["ci"][:], scalar1=-1.0, scalar2=None, op0=ALU.mult),
             [sc["ci"]], [nci])

        LY = P.sb("LY", [128, 2, 32, 128], BF16)
        lst = [P.sb("lst", [128, 8, 128], F32) for i in range(2)]
        k = 0
        for ri in range(2):
            for q4 in range(4):
                st_ = lst[k % 2]
                k += 1
                P.dma(st_[:], ly_in.h[ri, q4 * 8:(q4 + 1) * 8].rearrange("a k m -> k a m"),
                      reads=[ly_in], writes=[st_])
                P.op(act, lambda e: e.activation(out=LY[:, ri, q4 * 8:(q4 + 1) * 8, :], in_=st_[:],
                                                 func=AF.Copy, scale=(1.0, -1.0)[ri]), [st_], [LY])

        J = L // 8
        bst = [P.sb("bst", [128, 2, 4, 128], F32)] * 2
        cst = [P.sb("cst", [128, 2, 4, 128], F32)] * 2
        Gr = P.sb("Gr", [128, 128], F32)
        Gi = P.sb("Gi", [128, 128], F32)
        Gt = P.sb("Gt", [128, 128], F32)
        Gb = P.sb("Gb", [128, 4, 8, 2, 128], BF16)
        Bs = P.sb("Bs", [128, 4, 8, 2, 128], BF16)
        CA = P.sb("CA", [128, 4, 8, 2, 128], BF16)
        Kt = P.sb("Kt", [128, 8, 128], BF16)
        dI = P.sb("dI", [128, 128], F32)
        uP = [P.sb("uP", [128, L], BF16) for i in range(2)]
        uPm = P.sb("uPm", [128, 8, L // 8], BF16)
        tabc = [P.sb("tabc", [128, J], F32) for i in range(2)]
        tabs = [P.sb("tabs", [128, J], F32) for i in range(2)]
        mm = P.sb("mm", [128, J], F32)
        tmps = []
        for i_ in range(1):
            tmps.append({n: P.sb("tmp_" + n, [128, J], F32) for n in
                         ("vr", "vi", "a1", "a2", "a3", "a4", "er", "ei", "wr", "wi", "xr", "xi")})
        tmps.append(tmps[0])
        Xb = P.sb("Xb", [128, 4, 2, J], BF16)
        ypk = P.sb("ypk", [128, L], F32)
        g1 = [P.sb("g1", [128, 512], F32) for i in range(2)]
        g2 = [P.sb("g2", [128, 512], F32) for i in range(2)]
        ygo = [P.sb("ygo", [128, 512], BF16) for i in range(2)]
        P.op(pool, lambda e: e.memset(Xb[:, :, :, 0:1], 0.0), [], [Xb])

        def T2(eng, o, a, b, op):
            P.op(eng, lambda e: e.tensor_tensor(out=o[:], in0=a[:], in1=b[:], op=op), [a, b], [o])

        def cmul(o_r, o_i, i_r, i_i, sr, si, nsi, tmp_):
            P.op(dve, lambda e: e.tensor_scalar(out=tmp_[:], in0=i_r, scalar1=sr, scalar2=None, op0=ALU.mult),
                 [], [tmp_])
            P.op(dve, lambda e: e.scalar_tensor_tensor(out=o_r, in0=i_i, scalar=nsi, in1=tmp_[:],
                                                       op0=ALU.mult, op1=ALU.add), [tmp_], [])
            P.op(dve, lambda e: e.tensor_scalar(out=tmp_[:], in0=i_i, scalar1=sr, scalar2=None, op0=ALU.mult),
                 [], [tmp_])
            P.op(dve, lambda e: e.scalar_tensor_tensor(out=o_i, in0=i_r, scalar=si, in1=tmp_[:],
                                                       op0=ALU.mult, op1=ALU.add), [tmp_], [])

        def s_load1(pk):
            up_ = uP[pk % 2]
            bs_, cs2_ = bst[pk % 2], cst[pk % 2]
            upv = uPm
            P.dma(up_[:], ut[pk * 128:(pk + 1) * 128, :], reads=[ut], writes=[up_])
            for ri in range(2):
                P.dma(bs_[:, ri, :, :], lvt_in.h[ri, pk * 4:(pk + 1) * 4].rearrange("a k m -> k a m"),
                      reads=[lvt_in], writes=[bs_])

        def s_g(pk):
            up_ = uP[pk % 2]
            bs_, cs2_ = bst[pk % 2], cst[pk % 2]
            upv = uPm
            for p4 in range(4):
                gp = pk * 4 + p4
                g_ = slice(gp, gp + 1)
                P.op(dve, lambda e: e.tensor_scalar(out=Gt[:], in0=bs_[:, 0, p4, :], scalar1=sc["cr"][:, g_],
                                                    scalar2=None, op0=ALU.mult), [bs_, sc["cr"]], [Gt])
                P.op(dve, lambda e: e.scalar_tensor_tensor(out=Gr[:], in0=bs_[:, 1, p4, :], scalar=nci[:, g_],
                                                           in1=Gt[:], op0=ALU.mult, op1=ALU.add), [bs_, nci, Gt], [Gr])
                P.op(dve, lambda e: e.tensor_scalar(out=Gt[:], in0=bs_[:, 1, p4, :], scalar1=sc["cr"][:, g_],
                                                    scalar2=None, op0=ALU.mult), [bs_, sc["cr"]], [Gt])
                P.op(dve, lambda e: e.scalar_tensor_tensor(out=Gi[:], in0=bs_[:, 0, p4, :], scalar=sc["ci"][:, g_],
                                                           in1=Gt[:], op0=ALU.mult, op1=ALU.add), [bs_, sc["ci"], Gt], [Gi])
                for tau in range(8):
                    if tau > 0:
                        P.op(dve, lambda e: e.tensor_scalar(out=Gt[:], in0=Gr[:], scalar1=sc["aim"][:, g_],
                                                            scalar2=None, op0=ALU.mult), [Gr, sc["aim"]], [Gt])
                        P.op(dve, lambda e: e.tensor_scalar(out=Gr[:], in0=Gr[:], scalar1=sc["are"][:, g_],
                                                            scalar2=None, op0=ALU.mult), [Gr, sc["are"]], [Gr])
                        P.op(dve, lambda e: e.scalar_tensor_tensor(out=Gr[:], in0=Gi[:], scalar=nai_[1][:, g_],
                                                                   in1=Gr[:], op0=ALU.mult, op1=ALU.add),
                             [Gi, nai_[1], Gr], [Gr])
                        P.op(dve, lambda e: e.scalar_tensor_tensor(out=Gi[:], in0=Gi[:], scalar=sc["are"][:, g_],
                                                                   in1=Gt[:], op0=ALU.mult, op1=ALU.add),
                             [Gi, sc["are"], Gt], [Gi])
                    for ri, G_ in ((0, Gr), (1, Gi)):
                        P.op(act, lambda e: e.activation(out=Gb[:, p4, tau, ri, :], in_=G_[:], func=AF.Copy),
                             [G_], [Gb])
                        tb_ = PS[6 + (ri % 2)]
                        P.op(pe, lambda e: e.transpose(out=tb_[:, 0:128], in_=G_[:], identity=ident_f[:]),
                             [G_, ident_f], [tb_])
                        P.op(act, lambda e: e.activation(out=Bs[:, p4, tau, ri, :], in_=tb_[:, 0:128], func=AF.Copy),
                             [tb_], [Bs])

        def s_load2(pk):
            up_ = uP[pk % 2]
            bs_, cs2_ = bst[pk % 2], cst[pk % 2]
            upv = uPm
            for ri in range(2):
                P.dma(cs2_[:, ri, :, :], ly_in.h[ri, pk * 4:(pk + 1) * 4].rearrange("a k m -> k a m"),
                      reads=[ly_in], writes=[cs2_])
            P.op(dve, lambda e: e.tensor_scalar(out=cs2_[:, 1, :, :], in0=cs2_[:, 1, :, :], scalar1=-1.0, scalar2=None,
                                                op0=ALU.mult), [cs2_], [cs2_])
            upv0 = up_[:].rearrange("p (j s) -> p s j", s=8)
            P.op(act, lambda e: e.activation(out=uPm[:], in_=upv0, func=AF.Copy), [up_], [uPm])

        def s_ca(pk):
            up_ = uP[pk % 2]
            bs_, cs2_ = bst[pk % 2], cst[pk % 2]
            upv = uPm
            for p4 in range(4):
                gp = pk * 4 + p4
                g_ = slice(gp, gp + 1)
                for tl in range(8):
                    ar_, ai_, nai2 = par_[tl + 1][:, g_], pai_[tl + 1][:, g_], nai_[tl + 1][:, g_]
                    P.op(dve, lambda e: e.tensor_scalar(out=Gt[:], in0=cs2_[:, 0, p4, :], scalar1=ar_, scalar2=None,
                                                        op0=ALU.mult), [cs2_, par_[tl + 1]], [Gt])
                    P.op(dve, lambda e: e.scalar_tensor_tensor(out=CA[:, p4, tl, 0, :], in0=cs2_[:, 1, p4, :], scalar=ai_,
                                                               in1=Gt[:], op0=ALU.mult, op1=ALU.add),
                         [cs2_, pai_[tl + 1], Gt], [CA])
                    P.op(dve, lambda e: e.tensor_scalar(out=Gt[:], in0=cs2_[:, 0, p4, :], scalar1=nai2, scalar2=None,
                                                        op0=ALU.mult), [cs2_, nai_[tl + 1]], [Gt])
                    P.op(dve, lambda e: e.scalar_tensor_tensor(out=CA[:, p4, tl, 1, :], in0=cs2_[:, 1, p4, :], scalar=ar_,
                                                               in1=Gt[:], op0=ALU.mult, op1=ALU.add),
                         [cs2_, par_[tl + 1], Gt], [CA])

        def s_k(pk):
            up_ = uP[pk % 2]
            bs_, cs2_ = bst[pk % 2], cst[pk % 2]
            upv = uPm
            P.op(dve, lambda e: e.tensor_scalar(out=dI[:], in0=ident_f[:], scalar1=dsk[:, pk:pk + 1], scalar2=None,
                                                op0=ALU.mult), [ident_f, dsk], [dI])
            for tau in range(8):
                kb_ = PS[6 + (tau % 2)]
                for p4 in range(4):
                    gp = pk * 4 + p4
                    P.op(pe, lambda e: e.matmul(kb_[:, 0:128], lhsT=Gb[:, p4, tau, 0, :], rhs=LY[:, 0, gp, :],
                                                start=(p4 == 0), stop=False), [Gb, LY], [kb_])
                    P.op(pe, lambda e: e.matmul(kb_[:, 0:128], lhsT=Gb[:, p4, tau, 1, :], rhs=LY[:, 1, gp, :],
                                                start=False, stop=(p4 == 3)), [Gb, LY], [kb_])
                if tau == 0:
                    P.op(dve, lambda e: e.tensor_tensor(out=Kt[:, 0, :], in0=kb_[:, 0:128], in1=dI[:], op=ALU.add),
                         [kb_, dI], [Kt])
                else:
                    P.op(act, lambda e: e.activation(out=Kt[:, tau, :], in_=kb_[:, 0:128], func=AF.Copy), [kb_], [Kt])

        def s_main(pk):
            up_ = uP[pk % 2]
            bs_, cs2_ = bst[pk % 2], cst[pk % 2]
            upv = uPm
            def emit_e(p4_):
                for ri_ in range(2):
                    pb_ = PS[2 * (p4_ % 2) + ri_]
                    for s_ in range(8):
                        P.op(pe, lambda e: e.matmul(pb_[:], lhsT=Bs[:, p4_, 7 - s_, ri_, :], rhs=upv[:, s_, :],
                                                    start=(s_ == 0), stop=(s_ == 7)), [Bs, uPm], [pb_])

            def st_tab(p4_):
                gp_ = pk * 4 + p4_
                g2_ = slice(gp_, gp_ + 1)
                tc_, ts_ = tabc[p4_ % 2], tabs[p4_ % 2]
                P.op(dve, lambda e: e.tensor_copy(out=tc_[:, 0:1], in_=pc[3][:, g2_]), [pc[3]], [tc_])
                P.op(dve, lambda e: e.tensor_copy(out=ts_[:, 0:1], in_=pn[3][:, g2_]), [pn[3]], [ts_])
                for k_ in range(9):
                    n_ = 1 << k_
                    lo = slice(0, n_)
                    hi = slice(n_, 2 * n_)
                    kk = k_ + 3
                    P.op(dve, lambda e: e.tensor_scalar(out=mm[:, lo], in0=tc_[:, lo], scalar1=pc[kk][:, g2_],
                                                        scalar2=None, op0=ALU.mult), [tc_, pc[kk]], [mm])
                    P.op(dve, lambda e: e.scalar_tensor_tensor(out=tc_[:, hi], in0=ts_[:, lo], scalar=nn[kk][:, g2_],
                                                               in1=mm[:, lo], op0=ALU.mult, op1=ALU.add),
                         [ts_, nn[kk], mm], [tc_])
                    P.op(dve, lambda e: e.tensor_scalar(out=mm[:, lo], in0=ts_[:, lo], scalar1=pc[kk][:, g2_],
                                                        scalar2=None, op0=ALU.mult), [ts_, pc[kk]], [mm])
                    P.op(dve, lambda e: e.scalar_tensor_tensor(out=ts_[:, hi], in0=tc_[:, lo], scalar=pn[kk][:, g2_],
                                                               in1=mm[:, lo], op0=ALU.mult, op1=ALU.add),
                         [tc_, pn[kk], mm], [ts_])

            emit_e(0)
            for p4 in range(4):
                gp = pk * 4 + p4
                if p4 + 1 < 4:
                    emit_e(p4 + 1)
                st_tab(p4)
                tmp = tmps[p4 % 2]
                psa, psb = PS[2 * (p4 % 2)], PS[2 * (p4 % 2) + 1]
                tc_, ts_ = tabc[p4 % 2], tabs[p4 % 2]
                P.op(act, lambda e: e.activation(out=tmp["vr"][:], in_=psa[:], func=AF.Copy), [psa], [tmp["vr"]])
                P.op(act, lambda e: e.activation(out=tmp["vi"][:], in_=psb[:], func=AF.Copy), [psb], [tmp["vi"]])
                T2(dve, tmp["a1"], tc_, tmp["vr"], ALU.mult)
                T2(dve, tmp["a2"], ts_, tmp["vi"], ALU.mult)
                T2(dve, tmp["a3"], tc_, tmp["vi"], ALU.mult)
                T2(dve, tmp["a4"], ts_, tmp["vr"], ALU.mult)
                T2(dve, tmp["er"], tmp["a1"], tmp["a2"], ALU.add)
                T2(dve, tmp["ei"], tmp["a3"], tmp["a4"], ALU.subtract)
                rhoc = rho8[:, gp:gp + 1].to_broadcast([128, J])
                for (w_, e_) in ((tmp["wr"], tmp["er"]), (tmp["wi"], tmp["ei"])):
                    P.op(dve, lambda e: e.tensor_tensor_scan(out=w_[:], data0=rhoc, data1=e_[:], initial=0.0,
                                                             op0=ALU.mult, op1=ALU.add), [rho8, e_], [w_])
                T2(dve, tmp["a1"], tc_, tmp["wr"], ALU.mult)
                T2(dve, tmp["a2"], ts_, tmp["wi"], ALU.mult)
                T2(dve, tmp["a3"], ts_, tmp["wr"], ALU.mult)
                T2(dve, tmp["a4"], tc_, tmp["wi"], ALU.mult)
                T2(dve, tmp["xr"], tmp["a1"], tmp["a2"], ALU.subtract)
                T2(dve, tmp["xi"], tmp["a3"], tmp["a4"], ALU.add)
                P.op(act, lambda e: e.activation(out=Xb[:, p4, 0, 1:J], in_=tmp["xr"][:, 0:J - 1], func=AF.Copy),
                     [tmp["xr"]], [Xb])
                P.op(act, lambda e: e.activation(out=Xb[:, p4, 1, 1:J], in_=tmp["xi"][:, 0:J - 1], func=AF.Copy),
                     [tmp["xi"]], [Xb])

        def s_y(pk):
            up_ = uP[pk % 2]
            bs_, cs2_ = bst[pk % 2], cst[pk % 2]
            upv = uPm
            ypv = ypk[:].rearrange("p (j s) -> p s j", s=8)
            for tl in range(8):
                yb_ = PS[4 + (tl % 2)]
                first = True
                for p4 in range(4):
                    for ri in range(2):
                        P.op(pe, lambda e: e.matmul(yb_[:], lhsT=CA[:, p4, tl, ri, :], rhs=Xb[:, p4, ri, :],
                                                    start=first, stop=False), [CA, Xb], [yb_])
                        first = False
                for s_ in range(tl + 1):
                    P.op(pe, lambda e: e.matmul(yb_[:], lhsT=Kt[:, tl - s_, :], rhs=upv[:, s_, :],
                                                start=False, stop=(s_ == tl)), [Kt, uPm], [yb_])
                P.op(act, lambda e: e.activation(out=ypv[:, tl, :], in_=yb_[:], func=AF.Copy), [yb_], [ypk])

        def s_gelu(pk):
            for cch in range(8):
                cs_ = slice(cch * 512, (cch + 1) * 512)
                g1_, g2_, yo_ = g1[cch % 2], g2[cch % 2], ygo[cch % 2]
                P.op(act, lambda e: e.activation(out=g1_[:], in_=ypk[:, cs_], func=AF.Square), [ypk], [g1_])
                P.op(dve, lambda e: e.tensor_scalar(out=g1_[:], in0=g1_[:], scalar1=0.044715, scalar2=1.0,
                                                    op0=ALU.mult, op1=ALU.add), [g1_], [g1_])
                P.op(dve, lambda e: e.tensor_tensor(out=g2_[:], in0=g1_[:], in1=ypk[:, cs_], op=ALU.mult),
                     [g1_, ypk], [g2_])
                P.op(act, lambda e: e.activation(out=g2_[:], in_=g2_[:], func=AF.Sigmoid,
                                                 scale=1.5957691216057308), [g2_], [g2_])
                P.op(dve, lambda e: e.tensor_tensor(out=yo_[:], in0=g2_[:], in1=ypk[:, cs_], op=ALU.mult),
                     [g2_, ypk], [yo_])
                P.dma(ygd[pk * 128:(pk + 1) * 128, cs_], yo_[:], reads=[yo_], writes=[ygd], q=act)

        s_load1(0)
        s_g(0)
        s_load2(0)
        s_ca(0)
        s_k(0)
        for pk in range(8):
            s_main(pk)
            s_y(pk)
            if pk + 1 < 8:
                s_load1(pk + 1)
                s_g(pk + 1)
            s_gelu(pk)
            if pk + 1 < 8:
                s_load2(pk + 1)
                s_ca(pk + 1)
                s_k(pk + 1)
        P.phase_end()
        lst = [P.sb("lst2", [128, 8, 128], F32) for i in range(2)]
        wg = P.sb("wg", [128, 8, 1024], BF16)
        bgl = P.sb("bgl", [128, 8], F32)
        P.dma(bgl[:], bglu_in[:], reads=[bglu_in], writes=[bgl])
        wgv = wglu_in.h.rearrange("(c p) n -> p c n", p=128)
        for c in range(8):
            st_ = lst[c % 2]
            P.dma(st_[:].rearrange("p a b -> p (a b)"), wgv[:, c, :], reads=[wglu_in], writes=[st_])
            P.op(pool, lambda e: e.tensor_copy(out=wg[:, c, :], in_=st_[:].rearrange("p a b -> p (a b)")),
                 [st_], [wg])
        zt = [P.sb("zt", [128, T], BF16) for i in range(2)]
        gt = [P.sb("gt", [128, T], BF16) for i in range(2)]
        ygt = [P.sb("ygt", [128, 8, T], BF16) for i in range(2)]
        ygv = ygd.h.rearrange("(c p) t -> p c t", p=128)
        k = 0
        for ti in range(NT):
            t0 = ti * T
            yg_ = ygt[ti % 2]
            P.dma(yg_[:], ygv[:, :, t0:t0 + T], reads=[ygd], writes=[yg_])
            for nb in range(8):
                s_ = k % 2
                k += 1
                a = PS[k % 4]
                P.dma(zt[s_][:], zb[nb * 128:(nb + 1) * 128, t0:t0 + T], reads=[zb], writes=[zt[s_]])
                for jc in range(8):
                    P.op(pe, lambda e: e.matmul(a[:], lhsT=wg[:, jc, nb * 128:(nb + 1) * 128],
                                                rhs=yg_[:, jc, :], start=(jc == 0), stop=(jc == 7)),
                         [wg, yg_], [a])
                P.op(act, lambda e: e.activation(out=gt[s_][:], in_=a[:], func=AF.Sigmoid,
                                                 bias=bgl[:, nb:nb + 1]), [a, bgl], [gt[s_]])
                P.op(dve, lambda e: e.tensor_tensor(out=gt[s_][:], in0=gt[s_][:], in1=yg_[:, nb, :],
                                                    op=ALU.mult), [gt[s_], yg_], [gt[s_]])
                P.op(dve, lambda e: e.tensor_tensor(out=gt[s_][:], in0=gt[s_][:], in1=zt[s_][:],
                                                    op=ALU.mult), [gt[s_], zt[s_]], [gt[s_]])
                P.dma(mixT[1024 + nb * 128:1024 + (nb + 1) * 128, t0:t0 + T], gt[s_][:], reads=[gt[s_]],
                      writes=[mixT], q=act)
        P.phase_end()

    def back(w_out, resid, dst, final):
        P.phase_begin()
        wo = P.sb("wo", [128, NCH, D], BF16)
        wst = [P.sb("wst", [128, D], F32) for i in range(2)]
        mt = [P.sb("mt", [128, NCH, 512], BF16) for i in range(2)]
        xr = [P.sb("xr", [128, D], F32) for i in range(2)]
        xo = [P.sb("xo", [128, D], F32) for i in range(2)]
        wov = w_out.h.rearrange("(c p) n -> p c n", p=128)
        for c in range(NCH):
            P.dma(wst[c % 2][:], wov[:, c, :], reads=[w_out], writes=[wst[c % 2]])
            P.op(pool, lambda e: e.tensor_copy(out=wo[:, c, :], in_=wst[c % 2][:]), [wst[c % 2]], [wo])
        if final:
            gB = P.sb("gB", [128, D], F32)
            junk = P.sb("junk", [128, D], BF16)
            ss = [P.sb("ss", [128, 1], F32) for i in range(2)]
            rs = [P.sb("rs", [128, 1], F32) for i in range(2)]
            P.dma(gB[:], gf_in[0:1, :].partition_broadcast(128), reads=[gf_in], writes=[gB])
        mv = mixT.h.rearrange("(c p) t -> p c t", p=128)
        for tb in range(L // 128):
            s = tb % 2
            t0 = tb * 128
            ms = (tb // 4) % 2
            mo = (tb % 4) * 128
            if tb % 4 == 0:
                P.dma(mt[ms][:], mv[:, :, t0:t0 + 512], reads=[mixT], writes=[mt[ms]])
            P.dma(xr[s][:], resid[t0:t0 + 128, :], reads=[resid], writes=[xr[s]])
            for ng in range(4):
                a = PS[(tb % 2) * 4 + ng]
                for c in range(NCH):
                    P.op(pe, lambda e: e.matmul(a[:], lhsT=mt[ms][:, c, mo:mo + 128],
                                                rhs=wo[:, c, ng * 512:(ng + 1) * 512],
                                                start=(c == 0), stop=(c == NCH - 1)), [mt[ms], wo], [a])
                P.op(dve, lambda e: e.tensor_tensor(out=xo[s][:, ng * 512:(ng + 1) * 512], in0=a[:],
                                                    in1=xr[s][:, ng * 512:(ng + 1) * 512], op=ALU.add),
                     [a, xr[s]], [xo[s]])
            if final:
                P.op(act, lambda e: e.activation(out=junk[:], in_=xo[s][:], func=AF.Square,
                                                 accum_out=ss[s][:]), [xo[s]], [junk, ss[s]])
                P.op(dve, lambda e: e.tensor_scalar(out=rs[s][:], in0=ss[s][:], scalar1=1.0 / D,
                                                    scalar2=EPS, op0=ALU.mult, op1=ALU.add),
                     [ss[s]], [rs[s]])
                P.op(act, lambda e: e.activation(out=rs[s][:], in_=rs[s][:], func=AF.Sqrt),
                     [rs[s]], [rs[s]])
                P.op(dve, lambda e: e.reciprocal(out=rs[s][:], in_=rs[s][:]), [rs[s]], [rs[s]])
                P.op(dve, lambda e: e.scalar_tensor_tensor(out=xo[s][:], in0=xo[s][:],
                                                            scalar=rs[s][:, 0:1], in1=gB[:],
                                                            op0=ALU.mult, op1=ALU.mult),
                     [xo[s], rs[s], gB], [xo[s]])
            t = P.dma(dst[t0:t0 + 128, :], xo[s][:], reads=[xo[s]], writes=[dst], q=act)
            if final:
                P.out_toks.append(t)
        P.phase_end()

    if allp or "front0" in phases:
        front(x_in, w0_in, g0_in, [
            (0, 1024, qa, "fm", ("scale", 0.125)),
            (1024, 1024, ka, "fm", ("copy",)),
            (2048, 1024, va, "tm", ("copy",)),
            (3072, 1024, ut, "fm", ("copy",)),
            (4096, 1024, za, "fm", ("silu",)),
            (5120, 1024, zb, "fm", ("silu",)),
        ])
    if allp or "attn_a" in phases:
        attn_a()
    if allp or "s5" in phases:
        s5_glu()
    if allp or "back0" in phases:
        back(wo0_in, x_in, x1, False)
    if allp or "front1" in phases:
        sc = 1.0 / math.sqrt(128.0)
        front(x1, w1_in, g1_in, [
            (0, 2048, qc, "fm", ("scale", sc)),
            (2048, 2048, kc, "fm", ("copy",)),
            (4096, 2048, vc, "tm", ("copy",)),
            (6144, 2048, zc, "fm", ("silu",)),
        ], fcol=8192)
    if allp or "fox" in phases:
        fox()
    if allp or "back1" in phases:
        back(wo1_in, x1, out_d, True)

    name2buf = dict(qa=qa, ka=ka, va=va, ut=ut, za=za, zb=zb, mixT=mixT, ygd=ygd)
    if dbg:
        P.phase_begin()
        stage = P.sb("dbg_stage", [128, 4096], BF16)
        for nm in dbg:
            src = name2buf[nm]
            shp = src.h.shape
            dd = P.dram("dbg_" + nm, list(shp), BF16, kind="ExternalOutput")
            for r in range(0, shp[0], 128):
                for c in range(0, shp[1], 4096):
                    w = min(4096, shp[1] - c)
                    P.dma(stage[:, 0:w], src[r:r + 128, c:c + w], reads=[src], writes=[stage])
                    t = P.dma(dd[r:r + 128, c:c + w], stage[:, 0:w], reads=[stage], writes=[dd])
                    P.out_toks.append(t)
        P.phase_end()
    for t in P.out_toks:
        sp.wait(t)
    return P


def _blockmajor(w, nblk):
    w = np.asarray(w, dtype=np.float32)
    wp = np.zeros((2048, nblk * 128), np.float32)
    wp[:, :w.shape[1]] = w
    return np.ascontiguousarray(wp.reshape(16, 128, nblk, 128).transpose(2, 1, 0, 3))


def host_inputs(inp, b):
    f = np.float32
    A = lambda a: np.ascontiguousarray(a, dtype=f)
    m = {}
    m["x"] = A(inp["x"][b])
    m["ident"] = np.eye(128, dtype=f)
    sel = np.zeros((128, 128), f); sel[127, :] = 1
    m["sel127"] = sel
    m["tri"] = np.triu(np.ones((128, 128), f))
    o96 = np.zeros((96, 128), f); o96[[0, 32, 64], :] = 1
    m["ones96"] = o96
    m["g0"] = A(inp["norm_even_g"][0].reshape(16, 128).T)
    m["w0"] = _blockmajor(inp["w_in_even"][0], 48)
    m["g1"] = A(inp["norm_odd_g"][0].reshape(16, 128).T)
    m["w1"] = _blockmajor(inp["w_in_odd"][0], 65)
    m["bf"] = A(inp["b_forget"][0].reshape(16, 1))
    m["wo0"] = A(inp["w_out_even"][0])
    m["wo1"] = A(inp["w_out_odd"][0])
    m["gf"] = A(inp["final_norm_g"].reshape(1, 2048))
    rb = np.asarray(inp["rel_bias"][0])
    kl = np.arange(128)[:, None]; ql = np.arange(128)[None, :]
    ab = np.zeros((16, 3, 128, 128), f); am = np.zeros((3, 128, 128), f)
    for i, r in enumerate((0, 3, 4)):
        rel = (ql - kl) + 128 * (4 - r)
        idx = np.clip(rel, -128, 128) + 128
        ab[:, i] = rb[:, idx]
    am[0][(kl < 64) & (ql >= 64)] = -30000.0
    am[2][(kl >= 64) & (ql < 64)] = -30000.0
    m["abias"] = ab; m["amask"] = am
    m["abc"] = A(np.broadcast_to(rb[:, 256][None, :], (128, 16)))
    bre = np.asarray(inp["s5_b_re"][0]); bim = np.asarray(inp["s5_b_im"][0])
    cre = np.asarray(inp["s5_c_re"][0]); cim = np.asarray(inp["s5_c_im"][0])
    lv = np.zeros((2, 32, 128, 128), f); ly = np.zeros((2, 32, 128, 128), f); lvt = np.zeros((2, 32, 128, 128), f)
    slr = np.zeros((128, 32), f); sli = np.zeros((128, 32), f); sldt = np.zeros((128, 32), f)
    dsk = np.zeros((128, 8), f)
    lre = np.asarray(inp["s5_lambda_re"][0]); lim = np.asarray(inp["s5_lambda_im"][0])
    ldt = np.asarray(inp["s5_log_dt"][0]); dsp = np.asarray(inp["s5_d"][0])
    for g in range(64):
        gp = g // 2; slot = g % 2; gl = g % 8
        rs = slice(16 * gl, 16 * gl + 16); ps_ = slice(64 * slot, 64 * slot + 64)
        lv[0, gp, rs, ps_] = bre[g].T
        lv[1, gp, rs, ps_] = bim[g].T
        lvt[0, gp, ps_, rs] = bre[g]
        lvt[1, gp, ps_, rs] = bim[g]
        ly[0, gp, ps_, rs] = cre[g].T
        ly[1, gp, ps_, rs] = cim[g].T
        slr[ps_, gp] = lre[g]; sli[ps_, gp] = lim[g]; sldt[ps_, gp] = ldt[g]
        dsk[rs, g // 8] = dsp[g]
    m.update(lv=lv, ly=ly, lvt=lvt, slr=slr, sli=sli, sldt=sldt, dsk=dsk)
    m["iota1"] = A(np.broadcast_to(np.arange(1, 513, dtype=f)[None, :], (128, 512)))
    m["wglu"] = A(inp["w_glu"][0])
    m["bglu"] = A(inp["b_glu"][0].reshape(8, 128).T)
    return m


_PROG = None


def kernel(**inputs):
    global _PROG
    inp = {k: np.asarray(v) for k, v in inputs.items()}
    if _PROG is None:
        _PROG = build()
    nb = inp["x"].shape[0]
    in_maps = [host_inputs(inp, 0)]
    for b in range(1, nb):
        mb = dict(in_maps[0])
        mb["x"] = np.ascontiguousarray(inp["x"][b], dtype=np.float32)
        in_maps.append(mb)
    res = run_bass_kernel_spmd(_PROG.nc, in_maps, core_ids=list(range(nb)))
    return np.stack([np.asarray(r["out"]) for r in res.results], axis=0).astype(np.float32)
```
